# Optimizing a Trainium2 kernel written in Bass

```python
import jax, jax.numpy as jnp
from jax import lax
import numpy as np

D_MODEL = 2048
BATCH = 1
SEQ = 8192
DEPTH = 4

N_META = 16
FRONT = 128
N_PAD = FRONT - N_META

N_A_LAYERS = DEPTH // 2
N_B_LAYERS = DEPTH - N_A_LAYERS

ALPHA = (2.0 * DEPTH) ** 0.25
BETA = (8.0 * DEPTH) ** -0.25
LN_EPS = 1e-5

GLA_HEADS = 4
GLA_DK = D_MODEL // 2
GLA_DV = D_MODEL
GLA_HK = GLA_DK // GLA_HEADS
GLA_HV = GLA_DV // GLA_HEADS
GLA_RANK = 16
GLA_TAU = 16.0
GLA_CHUNK = 64
GLA_IN = 2 * GLA_DK + GLA_DV + GLA_RANK + GLA_DV

FOX_HEADS = 16
FOX_HD = D_MODEL // FOX_HEADS
FOX_BLOCK = 128
KV_OUT = 2 * D_MODEL + FOX_HEADS

D_FF = 5632
CONV_W = 3

kernel_name = "gla_fox_yoco_deepnorm_hybrid"


def layer_norm(x, g, b):
    xf = x.astype(jnp.float32)
    mu = xf.mean(-1, keepdims=True)
    var = jnp.square(xf - mu).mean(-1, keepdims=True)
    return ((xf - mu) * lax.rsqrt(var + LN_EPS) * g + b).astype(x.dtype)


def rms_norm(x, g):
    xf = x.astype(jnp.float32)
    y = xf * lax.rsqrt(jnp.mean(jnp.square(xf), -1, keepdims=True) + LN_EPS)
    return (y * g).astype(x.dtype)


def gla_mixer(x, valid, w_in, w_g2, b_g2, norm_g, w_out):
    B, L, _ = x.shape
    n_chunks = L // GLA_CHUNK
    proj = x @ w_in
    q, k, v, g_low, r = jnp.split(
        proj, [GLA_DK, 2 * GLA_DK, 2 * GLA_DK + GLA_DV, 2 * GLA_DK + GLA_DV + GLA_RANK], axis=-1)
    m = valid[None, :, None]
    log_a = jax.nn.log_sigmoid((g_low @ w_g2 + b_g2).astype(jnp.float32)) / GLA_TAU
    log_a = jnp.where(m, log_a, 0.0)
    k = jnp.where(m, k, 0)
    q = q * (GLA_HK ** -0.5)

    def to_chunks(t, hd):
        return t.astype(jnp.float32).reshape(B, n_chunks, GLA_CHUNK, GLA_HEADS, hd).transpose(1, 0, 3, 2, 4)

    qc, kc, gc = to_chunks(q, GLA_HK), to_chunks(k, GLA_HK), to_chunks(log_a, GLA_HK)
    vc = to_chunks(v, GLA_HV)
    causal = jnp.tril(jnp.ones((GLA_CHUNK, GLA_CHUNK), dtype=bool))

    def step(S, inp):
        q_c, k_c, v_c, g_c = inp
        b = jnp.cumsum(g_c, axis=2)
        b_last = b[:, :, -1:, :]
        inter = jnp.einsum('bhck,bhkv->bhcv', q_c * jnp.exp(b), S)
        diff = b[:, :, :, None, :] - b[:, :, None, :, :]
        decay = jnp.exp(jnp.where(causal[:, :, None], diff, -jnp.inf))
        att = jnp.einsum('bhik,bhjk,bhijk->bhij', q_c, k_c, decay)
        intra = jnp.einsum('bhij,bhjv->bhiv', att, v_c)
        S_new = (jnp.exp(b_last[:, :, 0, :])[..., None] * S
                 + jnp.einsum('bhck,bhcv->bhkv', k_c * jnp.exp(b_last - b), v_c))
        return S_new, inter + intra

    S0 = jnp.zeros((B, GLA_HEADS, GLA_HK, GLA_HV), jnp.float32)
    _, o = lax.scan(step, S0, (qc, kc, vc, gc))
    o = o.transpose(1, 0, 3, 2, 4).reshape(B, L, GLA_HEADS, GLA_HV).astype(x.dtype)
    o = rms_norm(o, norm_g).reshape(B, L, GLA_DV)
    o = o * jax.nn.silu(r)
    return o @ w_out


def shared_kv(x, valid, kv_w, kv_bf):
    B, L, _ = x.shape
    k, v, f_logit = jnp.split(x @ kv_w, [D_MODEL, 2 * D_MODEL], axis=-1)
    log_f = jax.nn.log_sigmoid((f_logit + kv_bf).astype(jnp.float32))
    log_f = jnp.where(valid[None, :, None], log_f, 0.0)
    c = jnp.cumsum(log_f, axis=1).transpose(0, 2, 1)
    kh = k.reshape(B, L, FOX_HEADS, FOX_HD).transpose(0, 2, 1, 3)
    vh = v.reshape(B, L, FOX_HEADS, FOX_HD).transpose(0, 2, 1, 3)
    return kh, vh, c


def fox_mixer(x, valid, kh, vh, c, w_in, w_out):
    B, L, _ = x.shape
    n_blocks = L // FOX_BLOCK
    q, og = jnp.split(x @ w_in, [D_MODEL], axis=-1)
    q = q.reshape(B, L, FOX_HEADS, FOX_HD).transpose(0, 2, 1, 3) * (FOX_HD ** -0.5)
    qb = q.reshape(B, FOX_HEADS, n_blocks, FOX_BLOCK, FOX_HD).transpose(2, 0, 1, 3, 4)
    cb = c.reshape(B, FOX_HEADS, n_blocks, FOX_BLOCK).transpose(2, 0, 1, 3)
    pos = jnp.arange(L)

    def block(args):
        i, q_i, c_i = args
        t = i * FOX_BLOCK + jnp.arange(FOX_BLOCK)
        logits = (jnp.einsum('bhqd,bhkd->bhqk', q_i, kh).astype(jnp.float32)
                  + c_i[..., None] - c[:, :, None, :])
        mask = (pos[None, :] <= t[:, None]) & (valid[None, :] | (pos[None, :] == t[:, None]))
        p = jax.nn.softmax(jnp.where(mask, logits, -jnp.inf), axis=-1).astype(vh.dtype)
        return jnp.einsum('bhqk,bhkd->bhqd', p, vh)

    o = lax.map(block, (jnp.arange(n_blocks), qb, cb))
    o = o.transpose(1, 0, 3, 2, 4).reshape(B, L, D_MODEL)
    o = o * jax.nn.sigmoid(og)
    return o @ w_out


def conv_ffn(x, valid, w_up, conv_w, conv_b, w_down):
    h = x @ w_up
    h = jnp.where(valid[None, :, None], h, 0)
    h = lax.conv_general_dilated(
        h, conv_w[:, None, :], window_strides=(1,), padding=[(CONV_W - 1, 0)],
        dimension_numbers=('NWC', 'WIO', 'NWC'), feature_group_count=2 * D_FF) + conv_b
    u, g = jnp.split(h, 2, axis=-1)
    return (jax.nn.silu(g) * u) @ w_down


def setup_inputs(seed: int = 0) -> dict:
    key = jax.random.key(seed)
    ks = jax.random.split(key, 20)

    def nrm(k, shape, fan_in, scale=1.0):
        return jax.random.normal(k, shape, jnp.float32) * (fan_in ** -0.5) * scale

    return {
        "x": jax.random.normal(ks[0], (BATCH, SEQ, D_MODEL), jnp.float32),
        "meta": jax.random.normal(ks[1], (N_META, D_MODEL), jnp.float32),
        "ln_g": 1.0 + 0.02 * jax.random.normal(ks[2], (DEPTH, 2, D_MODEL), jnp.float32),
        "ln_b": 0.02 * jax.random.normal(ks[3], (DEPTH, 2, D_MODEL), jnp.float32),
        "gla_w_in": nrm(ks[4], (N_A_LAYERS, D_MODEL, GLA_IN), D_MODEL),
        "gla_w_g2": nrm(ks[5], (N_A_LAYERS, GLA_RANK, GLA_DK), GLA_RANK),
        "gla_b_g2": 0.1 * jax.random.normal(ks[6], (N_A_LAYERS, GLA_DK), jnp.float32),
        "gla_norm_g": 1.0 + 0.02 * jax.random.normal(ks[7], (N_A_LAYERS, GLA_HV), jnp.float32),
        "gla_w_out": nrm(ks[8], (N_A_LAYERS, GLA_DV, D_MODEL), GLA_DV, BETA),
        "kv_w": nrm(ks[9], (D_MODEL, KV_OUT), D_MODEL),
        "kv_bf": 2.0 + 0.5 * jax.random.normal(ks[10], (FOX_HEADS,), jnp.float32),
        "fox_w_in": nrm(ks[11], (N_B_LAYERS, D_MODEL, 2 * D_MODEL), D_MODEL),
        "fox_w_out": nrm(ks[12], (N_B_LAYERS, D_MODEL, D_MODEL), D_MODEL, BETA),
        "ffn_w_up": nrm(ks[13], (DEPTH, D_MODEL, 2 * D_FF), D_MODEL),
        "ffn_conv_w": nrm(ks[14], (DEPTH, CONV_W, 2 * D_FF), CONV_W),
        "ffn_conv_b": 0.02 * jax.random.normal(ks[15], (DEPTH, 2 * D_FF), jnp.float32),
        "ffn_w_down": nrm(ks[16], (DEPTH, D_FF, D_MODEL), D_FF, BETA),
    }


def reference(x, meta, ln_g, ln_b, gla_w_in, gla_w_g2, gla_b_g2, gla_norm_g, gla_w_out,
              kv_w, kv_bf, fox_w_in, fox_w_out, ffn_w_up, ffn_conv_w, ffn_conv_b, ffn_w_down):
    B = x.shape[0]
    L = FRONT + x.shape[1]
    pad = jnp.zeros((B, N_PAD, D_MODEL), x.dtype)
    meta_b = jnp.broadcast_to(meta[None].astype(x.dtype), (B, N_META, D_MODEL))
    h = jnp.concatenate([pad, meta_b, x], axis=1)
    valid = jnp.arange(L) >= N_PAD

    kh = vh = c = None
    for l in range(DEPTH):
        if l < N_A_LAYERS:
            mix = gla_mixer(h, valid, gla_w_in[l], gla_w_g2[l], gla_b_g2[l], gla_norm_g[l], gla_w_out[l])
        else:
            if l == N_A_LAYERS:
                kh, vh, c = shared_kv(h, valid, kv_w, kv_bf)
            j = l - N_A_LAYERS
            mix = fox_mixer(h, valid, kh, vh, c, fox_w_in[j], fox_w_out[j])
        h = layer_norm(ALPHA * h + mix, ln_g[l, 0], ln_b[l, 0])
        ffn = conv_ffn(h, valid, ffn_w_up[l], ffn_conv_w[l], ffn_conv_b[l], ffn_w_down[l])
        h = layer_norm(ALPHA * h + ffn, ln_g[l, 1], ln_b[l, 1])
    return h[:, FRONT:]
```

```python
import contextlib
import numpy as np
import ml_dtypes
import concourse.bass as bass
import concourse.mybir as mybir
from concourse.bass_utils import run_bass_kernel_spmd

F32 = mybir.dt.float32
BF16 = mybir.dt.bfloat16
AF = mybir.ActivationFunctionType
ALU = mybir.AluOpType
NPBF = ml_dtypes.bfloat16

D = 2048
SEQ = 8192
DEPTH = 4
N_META = 16
NTOK = SEQ + N_META
NCORE = 8
TC = NTOK // NCORE
HALO = 2
TP = TC + HALO
LPAD = 8320
NT = LPAD // 128
ALPHA = (2.0 * DEPTH) ** 0.25
LN_EPS = 1e-5
GLA_HEADS, GLA_HK, GLA_HV, GLA_RANK, GLA_TAU = 4, 256, 512, 16, 16.0
GLA_DK, GLA_DV = 1024, 2048
FOX_HEADS, FOX_HD = 16, 128
D_FF = 5632
NEG = -30000.0

ENGS = ("pe", "act", "dve", "pool", "sp")
NDS = 8


class Prog:
    def __init__(self, nc):
        self.nc = nc
        self.ops = {e: [] for e in ENGS}
        self.cnt = {e: 0 for e in ENGS}
        self.dcnt = {e: 0 for e in ENGS}
        self.lastw = {}
        self.readers = {}
        self.known = {e: {} for e in ENGS}
        self.dknown = {e: {q: set() for q in ENGS} for e in ENGS}
        self.dmax = {e: {q: 0 for q in ENGS} for e in ENGS}

    def _deps(self, eng, reads, writes):
        deps = {}
        ddeps = set()

        def need(ev):
            if ev is None:
                return
            kind, e, idx = ev
            if kind == "d":
                ddeps.add((e, idx))
                return
            if e == eng == "pe":
                return
            if deps.get(e, 0) < idx:
                deps[e] = idx

        for t in reads:
            need(self.lastw.get(t))
        for t in writes:
            need(self.lastw.get(t))
            for r in self.readers.get(t, ()):
                need(r)
        out = []
        kn = self.known[eng]
        for e, idx in deps.items():
            if kn.get(e, 0) >= idx:
                continue
            kn[e] = idx
            out.append(("c", e, idx))
        for (q, idx) in sorted(ddeps):
            if idx in self.dknown[eng][q] or idx <= self.dmax[eng][q] - NDS:
                continue
            self.dknown[eng][q].add(idx)
            if idx > self.dmax[eng][q]:
                self.dmax[eng][q] = idx
            out.append(("d", q, idx))
        return out

    def _commit(self, ev, reads, writes):
        for t in reads:
            self.readers.setdefault(t, []).append(ev)
        for t in writes:
            self.lastw[t] = ev
            self.readers[t] = []

    def op(self, eng, fn, reads=(), writes=(), excl=()):
        if excl:
            reads = list(reads) + list(excl)
            writes = list(writes) + list(excl)
        waits = self._deps(eng, reads, writes)
        self.cnt[eng] += 1
        self.ops[eng].append(("c", fn, waits, 0))
        self._commit(("c", eng, self.cnt[eng]), reads, writes)

    def dma(self, eng, fn, reads=(), writes=()):
        waits = self._deps(eng, reads, writes)
        self.dcnt[eng] += 1
        i = self.dcnt[eng]
        if i > NDS:
            prev = i - NDS
            if not (prev in self.dknown[eng][eng] or prev <= self.dmax[eng][eng] - NDS):
                self.dknown[eng][eng].add(prev)
                self.dmax[eng][eng] = max(self.dmax[eng][eng], prev)
                waits.append(("d", eng, prev))
        self.ops[eng].append(("d", fn, waits, i))
        self._commit(("d", eng, i), reads, writes)

    def finish(self, eng="sp"):
        waits = []
        for e in ENGS:
            for i in range(max(1, self.dcnt[e] - NDS + 1), self.dcnt[e] + 1):
                waits.append(("d", e, i))
            if self.cnt[e]:
                waits.append(("c", e, self.cnt[e]))
        self.ops[eng].append(("w", None, waits, 0))

    def emit(self):
        nc = self.nc
        with contextlib.ExitStack() as st:
            csem = {e: st.enter_context(nc.semaphore("c_" + e)) for e in ENGS}
            dsem = {e: [st.enter_context(nc.semaphore("d_%s%d" % (e, k))) for k in range(NDS)]
                    for e in ("sp", "act", "pool")}
            block = st.enter_context(nc.Block())

            def run(engname):
                def body(eh):
                    for kind, fn, waits, di in self.ops[engname]:
                        for (k, e, idx) in waits:
                            if k == "c":
                                eh.wait_ge(csem[e], idx)
                            else:
                                eh.wait_ge(dsem[e][(idx - 1) % NDS], 16 * ((idx - 1) // NDS + 1))
                        if kind == "c":
                            fn(eh).then_inc(csem[engname], 1)
                        elif kind == "d":
                            fn(eh).then_inc(dsem[engname][(di - 1) % NDS], 16)
                return body

            block.sync(run("sp"))
            block.scalar(run("act"))
            block.vector(run("dve"))
            block.gpsimd(run("pool"))
            block.tensor(run("pe"))


def ntiles(total, maxn=512):
    n = -(-total // maxn)
    base = -(-total // n)
    out = []
    s = 0
    while s < total:
        e = min(total, s + base)
        out.append((s, e))
        s = e
    return out


class Ctx:
    def __init__(self, name):
        self.nc = bass.Bass("TRN2", target_bir_lowering=False)
        self.P = Prog(self.nc)
        self.ps = [self.nc.alloc_psum_tensor("ps%d" % b, [128, 512], F32) for b in range(8)]
        self.uid = 0

    def sb(self, name, shape, dt):
        return self.nc.alloc_sbuf_tensor(name, list(shape), dt)

    def din(self, name, shape, dt):
        return self.nc.dram_tensor(name, list(shape), dt, kind="ExternalInput").ap()

    def dout(self, name, shape, dt):
        return self.nc.dram_tensor(name, list(shape), dt, kind="ExternalOutput").ap()


def build_pre(kind):
    C = Ctx("pre_" + kind)
    nc, P = C.nc, C.P
    T = TC
    ht = C.din("ht", [D, T], F32)
    if kind == "gla":
        dout_w = 2 * GLA_DK + GLA_DV + GLA_RANK + GLA_DV
        w = C.din("w", [D, dout_w], F32)
        wg2 = C.din("wg2", [GLA_RANK, GLA_DK], F32)
        bg2 = C.din("bg2", [128, GLA_DK // 128], F32)
        segs = [("g", 2 * GLA_DK + GLA_DV, 16, "glow", None),
                ("q", 0, GLA_DK, "scale", BF16), ("k", GLA_DK, GLA_DK, "copy", BF16),
                ("v", 2 * GLA_DK, GLA_DV, "copy", BF16),
                ("r", 2 * GLA_DK + GLA_DV + GLA_RANK, GLA_DV, "copy", F32)]
        qscale = GLA_HK ** -0.5
        outs = {"q": C.dout("q", [GLA_DK, T], BF16), "k": C.dout("k", [GLA_DK, T], BF16),
                "v": C.dout("v", [GLA_DV, T], BF16), "r": C.dout("r", [GLA_DV, T], F32),
                "la": C.dout("la", [GLA_DK, T], F32)}
    else:
        wq = C.din("w", [D, 2 * D], F32)
        segs = [("q", 0, D, "scale", BF16), ("og", D, D, "copy", F32)]
        qscale = FOX_HD ** -0.5
        outs = {"q": C.dout("q", [D, T], BF16), "og": C.dout("og", [D, T], F32)}
        if kind == "fox_kv":
            wkv = C.din("wkv", [D, 2 * D + FOX_HEADS], F32)
            bf = C.din("bf", [FOX_HEADS, 1], F32)
            outs.update({"kk": C.dout("kk", [D, T], BF16), "vv": C.dout("vv", [D, T], BF16),
                         "lf": C.dout("lf", [FOX_HEADS, T], F32)})

    hs = C.sb("hs", [128, 16, T], F32)
    hb = C.sb("hb", [128, 16, T], BF16)
    NW = 4
    wt = [C.sb("wt%d" % i, [128, 16, 128], BF16) for i in range(NW)]
    NS = 3
    stf = [C.sb("stf%d" % i, [128, T], F32) for i in range(NS)]
    stb = [C.sb("stb%d" % i, [128, T], BF16) for i in range(NS)]
    tiles = ntiles(T)

    for c in range(16):
        q = "sp" if c % 2 == 0 else "act"
        P.dma(q, lambda e, c=c: e.dma_start(out=hs[:, c, :], in_=ht[c * 128:(c + 1) * 128, :]),
              writes=[("hs", c)])
        eng = "dve" if c % 2 == 0 else "pool"
        P.op(eng, lambda e, c=c: e.tensor_copy(out=hb[:, c, :], in_=hs[:, c, :]),
             reads=[("hs", c)], writes=[("hb", c)])
    hb_tok = [("hb", c) for c in range(16)]

    state = {"wi": 0, "si": 0, "bank": 0}

    def proj_chunk(wsrc, col0, ncols, epilogue):
        wi = state["wi"] % NW
        state["wi"] += 1
        P.dma("pool", lambda e: e.dma_start(
            out=wt[wi][:, :, 0:ncols],
            in_=wsrc[:, col0:col0 + ncols].rearrange("(kc p) j -> p kc j", p=128)),
            writes=[("wt", wi)])
        for (t0, t1) in tiles:
            b = state["bank"] % 4
            state["bank"] += 1
            for kc in range(16):
                P.op("pe", lambda e, kc=kc, b=b, t0=t0, t1=t1: e.matmul(
                    C.ps[b][0:ncols, 0:t1 - t0], lhsT=wt[wi][:, kc, 0:ncols], rhs=hb[:, kc, t0:t1],
                    start=(kc == 0), stop=(kc == 15)),
                    reads=[("wt", wi), ("hb", kc)], excl=[("ps", b)])
            epilogue(b, t0, t1)

    if kind == "gla":
        gl = C.sb("gl", [GLA_RANK, T], BF16)
        wg2f = C.sb("wg2f", [GLA_RANK, GLA_DK], F32)
        wg2b = C.sb("wg2b", [GLA_RANK, GLA_DK], BF16)
        bg = C.sb("bg", [128, 8], F32)
        nbg = C.sb("nbg", [128, 8], F32)
        P.dma("sp", lambda e: e.dma_start(out=wg2f[:], in_=wg2), writes=["wg2f"])
        P.dma("sp", lambda e: e.dma_start(out=bg[:], in_=bg2), writes=["bg"])
        P.op("dve", lambda e: e.tensor_copy(out=wg2b[:], in_=wg2f[:]), reads=["wg2f"], writes=["wg2b"])
        P.op("dve", lambda e: e.tensor_scalar(out=nbg[:], in0=bg[:], scalar1=-1.0, scalar2=None, op0=ALU.mult),
             reads=["bg"], writes=["nbg"])

        def ep_glow(b, t0, t1):
            P.op("act", lambda e: e.activation(out=gl[:, t0:t1], in_=C.ps[b][0:GLA_RANK, 0:t1 - t0], func=AF.Copy),
                 excl=[("ps", b)], writes=[("gl", t0)])
        proj_chunk(w, 2 * GLA_DK + GLA_DV, GLA_RANK, ep_glow)
        for fc in range(8):
            si = state["si"] % NS
            state["si"] += 1
            for (t0, t1) in tiles:
                b = 4 + (state["bank"] % 2)
                state["bank"] += 1
                P.op("pe", lambda e, fc=fc, b=b, t0=t0, t1=t1: e.matmul(
                    C.ps[b][:, 0:t1 - t0], lhsT=wg2b[:, fc * 128:(fc + 1) * 128], rhs=gl[:, t0:t1],
                    start=True, stop=True), reads=["wg2b", ("gl", t0)], excl=[("ps", b)])
                P.op("act", lambda e, fc=fc, b=b, t0=t0, t1=t1, si=si: e.activation(
                    out=stf[si][:, t0:t1], in_=C.ps[b][:, 0:t1 - t0], func=AF.Exp,
                    bias=nbg[:, fc:fc + 1], scale=-1.0),
                    reads=["nbg"], excl=[("ps", b)], writes=[("stf", si)])
            P.op("act", lambda e, si=si: e.activation(out=stf[si][:], in_=stf[si][:], func=AF.Ln, bias=1.0, scale=1.0),
                 reads=[("stf", si)], writes=[("stf", si)])
            P.op("dve", lambda e, si=si: e.tensor_scalar(out=stf[si][:], in0=stf[si][:], scalar1=-1.0 / GLA_TAU,
                                                        scalar2=None, op0=ALU.mult),
                 reads=[("stf", si)], writes=[("stf", si)])
            P.dma("sp", lambda e, fc=fc, si=si: e.dma_start(out=outs["la"][fc * 128:(fc + 1) * 128, :], in_=stf[si][:]),
                  reads=[("stf", si)])

    def run_seg(wsrc, name, col0, ncols, mode, odt, oname=None):
        oname = oname or name
        for ch in range(ncols // 128):
            si = state["si"] % NS
            state["si"] += 1
            dst = stf[si] if odt == F32 else stb[si]
            tok = ("stf", si) if odt == F32 else ("stb", si)

            def ep(b, t0, t1, dst=dst, tok=tok):
                if mode == "scale":
                    P.op("act", lambda e: e.activation(out=dst[:, t0:t1], in_=C.ps[b][:, 0:t1 - t0], func=AF.Copy,
                                                       scale=float(qscale)),
                         excl=[("ps", b)], writes=[tok])
                else:
                    eng = "dve" if (t0 // 8) % 2 == 0 else "act"
                    if eng == "dve":
                        P.op("dve", lambda e: e.tensor_copy(out=dst[:, t0:t1], in_=C.ps[b][:, 0:t1 - t0]),
                             excl=[("ps", b)], writes=[tok])
                    else:
                        P.op("act", lambda e: e.activation(out=dst[:, t0:t1], in_=C.ps[b][:, 0:t1 - t0], func=AF.Copy),
                             excl=[("ps", b)], writes=[tok])
            proj_chunk(wsrc, col0 + ch * 128, 128, ep)
            P.dma("sp" if ch % 2 == 0 else "act",
                  lambda e, ch=ch, dst=dst: e.dma_start(out=outs[oname][ch * 128:(ch + 1) * 128, :], in_=dst[:]),
                  reads=[tok])

    if kind == "gla":
        for (name, col0, ncols, mode, odt) in segs[1:]:
            run_seg(w, name, col0, ncols, mode, odt)
    else:
        for (name, col0, ncols, mode, odt) in segs:
            run_seg(wq, name, col0, ncols, mode, odt)
        if kind == "fox_kv":
            run_seg(wkv, "kk", 0, D, "copy", BF16)
            run_seg(wkv, "vv", D, D, "copy", BF16)
            bfs = C.sb("bfs", [FOX_HEADS, 1], F32)
            nbf = C.sb("nbf", [FOX_HEADS, 1], F32)
            lfs = C.sb("lfs", [FOX_HEADS, T], F32)
            P.dma("sp", lambda e: e.dma_start(out=bfs[:], in_=bf), writes=["bfs"])
            P.op("dve", lambda e: e.tensor_scalar(out=nbf[:], in0=bfs[:], scalar1=-1.0, scalar2=None, op0=ALU.mult),
                 reads=["bfs"], writes=["nbf"])

            def ep_f(b, t0, t1):
                P.op("act", lambda e: e.activation(out=lfs[:, t0:t1], in_=C.ps[b][0:FOX_HEADS, 0:t1 - t0], func=AF.Exp,
                                                   bias=nbf[:, 0:1], scale=-1.0),
                     reads=["nbf"], excl=[("ps", b)], writes=["lfs"])
            proj_chunk(wkv, 2 * D, FOX_HEADS, ep_f)
            P.op("act", lambda e: e.activation(out=lfs[:], in_=lfs[:], func=AF.Ln, bias=1.0, scale=1.0),
                 reads=["lfs"], writes=["lfs"])
            P.op("dve", lambda e: e.tensor_scalar(out=lfs[:], in0=lfs[:], scalar1=-1.0, scalar2=None, op0=ALU.mult),
                 reads=["lfs"], writes=["lfs"])
            P.dma("sp", lambda e: e.dma_start(out=outs["lf"], in_=lfs[:]), reads=["lfs"])

    P.finish()
    P.emit()
    return nc


def build_post(kind):
    C = Ctx("post_" + kind)
    nc, P = C.nc, C.P
    ht = C.din("ht", [D, TP], F32)
    ot = C.din("ot", [D, TP], F32)
    gt = C.din("gt", [D, TP], F32)
    vm = C.din("vmask", [128, TP], F32)
    wo = C.din("wo", [D, D], F32)
    lnp = C.din("lnp", [128, 4, 16], F32)
    wup = C.din("wup", [D, 2 * D_FF], F32)
    cwd = C.din("cw", [128, 88, 4], F32)
    wdn = C.din("wdn", [D_FF, D], F32)
    hout = C.dout("hout", [D, TC], F32)
    if kind == "gla":
        ngd = C.din("ng", [128, 4], F32)

    hs = C.sb("hs", [128, 16, TP], F32)
    hb = C.sb("hb", [128, 16, TP], BF16)
    tA = [C.sb("tA%d" % i, [128, TP], F32) for i in range(2)]
    tB = [C.sb("tB%d" % i, [128, TP], F32) for i in range(2)]
    tC = [C.sb("tC%d" % i, [128, TP], F32) for i in range(2)]
    NW = 4
    wt = [C.sb("wt%d" % i, [128, 16, 128], BF16) for i in range(NW)]
    GP = 11
    NG = 4
    act = C.sb("act", [128, GP, TC], BF16)
    NWD = 3
    wd = [C.sb("wd%d" % i, [128, GP, 128], BF16) for i in range(NWD)]
    st = [C.sb("st%d" % i, [128, TP], F32) for i in range(3)]
    ones = C.sb("ones", [128, 128], F32)
    lns = C.sb("lns", [128, 4, 16], F32)
    cws = C.sb("cws", [128, 88, 4], F32)
    vms = C.sb("vms", [128, TP], F32)
    tiles_p = ntiles(TP)
    tiles_c = ntiles(TC)
    cnt = {"w": 0, "wd": 0, "a": 0, "b": 0, "c": 0, "dq": 0}

    def dq():
        cnt["dq"] += 1
        return "sp" if cnt["dq"] % 2 else "act"

    P.op("pool", lambda e: e.memset(ones[:], 1.0), writes=["ones"])
    P.dma("sp", lambda e: e.dma_start(out=lns[:], in_=lnp), writes=["lns"])
    P.dma("sp", lambda e: e.dma_start(out=cws[:], in_=cwd), writes=["cws"])
    P.dma("sp", lambda e: e.dma_start(out=vms[:], in_=vm), writes=["vms"])
    if kind == "gla":
        ngs = C.sb("ngs", [128, 4], F32)
        P.dma("sp", lambda e: e.dma_start(out=ngs[:], in_=ngd), writes=["ngs"])
    for c in range(16):
        P.dma(dq(), lambda e, c=c: e.dma_start(out=hs[:, c, :], in_=ht[c * 128:(c + 1) * 128, :]),
              writes=[("hs", c)])

    def load(buf, tokname, src, c):
        i = cnt[tokname] % 2
        cnt[tokname] += 1
        P.dma(dq(), lambda e: e.dma_start(out=buf[i][:], in_=src[c * 128:(c + 1) * 128, :]),
              writes=[(tokname, i)])
        return i

    def rstd_from(src_banks, tiles, scale, dst, dst_tok):
        for ti, (t0, t1) in enumerate(tiles):
            b = src_banks[ti]
            P.op("dve", lambda e, b=b, t0=t0, t1=t1: e.tensor_scalar(
                out=dst[:, t0:t1], in0=C.ps[b][:, 0:t1 - t0], scalar1=float(scale), scalar2=float(LN_EPS),
                op0=ALU.mult, op1=ALU.add), excl=[("ps", b)], writes=[dst_tok])
        P.op("act", lambda e: e.activation(out=dst[:], in_=dst[:], func=AF.Sqrt), reads=[dst_tok],
             writes=[dst_tok])
        P.op("dve", lambda e: e.reciprocal(out=dst[:], in_=dst[:]), reads=[dst_tok],
             writes=[dst_tok])

    if kind == "gla":
        for hd in range(4):
            for cc in range(4):
                c = hd * 4 + cc
                ia = load(tA, "a", ot, c)
                ic = cnt["c"] % 2
                cnt["c"] += 1
                P.op("act", lambda e, ia=ia, ic=ic: e.activation(out=tC[ic][:], in_=tA[ia][:], func=AF.Square),
                     reads=[("a", ia)], writes=[("c", ic)])
                for ti, (t0, t1) in enumerate(tiles_p):
                    P.op("pe", lambda e, ic=ic, ti=ti, t0=t0, t1=t1, cc=cc: e.matmul(
                        C.ps[ti][:, 0:t1 - t0], lhsT=ones[:], rhs=tC[ic][:, t0:t1], start=(cc == 0), stop=(cc == 3)),
                        reads=["ones", ("c", ic)], excl=[("ps", ti)])
            rstd_from([0, 1, 2], tiles_p, 1.0 / GLA_HV, st[0], ("st", 0))
            for cc in range(4):
                c = hd * 4 + cc
                ia = load(tA, "a", ot, c)
                ib = load(tB, "b", gt, c)
                P.op("act", lambda e, ib=ib: e.activation(out=tB[ib][:], in_=tB[ib][:], func=AF.Silu),
                     reads=[("b", ib)], writes=[("b", ib)])
                P.op("dve", lambda e, ia=ia: e.tensor_tensor(out=tA[ia][:], in0=tA[ia][:], in1=st[0][:], op=ALU.mult),
                     reads=[("a", ia), ("st", 0)], writes=[("a", ia)])
                P.op("pool", lambda e, ib=ib, cc=cc: e.tensor_scalar(
                    out=tB[ib][:], in0=tB[ib][:], scalar1=ngs[:, cc:cc + 1], scalar2=None, op0=ALU.mult),
                    reads=[("b", ib), "ngs"], writes=[("b", ib)])
                P.op("pool", lambda e, ia=ia, ib=ib, c=c: e.tensor_tensor(
                    out=hb[:, c, :], in0=tA[ia][:], in1=tB[ib][:], op=ALU.mult),
                    reads=[("a", ia), ("b", ib)], writes=[("hb", c)])
    else:
        for c in range(16):
            ia = load(tA, "a", ot, c)
            ib = load(tB, "b", gt, c)
            P.op("act", lambda e, ib=ib: e.activation(out=tB[ib][:], in_=tB[ib][:], func=AF.Sigmoid),
                 reads=[("b", ib)], writes=[("b", ib)])
            P.op("dve" if c % 2 == 0 else "pool",
                 lambda e, ia=ia, ib=ib, c=c: e.tensor_tensor(out=hb[:, c, :], in0=tA[ia][:], in1=tB[ib][:], op=ALU.mult),
                 reads=[("a", ia), ("b", ib)], writes=[("hb", c)])

    def wload(src_ap):
        wi = cnt["w"] % NW
        cnt["w"] += 1
        P.dma("pool", lambda e: e.dma_start(out=wt[wi][:], in_=src_ap.rearrange("(kc p) j -> p kc j", p=128)),
              writes=[("wt", wi)])
        return wi

    obank = [6, 7]
    ob = {"i": 0}
    wow = {}

    def ensure_wo(fc):
        if fc in wow or fc >= 16:
            return
        wow[fc] = wload(wo[:, fc * 128:(fc + 1) * 128])

    for fc in range(16):
        for k in range(3):
            ensure_wo(fc + k)
        wi = wow[fc]
        for (t0, t1) in tiles_p:
            b = obank[ob["i"] % 2]
            ob["i"] += 1
            for kc in range(16):
                P.op("pe", lambda e, kc=kc, b=b, t0=t0, t1=t1, wi=wi: e.matmul(
                    C.ps[b][:, 0:t1 - t0], lhsT=wt[wi][:, kc, :], rhs=hb[:, kc, t0:t1], start=(kc == 0), stop=(kc == 15)),
                    reads=[("wt", wi), ("hb", kc)], excl=[("ps", b)])
            P.op("dve", lambda e, fc=fc, b=b, t0=t0, t1=t1: e.scalar_tensor_tensor(
                out=hs[:, fc, t0:t1], in0=hs[:, fc, t0:t1], scalar=float(ALPHA), in1=C.ps[b][:, 0:t1 - t0],
                op0=ALU.mult, op1=ALU.add), reads=[("hs", fc)], writes=[("hs", fc)], excl=[("ps", b)])

    def layer_norm(lo, gi, bi, make_hb):
        n = TP - lo
        tl = [(lo + a, lo + b_) for (a, b_) in ntiles(n)]
        for c in range(16):
            ic = cnt["c"] % 2
            cnt["c"] += 1
            P.op("act", lambda e, c=c, ic=ic: e.activation(out=tC[ic][:, lo:TP], in_=hs[:, c, lo:TP], func=AF.Square),
                 reads=[("hs", c)], writes=[("c", ic)])
            for ti, (t0, t1) in enumerate(tl):
                P.op("pe", lambda e, c=c, ti=ti, t0=t0, t1=t1: e.matmul(
                    C.ps[ti][:, 0:t1 - t0], lhsT=ones[:], rhs=hs[:, c, t0:t1], start=(c == 0), stop=(c == 15)),
                    reads=["ones", ("hs", c)], excl=[("ps", ti)])
                P.op("pe", lambda e, c=c, ic=ic, ti=ti, t0=t0, t1=t1: e.matmul(
                    C.ps[3 + ti][:, 0:t1 - t0], lhsT=ones[:], rhs=tC[ic][:, t0:t1], start=(c == 0), stop=(c == 15)),
                    reads=["ones", ("c", ic)], excl=[("ps", 3 + ti)])
        mean, var, tmp = st[0], st[1], st[2]
        for ti, (t0, t1) in enumerate(tl):
            P.op("dve", lambda e, ti=ti, t0=t0, t1=t1: e.tensor_scalar(
                out=mean[:, t0:t1], in0=C.ps[ti][:, 0:t1 - t0], scalar1=1.0 / D, scalar2=None, op0=ALU.mult),
                excl=[("ps", ti)], writes=[("st", 0)])
            P.op("dve", lambda e, ti=ti, t0=t0, t1=t1: e.tensor_scalar(
                out=var[:, t0:t1], in0=C.ps[3 + ti][:, 0:t1 - t0], scalar1=1.0 / D, scalar2=float(LN_EPS),
                op0=ALU.mult, op1=ALU.add), excl=[("ps", 3 + ti)], writes=[("st", 1)])
        P.op("dve", lambda e: e.tensor_tensor(out=tmp[:, lo:TP], in0=mean[:, lo:TP], in1=mean[:, lo:TP], op=ALU.mult),
             reads=[("st", 0)], writes=[("st", 2)])
        P.op("dve", lambda e: e.tensor_tensor(out=var[:, lo:TP], in0=var[:, lo:TP], in1=tmp[:, lo:TP], op=ALU.subtract),
             reads=[("st", 1), ("st", 2)], writes=[("st", 1)])
        P.op("act", lambda e: e.activation(out=var[:, lo:TP], in_=var[:, lo:TP], func=AF.Sqrt), reads=[("st", 1)], writes=[("st", 1)])
        P.op("dve", lambda e: e.reciprocal(out=var[:, lo:TP], in_=var[:, lo:TP]), reads=[("st", 1)], writes=[("st", 1)])
        for c in range(16):
            ia = cnt["a"] % 2
            cnt["a"] += 1
            P.op("dve", lambda e, c=c, ia=ia: e.tensor_tensor(out=tA[ia][:, lo:TP], in0=hs[:, c, lo:TP], in1=mean[:, lo:TP],
                                                             op=ALU.subtract),
                 reads=[("hs", c), ("st", 0)], writes=[("a", ia)])
            P.op("pool", lambda e, ia=ia: e.tensor_tensor(out=tA[ia][:, lo:TP], in0=tA[ia][:, lo:TP], in1=var[:, lo:TP],
                                                          op=ALU.mult),
                 reads=[("a", ia), ("st", 1)], writes=[("a", ia)])
            P.op("dve", lambda e, c=c, ia=ia: e.tensor_scalar(
                out=hs[:, c, lo:TP], in0=tA[ia][:, lo:TP], scalar1=lns[:, gi, c:c + 1], scalar2=lns[:, bi, c:c + 1],
                op0=ALU.mult, op1=ALU.add), reads=[("a", ia), "lns"], writes=[("hs", c)])
            if make_hb:
                P.op("pool", lambda e, c=c: e.tensor_tensor(out=hb[:, c, :], in0=hs[:, c, :], in1=vms[:], op=ALU.mult),
                     reads=[("hs", c), "vms"], writes=[("hb", c)])

    layer_norm(0, 0, 1, True)

    xu, xg, cu, cg = tA, tB, tC[0], tC[1]
    upw = {}
    dnw = {}

    def ensure_up(j):
        if j in upw or j >= NG * GP:
            return
        upw[j] = (wload(wup[:, j * 128:(j + 1) * 128]), wload(wup[:, D_FF + j * 128:D_FF + (j + 1) * 128]))

    def ensure_down(g, fc):
        if (g, fc) in dnw or fc >= 16:
            return
        wi = cnt["wd"] % NWD
        cnt["wd"] += 1
        P.dma("pool", lambda e: e.dma_start(
            out=wd[wi][:], in_=wdn[g * GP * 128:(g + 1) * GP * 128, fc * 128:(fc + 1) * 128].rearrange(
                "(jj p) m -> p jj m", p=128)), writes=[("wd", wi)])
        dnw[(g, fc)] = wi
    for g in range(NG):
        for jj in range(GP):
            j = g * GP + jj
            ensure_up(j)
            ensure_up(j + 1)
            if jj == GP - 1:
                ensure_down(g, 0)
                ensure_down(g, 1)
            wa, wb = upw[j]
            ia = cnt["a"] % 2
            cnt["a"] += 1
            ib = cnt["b"] % 2
            cnt["b"] += 1
            for ti, (t0, t1) in enumerate(tiles_p):
                for (wi, bank, dst, tok, eng) in ((wa, ti, xu[ia], ("a", ia), "act"), (wb, 3 + ti, xg[ib], ("b", ib), "act")):
                    for kc in range(16):
                        P.op("pe", lambda e, kc=kc, wi=wi, bank=bank, t0=t0, t1=t1: e.matmul(
                            C.ps[bank][:, 0:t1 - t0], lhsT=wt[wi][:, kc, :], rhs=hb[:, kc, t0:t1],
                            start=(kc == 0), stop=(kc == 15)),
                            reads=[("wt", wi), ("hb", kc)], excl=[("ps", bank)])
                    P.op(eng, lambda e, bank=bank, dst=dst, t0=t0, t1=t1: e.activation(
                        out=dst[:, t0:t1], in_=C.ps[bank][:, 0:t1 - t0], func=AF.Copy),
                        excl=[("ps", bank)], writes=[tok])
            for (eng, x, xtok, y, ytok, col) in (("dve", xu[ia], ("a", ia), cu, ("c", 0), j), ("pool", xg[ib], ("b", ib), cg, ("c", 1), 44 + j)):
                P.op(eng, lambda e, x=x, y=y, col=col: e.tensor_scalar(
                    out=y[:, 0:TC], in0=x[:, 2:TP], scalar1=cws[:, col, 2:3], scalar2=cws[:, col, 3:4],
                    op0=ALU.mult, op1=ALU.add), reads=[xtok, "cws"], writes=[ytok])
                for tap in (1, 0):
                    if eng == "dve":
                        P.op(eng, lambda e, x=x, y=y, col=col, tap=tap: e.scalar_tensor_tensor(
                            out=y[:, 0:TC], in0=x[:, tap:tap + TC], scalar=cws[:, col, tap:tap + 1], in1=y[:, 0:TC],
                            op0=ALU.mult, op1=ALU.add), reads=[xtok, "cws", ytok], writes=[ytok])
                    else:
                        P.op(eng, lambda e, x=x, col=col, tap=tap: e.tensor_scalar(
                            out=st[2][:, 0:TC], in0=x[:, tap:tap + TC], scalar1=cws[:, col, tap:tap + 1], scalar2=None,
                            op0=ALU.mult), reads=[xtok, "cws"], writes=[("st", 2)])
                        P.op(eng, lambda e, y=y: e.tensor_tensor(out=y[:, 0:TC], in0=y[:, 0:TC], in1=st[2][:, 0:TC], op=ALU.add),
                             reads=[ytok, ("st", 2)], writes=[ytok])
            P.op("act", lambda e: e.activation(out=cg[:, 0:TC], in_=cg[:, 0:TC], func=AF.Silu), reads=[("c", 1)], writes=[("c", 1)])
            P.op("dve", lambda e, jj=jj: e.tensor_tensor(out=act[:, jj, :], in0=cu[:, 0:TC], in1=cg[:, 0:TC], op=ALU.mult),
                 reads=[("c", 0), ("c", 1)], writes=[("act", jj)])
        for fc in range(16):
            ensure_down(g, fc)
            ensure_down(g, fc + 1)
            ensure_down(g, fc + 2)
            wi = dnw[(g, fc)]
            for (t0, t1) in tiles_c:
                b = obank[ob["i"] % 2]
                ob["i"] += 1
                for jj in range(GP):
                    P.op("pe", lambda e, jj=jj, wi=wi, b=b, t0=t0, t1=t1: e.matmul(
                        C.ps[b][:, 0:t1 - t0], lhsT=wd[wi][:, jj, :], rhs=act[:, jj, t0:t1],
                        start=(jj == 0), stop=(jj == GP - 1)),
                        reads=[("wd", wi), ("act", jj)], excl=[("ps", b)])
                P.op("dve", lambda e, fc=fc, b=b, t0=t0, t1=t1, g=g: e.scalar_tensor_tensor(
                    out=hs[:, fc, 2 + t0:2 + t1], in0=hs[:, fc, 2 + t0:2 + t1], scalar=float(ALPHA if g == 0 else 1.0),
                    in1=C.ps[b][:, 0:t1 - t0], op0=ALU.mult, op1=ALU.add),
                    reads=[("hs", fc)], writes=[("hs", fc)], excl=[("ps", b)])

    layer_norm(2, 2, 3, False)
    for c in range(16):
        P.dma(dq(), lambda e, c=c: e.dma_start(out=hout[c * 128:(c + 1) * 128, :], in_=hs[:, c, 2:TP]),
              reads=[("hs", c)])
    P.finish()
    P.emit()
    return nc


def build_scan(debug=False):
    C = Ctx("scan")
    nc, P = C.nc, C.P
    L = LPAD
    qf = C.din("qf", [256, L], BF16)
    kf = C.din("kf", [256, L], BF16)
    ktm = C.din("ktm", [L, 256], BF16)
    vtm = C.din("vtm", [L, 256], BF16)
    latm = C.din("latm", [L, 256], F32)
    cmd = C.din("cm", [128, 2, 128], F32)
    od = C.dout("o", [L, 256], F32)
    GT = 13
    NGRP = NT // GT
    GL = GT * 128
    qin = [C.sb("qin%d" % i, [128, 2, GL], BF16) for i in range(2)]
    kin = [C.sb("kin%d" % i, [128, 2, GL], BF16) for i in range(2)]
    ktin = [C.sb("ktin%d" % i, [128, GT, 256], BF16) for i in range(2)]
    vtin = [C.sb("vtin%d" % i, [128, GT, 256], BF16) for i in range(2)]
    lain = [C.sb("lain%d" % i, [128, GT, 256], F32) for i in range(2)]
    cm = C.sb("cm_sb", [128, 2, 128], F32)
    eb = [C.sb("eb%d" % i, [128, 2, 128], F32) for i in range(2)]
    enb = [C.sb("enb%d" % i, [128, 2, 128], F32) for i in range(2)]
    qg = [C.sb("qg%d" % i, [128, 2, 128], BF16) for i in range(2)]
    kg = [C.sb("kg%d" % i, [128, 2, 128], BF16) for i in range(2)]
    ekd = [C.sb("ekd%d" % i, [128, 256], F32) for i in range(2)]
    kd = [C.sb("kd%d" % i, [128, 256], BF16) for i in range(2)]
    attm = [C.sb("attm%d" % i, [128, 128], BF16) for i in range(2)]
    osb = [C.sb("osb%d" % i, [64, 2, 256], F32) for i in range(2)]
    S = C.sb("S_sb", [128, 2, 256], F32)
    Sb = C.sb("Sb_sb", [128, 2, 256], BF16)
    ps = C.ps

    P.dma("sp", lambda e: e.dma_start(out=cm[:], in_=cmd), writes=["cm"])
    P.op("dve", lambda e: e.memset(S[:], 0.0), writes=[("S", 0), ("S", 1)])
    P.op("pool", lambda e: e.memset(Sb[:], 0.0), writes=[("Sb", 0), ("Sb", 1)])

    def load_group(g):
        i = g % 2
        c0 = g * GL
        P.dma("sp", lambda e: e.dma_start(out=qin[i][:], in_=qf[:, c0:c0 + GL].rearrange("(kc p) t -> p kc t", p=128)),
              writes=[("qin", i)])
        P.dma("act", lambda e: e.dma_start(out=kin[i][:], in_=kf[:, c0:c0 + GL].rearrange("(kc p) t -> p kc t", p=128)),
              writes=[("kin", i)])
        P.dma("sp", lambda e: e.dma_start(out=ktin[i][:], in_=ktm[c0:c0 + GL, :].rearrange("(t p) k -> p t k", p=128)),
              writes=[("ktin", i)])
        P.dma("act", lambda e: e.dma_start(out=vtin[i][:], in_=vtm[c0:c0 + GL, :].rearrange("(t p) k -> p t k", p=128)),
              writes=[("vtin", i)])
        P.dma("sp", lambda e: e.dma_start(out=lain[i][:], in_=latm[c0:c0 + GL, :].rearrange("(t p) k -> p t k", p=128)),
              writes=[("lain", i)])

    def precompute(tt):
        g, tl = divmod(tt, GT)
        gi = g % 2
        pr = tt % 2
        bb = 0 if pr == 0 else 6
        b2 = 1 if pr == 0 else 7
        cs = slice(tl * 128, (tl + 1) * 128)
        for kc in range(2):
            P.op("pe", lambda e, kc=kc: e.matmul(ps[bb][:, kc * 128:(kc + 1) * 128], lhsT=lain[gi][:, tl, kc * 128:(kc + 1) * 128],
                                                rhs=cm[:, 0, :], start=True, stop=True),
                 reads=[("lain", gi), "cm"], excl=[("ps", bb)])
        P.op("act", lambda e: e.activation(out=eb[pr][:].rearrange("p a b -> p (a b)"), in_=ps[bb][:, 0:256], func=AF.Exp),
             excl=[("ps", bb)], writes=[("eb", pr)])
        P.op("act", lambda e: e.activation(out=enb[pr][:].rearrange("p a b -> p (a b)"), in_=ps[bb][:, 0:256], func=AF.Exp, scale=-1.0),
             excl=[("ps", bb)], writes=[("enb", pr)])
        P.op("dve", lambda e: e.tensor_tensor(out=qg[pr][:], in0=qin[gi][:, :, cs], in1=eb[pr][:], op=ALU.mult),
             reads=[("qin", gi), ("eb", pr)], writes=[("qg", pr)])
        P.op("pool", lambda e: e.tensor_tensor(out=kg[pr][:], in0=kin[gi][:, :, cs], in1=enb[pr][:], op=ALU.mult),
             reads=[("kin", gi), ("enb", pr)], writes=[("kg", pr)])
        P.op("pe", lambda e: e.matmul(ps[b2][:, 0:256], lhsT=cm[:, 1, :], rhs=lain[gi][:, tl, :], start=True, stop=True),
             reads=[("lain", gi), "cm"], excl=[("ps", b2)])
        P.op("act", lambda e: e.activation(out=ekd[pr][:], in_=ps[b2][:, 0:256], func=AF.Exp),
             excl=[("ps", b2)], writes=[("ekd", pr)])
        P.op("pool", lambda e: e.tensor_tensor(out=kd[pr][:], in0=ktin[gi][:, tl, :], in1=ekd[pr][:], op=ALU.mult),
             reads=[("ktin", gi), ("ekd", pr)], writes=[("kd", pr)])
        for kc in range(2):
            P.op("pe", lambda e, kc=kc: e.matmul(ps[b2][:, 256:384], lhsT=kg[pr][:, kc, :], rhs=qg[pr][:, kc, :],
                                                start=(kc == 0), stop=(kc == 1)),
                 reads=[("kg", pr), ("qg", pr)], excl=[("ps", b2)])
        P.op("dve", lambda e: e.tensor_tensor(out=attm[pr][:], in0=ps[b2][:, 256:384], in1=cm[:, 0, :], op=ALU.mult),
             reads=["cm"], excl=[("ps", b2)], writes=[("attm", pr)])

    def chain(tt):
        g, tl = divmod(tt, GT)
        gi = g % 2
        pr = tt % 2
        for s in range(2):
            rs = slice(s * 64, (s + 1) * 64)
            ob = 3 + s
            P.op("pe", lambda e, rs=rs, ob=ob: e.matmul(ps[ob][0:64, 0:256], lhsT=attm[pr][rs, rs], rhs=vtin[gi][rs, tl, :],
                                          start=True, stop=False),
                 reads=[("attm", pr), ("vtin", gi)], excl=[("ps", ob)])
            for kc in range(2):
                P.op("pe", lambda e, kc=kc, rs=rs, ob=ob: e.matmul(ps[ob][0:64, 0:256], lhsT=qg[pr][:, kc, rs], rhs=Sb[:, kc, :],
                                                    start=False, stop=(kc == 1)),
                     reads=[("qg", pr), ("Sb", kc)], excl=[("ps", ob)])
            P.op("act", lambda e, s=s, ob=ob: e.activation(out=osb[pr][:, s, :], in_=ps[ob][0:64, 0:256], func=AF.Copy),
                 excl=[("ps", ob)], writes=[("osb", pr)])
            for kc in range(2):
                P.op("pe", lambda e, kc=kc, rs=rs: e.matmul(ps[5][:, kc * 256:(kc + 1) * 256], lhsT=kd[pr][rs, kc * 128:(kc + 1) * 128],
                                                    rhs=vtin[gi][rs, tl, :], start=True, stop=True),
                     reads=[("kd", pr), ("vtin", gi)], excl=[("ps", 5)])
            for kc in range(2):
                P.op("dve", lambda e, kc=kc, s=s: e.scalar_tensor_tensor(
                    out=S[:, kc, :], in0=S[:, kc, :], scalar=eb[pr][:, kc, s * 64 + 63:s * 64 + 64],
                    in1=ps[5][:, kc * 256:(kc + 1) * 256], op0=ALU.mult, op1=ALU.add),
                    reads=[("S", kc), ("eb", pr)], writes=[("S", kc)], excl=[("ps", 5)])
                P.op("pool", lambda e, kc=kc: e.tensor_copy(out=Sb[:, kc, :], in_=S[:, kc, :]),
                     reads=[("S", kc)], writes=[("Sb", kc)])
        P.dma("sp" if tt % 2 == 0 else "act",
              lambda e: e.dma_start(out=od[tt * 128:(tt + 1) * 128, :].rearrange("(s i) v -> i s v", s=2), in_=osb[pr][:]),
              reads=[("osb", pr)])

    load_group(0)
    precompute(0)
    if debug:
        dbg = {n: C.dout("dbg_" + n, shp, dt) for n, shp, dt in (("eb", [128, 256], F32), ("qg", [128, 256], BF16),
               ("kg", [128, 256], BF16), ("kd", [128, 256], BF16), ("attm", [128, 128], BF16), ("ekd", [128, 256], F32))}
        P.dma("sp", lambda e: e.dma_start(out=dbg["eb"], in_=eb[0][:].rearrange("p a b -> p (a b)")), reads=[("eb", 0)])
        P.dma("sp", lambda e: e.dma_start(out=dbg["qg"], in_=qg[0][:].rearrange("p a b -> p (a b)")), reads=[("qg", 0)])
        P.dma("sp", lambda e: e.dma_start(out=dbg["kg"], in_=kg[0][:].rearrange("p a b -> p (a b)")), reads=[("kg", 0)])
        P.dma("sp", lambda e: e.dma_start(out=dbg["kd"], in_=kd[0][:]), reads=[("kd", 0)])
        P.dma("sp", lambda e: e.dma_start(out=dbg["attm"], in_=attm[0][:]), reads=[("attm", 0)])
        P.dma("sp", lambda e: e.dma_start(out=dbg["ekd"], in_=ekd[0][:]), reads=[("ekd", 0)])
    for tt in range(NT if not debug else 2):
        g, tl = divmod(tt, GT)
        if tl == 0 and g + 1 < NGRP:
            load_group(g + 1)
        if tt + 1 < NT:
            precompute(tt + 1)
        chain(tt)
    P.finish()
    P.emit()
    return nc


HPC = FOX_HEADS // NCORE


def build_attn():
    C = Ctx("attn")
    nc, P = C.nc, C.P
    L = LPAD
    qf = C.din("qf", [HPC, 128, L], BF16)
    kf = C.din("kf", [HPC, 128, L], BF16)
    vtm = C.din("vtm", [HPC, L, 128], BF16)
    lft = C.din("lft", [HPC, 128, NT], F32)
    cmd = C.din("cm", [128, 4, 128], F32)
    lmd = C.din("lm", [NT, NT], F32)
    pmd = C.din("pm", [3, 3], F32)
    od = C.dout("o", [HPC, L, 128], F32)
    scr = [nc.dram_tensor("cscr%d" % h, [NT, 128], F32, kind="Internal").ap() for h in range(HPC)]
    ps = C.ps

    cm = C.sb("cm_sb", [128, 4, 128], F32)
    lm = C.sb("lm_sb", [NT, NT], F32)
    pm = C.sb("pm_sb", [3, 3], F32)
    ibf = C.sb("ibf", [128, 128], BF16)
    mneg = C.sb("mneg", [128, 128], BF16)
    ones3 = C.sb("ones3", [3, 128], BF16)
    qh = C.sb("qh", [128, L], BF16)
    kh = C.sb("kh", [128, L], BF16)
    vaug = C.sb("vaug", [128, NT, 129], BF16)
    lfs = C.sb("lfs", [128, NT], F32)
    asb = C.sb("asb", [NT, 128], F32)
    csb = C.sb("csb", [128, NT], F32)
    negc = C.sb("negc", [128, NT], F32)
    ctsb = C.sb("ctsb", [NT, 128], F32)
    cqf = C.sb("cqf", [3, L], F32)
    cr1 = C.sb("cr1", [3, L], F32)
    chi = C.sb("chi", [3, L], BF16)
    cq3 = C.sb("cq3", [3, L], BF16)
    pt = [C.sb("pt%d" % i, [128, 512], BF16) for i in range(2)]
    osb = [C.sb("osb%d" % i, [128, 4, 128], F32) for i in range(2)]
    rec = [C.sb("rec%d" % i, [128, 4], F32) for i in range(2)]

    P.dma("sp", lambda e: e.dma_start(out=cm[:], in_=cmd), writes=["cm"])
    P.dma("sp", lambda e: e.dma_start(out=lm[:], in_=lmd), writes=["lm"])
    P.dma("sp", lambda e: e.dma_start(out=pm[:], in_=pmd), writes=["pm"])
    P.op("dve", lambda e: e.tensor_copy(out=ibf[:], in_=cm[:, 2, :]), reads=["cm"], writes=["ibf"])
    P.op("dve", lambda e: e.tensor_copy(out=mneg[:], in_=cm[:, 3, :]), reads=["cm"], writes=["mneg"])
    P.op("pool", lambda e: e.memset(ones3[:], 1.0), writes=["ones3"])

    cnt = {"sb": 0, "pb": 0, "ob": 0}
    for h in range(HPC):
        P.dma("sp", lambda e, h=h: e.dma_start(out=qh[:], in_=qf[h]), writes=["qh"])
        P.dma("act", lambda e, h=h: e.dma_start(out=kh[:], in_=kf[h]), writes=["kh"])
        P.op("pool", lambda e: e.memset(vaug[:], 1.0), writes=["vaug"])
        P.dma("sp", lambda e, h=h: e.dma_start(out=vaug[:, :, 0:128], in_=vtm[h].rearrange("(t p) d -> p t d", p=128)),
              writes=["vaug"])
        P.dma("act", lambda e, h=h: e.dma_start(out=lfs[:], in_=lft[h]), writes=["lfs"])
        P.op("pe", lambda e: e.matmul(ps[6][0:NT, 0:128], lhsT=lfs[:], rhs=cm[:, 1, :], start=True, stop=True),
             reads=["lfs", "cm"], excl=[("ps", 6)])
        P.op("act", lambda e: e.activation(out=asb[:], in_=ps[6][0:NT, 0:128], func=AF.Copy), excl=[("ps", 6)], writes=["asb"])
        P.op("pe", lambda e: e.matmul(ps[7][:, 0:NT], lhsT=cm[:, 0, :], rhs=lfs[:], start=True, stop=False),
             reads=["lfs", "cm"], excl=[("ps", 7)])
        P.op("pe", lambda e: e.matmul(ps[7][:, 0:NT], lhsT=asb[:], rhs=lm[:], start=False, stop=True),
             reads=["asb", "lm"], excl=[("ps", 7)])
        P.op("dve", lambda e: e.tensor_copy(out=csb[:], in_=ps[7][:, 0:NT]), excl=[("ps", 7)], writes=["csb"])
        P.op("dve", lambda e: e.tensor_scalar(out=negc[:], in0=csb[:], scalar1=-1.0, scalar2=None, op0=ALU.mult),
             reads=["csb"], writes=["negc"])
        P.op("pe", lambda e: e.transpose(ps[6][0:NT, 0:128], csb[:], cm[:, 2, :]), reads=["csb", "cm"], excl=[("ps", 6)])
        P.op("act", lambda e: e.activation(out=ctsb[:], in_=ps[6][0:NT, 0:128], func=AF.Copy), excl=[("ps", 6)], writes=["ctsb"])
        P.dma("sp", lambda e, h=h: e.dma_start(out=scr[h], in_=ctsb[:]), reads=["ctsb"], writes=[("scr", h)])
        for r in range(3):
            P.dma("sp" if r != 1 else "act",
                  lambda e, h=h, r=r: e.dma_start(out=cqf[r:r + 1, :], in_=scr[h].rearrange("(o t) j -> o (t j)", o=1)),
                  reads=[("scr", h)], writes=["cqf"])
        P.op("dve", lambda e: e.tensor_copy(out=chi[:], in_=cqf[:]), reads=["cqf"], writes=["chi"])
        P.op("dve", lambda e: e.tensor_scalar(out=cq3[:], in0=chi[:], scalar1=pm[:, 0:1], scalar2=None, op0=ALU.mult),
             reads=["chi", "pm"], writes=["cq3"])
        P.op("dve", lambda e: e.tensor_tensor(out=cr1[:], in0=cqf[:], in1=chi[:], op=ALU.subtract),
             reads=["cqf", "chi"], writes=["cr1"])
        P.op("dve", lambda e: e.tensor_copy(out=chi[:], in_=cr1[:]), reads=["cr1"], writes=["chi"])
        P.op("dve", lambda e: e.scalar_tensor_tensor(out=cq3[:], in0=chi[:], scalar=pm[:, 1:2], in1=cq3[:],
                                                     op0=ALU.mult, op1=ALU.add), reads=["chi", "pm", "cq3"], writes=["cq3"])
        P.op("dve", lambda e: e.tensor_tensor(out=cr1[:], in0=cr1[:], in1=chi[:], op=ALU.subtract),
             reads=["cr1", "chi"], writes=["cr1"])
        P.op("dve", lambda e: e.tensor_copy(out=chi[:], in_=cr1[:]), reads=["cr1"], writes=["chi"])
        P.op("dve", lambda e: e.scalar_tensor_tensor(out=cq3[:], in0=chi[:], scalar=pm[:, 2:3], in1=cq3[:],
                                                     op0=ALU.mult, op1=ALU.add), reads=["chi", "pm", "cq3"], writes=["cq3"])
        for q0 in range(0, NT, 4):
            nq = min(4, NT - q0)
            oi = cnt["ob"] % 2
            cnt["ob"] += 1
            for kt in range(q0 + nq):
                r0 = max(0, kt - q0)
                n = (nq - r0) * 128
                qa, qb = (q0 + r0) * 128, (q0 + nq) * 128
                sbk = cnt["sb"] % 2
                cnt["sb"] += 1
                pb = cnt["pb"] % 2
                cnt["pb"] += 1
                diag = kt >= q0
                P.op("pe", lambda e, kt=kt, n=n, qa=qa, qb=qb, sbk=sbk: e.matmul(
                    ps[sbk][:, 0:n], lhsT=kh[:, kt * 128:(kt + 1) * 128], rhs=qh[:, qa:qb], start=True, stop=False),
                    reads=["kh", "qh"], excl=[("ps", sbk)])
                P.op("pe", lambda e, n=n, qa=qa, qb=qb, sbk=sbk, diag=diag: e.matmul(
                    ps[sbk][:, 0:n], lhsT=ones3[:], rhs=cq3[:, qa:qb], start=False, stop=(not diag)),
                    reads=["ones3", "cq3"], excl=[("ps", sbk)])
                if diag:
                    P.op("pe", lambda e, sbk=sbk: e.matmul(ps[sbk][:, 0:128], lhsT=ibf[:], rhs=mneg[:], start=False, stop=True),
                         reads=["ibf", "mneg"], excl=[("ps", sbk)])
                P.op("act", lambda e, kt=kt, n=n, sbk=sbk, pb=pb: e.activation(
                    out=pt[pb][:, 0:n], in_=ps[sbk][:, 0:n], func=AF.Exp, bias=negc[:, kt:kt + 1], scale=1.0),
                    reads=["negc"], excl=[("ps", sbk)], writes=[("pt", pb)])
                for r in range(r0, nq):
                    qt = q0 + r
                    P.op("pe", lambda e, kt=kt, r=r, r0=r0, pb=pb, qt=qt: e.matmul(
                        ps[2 + r][:, 0:129], lhsT=pt[pb][:, (r - r0) * 128:(r - r0 + 1) * 128], rhs=vaug[:, kt, :],
                        start=(kt == 0), stop=(kt == qt)),
                        reads=[("pt", pb), "vaug"], excl=[("ps", 2 + r)])
            for r in range(nq):
                P.op("dve", lambda e, r=r, oi=oi: e.reciprocal(out=rec[oi][:, r:r + 1], in_=ps[2 + r][:, 128:129]),
                     excl=[("ps", 2 + r)], writes=[("rec", oi)])
                P.op("dve", lambda e, r=r, oi=oi: e.tensor_scalar(out=osb[oi][:, r, :], in0=ps[2 + r][:, 0:128],
                                                                 scalar1=rec[oi][:, r:r + 1], scalar2=None, op0=ALU.mult),
                     reads=[("rec", oi)], excl=[("ps", 2 + r)], writes=[("osb", oi)])
            P.dma("sp" if oi == 0 else "act",
                  lambda e, h=h, q0=q0, nq=nq, oi=oi: e.dma_start(
                      out=od[h, q0 * 128:(q0 + nq) * 128, :].rearrange("(r i) d -> i r d", i=128), in_=osb[oi][:, 0:nq, :]),
                  reads=[("osb", oi)])
    P.finish()
    P.emit()
    return nc


def attn_consts():
    ii = np.arange(128)
    cm = np.zeros((128, 4, 128), np.float32)
    cm[:, 0, :] = (ii[:, None] <= ii[None, :])
    cm[:, 1, :] = 1.0
    cm[:, 2, :] = np.eye(128)
    cm[:, 3, :] = np.where(ii[:, None] > ii[None, :], NEG, 0.0)
    tt = np.arange(NT)
    lm = (tt[:, None] < tt[None, :]).astype(np.float32)
    return {"cm": cm, "lm": lm, "pm": np.eye(3, dtype=np.float32)}


_PROGS = {}


def _prog(name):
    if name not in _PROGS:
        if name.startswith("pre_"):
            _PROGS[name] = build_pre(name[4:])
        elif name.startswith("post_"):
            _PROGS[name] = build_post(name[5:])
        elif name == "scan":
            _PROGS[name] = build_scan()
        elif name == "attn":
            _PROGS[name] = build_attn()
    return _PROGS[name]


def _run(name, in_maps):
    res = run_bass_kernel_spmd(_prog(name), in_maps, core_ids=list(range(NCORE)))
    return res.results


def _fm(vec, nch):
    return np.ascontiguousarray(np.asarray(vec, np.float32).reshape(nch, 128).T)


def _cat_tokens(results, key, dtype):
    full = np.concatenate([np.asarray(r[key]) for r in results], axis=1)
    out = np.zeros((full.shape[0], LPAD), dtype)
    out[:, :NTOK] = full
    return out


def _scan_consts():
    ii = np.arange(128)
    same = (ii[:, None] // 64) == (ii[None, :] // 64)
    return np.stack([((ii[:, None] <= ii[None, :]) & same), ((ii[:, None] > ii[None, :]) & same)],
                    axis=1).astype(np.float32)


def _post_inputs(h, o_full, g_fm, c):
    lo = c * TC - HALO
    def tok_major(a):
        if lo < 0:
            blk = np.concatenate([np.zeros((HALO, D), np.float32), a[0:TC]], axis=0)
        else:
            blk = a[lo:lo + TP]
        return np.ascontiguousarray(blk.T)
    if lo < 0:
        g = np.concatenate([np.zeros((D, HALO), np.float32), g_fm[:, 0:TC]], axis=1)
    else:
        g = g_fm[:, lo:lo + TP]
    vm = np.ones((128, TP), np.float32)
    if c == 0:
        vm[:, 0:HALO] = 0.0
    return tok_major(h), tok_major(o_full), np.ascontiguousarray(g, dtype=np.float32), vm


def kernel(x, meta, ln_g, ln_b, gla_w_in, gla_w_g2, gla_b_g2, gla_norm_g, gla_w_out,
           kv_w, kv_bf, fox_w_in, fox_w_out, ffn_w_up, ffn_conv_w, ffn_conv_b, ffn_w_down):
    f32 = np.float32
    x = np.asarray(x, f32)
    h = np.concatenate([np.asarray(meta, f32), x[0]], axis=0)
    ln_g, ln_b = np.asarray(ln_g, f32), np.asarray(ln_b, f32)
    kF = vF = lfF = None
    for l in range(DEPTH):
        hts = [np.ascontiguousarray(h[c * TC:(c + 1) * TC].T) for c in range(NCORE)]
        if l < 2:
            w_in = np.asarray(gla_w_in[l], f32)
            wg2 = np.asarray(gla_w_g2[l], f32)
            bg2 = _fm(gla_b_g2[l], 8)
            res = _run("pre_gla", [{"ht": hts[c], "w": w_in, "wg2": wg2, "bg2": bg2} for c in range(NCORE)])
            qF = _cat_tokens(res, "q", NPBF)
            kFg = _cat_tokens(res, "k", NPBF)
            vFg = _cat_tokens(res, "v", NPBF)
            laF = _cat_tokens(res, "la", f32)
            rF = _cat_tokens(res, "r", f32)
            cmc = _scan_consts()
            ims = []
            for u in range(NCORE):
                hd, vh = divmod(u, 2)
                ks = kFg[hd * 256:(hd + 1) * 256]
                ims.append({"qf": np.ascontiguousarray(qF[hd * 256:(hd + 1) * 256]), "kf": np.ascontiguousarray(ks),
                            "ktm": np.ascontiguousarray(ks.T),
                            "vtm": np.ascontiguousarray(vFg[hd * 512 + vh * 256:hd * 512 + (vh + 1) * 256].T),
                            "latm": np.ascontiguousarray(laF[hd * 256:(hd + 1) * 256].T), "cm": cmc})
            res = _run("scan", ims)
            o_full = np.zeros((NTOK, D), f32)
            for u in range(NCORE):
                hd, vh = divmod(u, 2)
                o_full[:, hd * 512 + vh * 256:hd * 512 + (vh + 1) * 256] = np.asarray(res[u]["o"])[:NTOK]
            g_fm = rF
            pname = "post_gla"
            w_out = np.asarray(gla_w_out[l], f32)
        else:
            j = l - 2
            w_in = np.asarray(fox_w_in[j], f32)
            if l == 2:
                wkv = np.asarray(kv_w, f32)
                bfv = np.asarray(kv_bf, f32).reshape(FOX_HEADS, 1)
                res = _run("pre_fox_kv", [{"ht": hts[c], "w": w_in, "wkv": wkv, "bf": bfv} for c in range(NCORE)])
                kF = _cat_tokens(res, "kk", NPBF)
                vF = _cat_tokens(res, "vv", NPBF)
                lfF = _cat_tokens(res, "lf", f32)
            else:
                res = _run("pre_fox", [{"ht": hts[c], "w": w_in} for c in range(NCORE)])
            qF = _cat_tokens(res, "q", NPBF)
            ogF = _cat_tokens(res, "og", f32)
            ac = attn_consts()
            ims = []
            for c in range(NCORE):
                rows = slice(c * HPC * 128, (c + 1) * HPC * 128)
                im = dict(ac)
                im["qf"] = np.ascontiguousarray(qF[rows].reshape(HPC, 128, LPAD))
                im["kf"] = np.ascontiguousarray(kF[rows].reshape(HPC, 128, LPAD))
                im["vtm"] = np.ascontiguousarray(vF[rows].reshape(HPC, 128, LPAD).transpose(0, 2, 1))
                im["lft"] = np.ascontiguousarray(lfF[c * HPC:(c + 1) * HPC].reshape(HPC, NT, 128).transpose(0, 2, 1))
                ims.append(im)
            res = _run("attn", ims)
            o_full = np.zeros((NTOK, D), f32)
            for c in range(NCORE):
                o = np.asarray(res[c]["o"])
                for hh in range(HPC):
                    o_full[:, (c * HPC + hh) * 128:(c * HPC + hh + 1) * 128] = o[hh, :NTOK]
            g_fm = ogF
            pname = "post_fox"
            w_out = np.asarray(fox_w_out[j], f32)
        lnp = np.ascontiguousarray(np.stack([_fm(ln_g[l, 0], 16), _fm(ln_b[l, 0], 16), _fm(ln_g[l, 1], 16),
                                             _fm(ln_b[l, 1], 16)], axis=1))
        cw = np.asarray(ffn_conv_w[l], f32)
        cb = np.asarray(ffn_conv_b[l], f32)
        cwl = np.ascontiguousarray(np.concatenate([cw.reshape(3, 88, 128).transpose(2, 1, 0),
                                                   cb.reshape(88, 128).T[:, :, None]], axis=2))
        wup = np.asarray(ffn_w_up[l], f32)
        wdn = np.asarray(ffn_w_down[l], f32)
        ims = []
        for c in range(NCORE):
            ht_, ot_, gt_, vm_ = _post_inputs(h, o_full, g_fm, c)
            im = {"ht": ht_, "ot": ot_, "gt": gt_, "vmask": vm_, "wo": w_out, "lnp": lnp, "wup": wup, "cw": cwl, "wdn": wdn}
            if pname == "post_gla":
                im["ng"] = _fm(gla_norm_g[l], 4)
            ims.append(im)
        res = _run(pname, ims)
        h = np.concatenate([np.asarray(res[c]["hout"]).T for c in range(NCORE)], axis=0)
    return np.ascontiguousarray(h[N_META:].reshape(1, SEQ, D).astype(f32))
```

```python
import contextlib
import numpy as np
import ml_dtypes
import concourse.bass as bass
import concourse.mybir as mybir
from concourse.bass_utils import run_bass_kernel_spmd

F32 = mybir.dt.float32
BF16 = mybir.dt.bfloat16
AF = mybir.ActivationFunctionType
ALU = mybir.AluOpType
NPBF = ml_dtypes.bfloat16

D = 2048
SEQ = 8192
DEPTH = 4
N_META = 16
NTOK = SEQ + N_META
NCORE = 8
TC = NTOK // NCORE
HALO = 2
TP = TC + HALO
LPAD = 8320
NT = LPAD // 128
ALPHA = (2.0 * DEPTH) ** 0.25
LN_EPS = 1e-5
GLA_HEADS, GLA_HK, GLA_HV, GLA_RANK, GLA_TAU = 4, 256, 512, 16, 16.0
GLA_DK, GLA_DV = 1024, 2048
FOX_HEADS, FOX_HD = 16, 128
D_FF = 5632
NEG = -30000.0

ENGS = ("pe", "act", "dve", "pool", "sp")
NDS = 8


class Prog:
    def __init__(self, nc):
        self.nc = nc
        self.ops = {e: [] for e in ENGS}
        self.cnt = {e: 0 for e in ENGS}
        self.dcnt = {e: 0 for e in ENGS}
        self.lastw = {}
        self.readers = {}
        self.known = {e: {} for e in ENGS}
        self.dknown = {e: {q: set() for q in ENGS} for e in ENGS}
        self.dmax = {e: {q: 0 for q in ENGS} for e in ENGS}

    def _deps(self, eng, reads, writes):
        deps = {}
        ddeps = set()

        def need(ev):
            if ev is None:
                return
            kind, e, idx = ev
            if kind == "d":
                ddeps.add((e, idx))
                return
            if e == eng == "pe":
                return
            if deps.get(e, 0) < idx:
                deps[e] = idx

        for t in reads:
            need(self.lastw.get(t))
        for t in writes:
            need(self.lastw.get(t))
            for r in self.readers.get(t, ()):
                need(r)
        out = []
        kn = self.known[eng]
        for e, idx in deps.items():
            if kn.get(e, 0) >= idx:
                continue
            kn[e] = idx
            out.append(("c", e, idx))
        for (q, idx) in sorted(ddeps):
            if idx in self.dknown[eng][q] or idx <= self.dmax[eng][q] - NDS:
                continue
            self.dknown[eng][q].add(idx)
            if idx > self.dmax[eng][q]:
                self.dmax[eng][q] = idx
            out.append(("d", q, idx))
        return out

    def _commit(self, ev, reads, writes):
        for t in reads:
            self.readers.setdefault(t, []).append(ev)
        for t in writes:
            self.lastw[t] = ev
            self.readers[t] = []

    def op(self, eng, fn, reads=(), writes=(), excl=()):
        if excl:
            reads = list(reads) + list(excl)
            writes = list(writes) + list(excl)
        waits = self._deps(eng, reads, writes)
        self.cnt[eng] += 1
        self.ops[eng].append(("c", fn, waits, 0))
        self._commit(("c", eng, self.cnt[eng]), reads, writes)

    def dma(self, eng, fn, reads=(), writes=()):
        waits = self._deps(eng, reads, writes)
        self.dcnt[eng] += 1
        i = self.dcnt[eng]
        if i > NDS:
            prev = i - NDS
            if not (prev in self.dknown[eng][eng] or prev <= self.dmax[eng][eng] - NDS):
                self.dknown[eng][eng].add(prev)
                self.dmax[eng][eng] = max(self.dmax[eng][eng], prev)
                waits.append(("d", eng, prev))
        self.ops[eng].append(("d", fn, waits, i))
        self._commit(("d", eng, i), reads, writes)

    def finish(self, eng="sp"):
        waits = []
        for e in ENGS:
            for i in range(max(1, self.dcnt[e] - NDS + 1), self.dcnt[e] + 1):
                waits.append(("d", e, i))
            if self.cnt[e]:
                waits.append(("c", e, self.cnt[e]))
        self.ops[eng].append(("w", None, waits, 0))

    def emit(self):
        nc = self.nc
        with contextlib.ExitStack() as st:
            csem = {e: st.enter_context(nc.semaphore("c_" + e)) for e in ENGS}
            dsem = {e: [st.enter_context(nc.semaphore("d_%s%d" % (e, k))) for k in range(NDS)]
                    for e in ("sp", "act", "pool")}
            block = st.enter_context(nc.Block())

            def run(engname):
                def body(eh):
                    for kind, fn, waits, di in self.ops[engname]:
                        for (k, e, idx) in waits:
                            if k == "c":
                                eh.wait_ge(csem[e], idx)
                            else:
                                eh.wait_ge(dsem[e][(idx - 1) % NDS], 16 * ((idx - 1) // NDS + 1))
                        if kind == "c":
                            fn(eh).then_inc(csem[engname], 1)
                        elif kind == "d":
                            fn(eh).then_inc(dsem[engname][(di - 1) % NDS], 16)
                return body

            block.sync(run("sp"))
            block.scalar(run("act"))
            block.vector(run("dve"))
            block.gpsimd(run("pool"))
            block.tensor(run("pe"))


def ntiles(total, maxn=512):
    n = -(-total // maxn)
    base = -(-total // n)
    out = []
    s = 0
    while s < total:
        e = min(total, s + base)
        out.append((s, e))
        s = e
    return out


class Ctx:
    def __init__(self, name):
        self.nc = bass.Bass("TRN2", target_bir_lowering=False)
        self.P = Prog(self.nc)
        self.psbig = self.nc.alloc_psum_tensor("psbig", [128, 4096], F32)
        self.ps = [self.psbig[:, b * 512:(b + 1) * 512] for b in range(8)]
        self.uid = 0

    def sb(self, name, shape, dt):
        return self.nc.alloc_sbuf_tensor(name, list(shape), dt)

    def din(self, name, shape, dt):
        return self.nc.dram_tensor(name, list(shape), dt, kind="ExternalInput").ap()

    def dout(self, name, shape, dt):
        return self.nc.dram_tensor(name, list(shape), dt, kind="ExternalOutput").ap()


def build_pre(kind):
    C = Ctx("pre_" + kind)
    nc, P = C.nc, C.P
    T = TC
    ht = C.din("ht", [D, T], F32)
    if kind == "gla":
        dout_w = 2 * GLA_DK + GLA_DV + GLA_RANK + GLA_DV
        w = C.din("w", [D, dout_w], F32)
        wg2 = C.din("wg2", [GLA_RANK, GLA_DK], F32)
        bg2 = C.din("bg2", [128, GLA_DK // 128], F32)
        segs = [("g", 2 * GLA_DK + GLA_DV, 16, "glow", None),
                ("q", 0, GLA_DK, "scale", BF16), ("k", GLA_DK, GLA_DK, "copy", BF16),
                ("v", 2 * GLA_DK, GLA_DV, "copy", BF16),
                ("r", 2 * GLA_DK + GLA_DV + GLA_RANK, GLA_DV, "copy", F32)]
        qscale = GLA_HK ** -0.5
        outs = {"q": C.dout("q", [GLA_DK, T], BF16), "k": C.dout("k", [GLA_DK, T], BF16),
                "v": C.dout("v", [GLA_DV, T], BF16), "r": C.dout("r", [GLA_DV, T], F32),
                "la": C.dout("la", [GLA_DK, T], F32)}
    else:
        wq = C.din("w", [D, 2 * D], F32)
        segs = [("q", 0, D, "scale", BF16), ("og", D, D, "copy", F32)]
        qscale = FOX_HD ** -0.5
        outs = {"q": C.dout("q", [D, T], BF16), "og": C.dout("og", [D, T], F32)}
        if kind == "fox_kv":
            wkv = C.din("wkv", [D, 2 * D + FOX_HEADS], F32)
            bf = C.din("bf", [FOX_HEADS, 1], F32)
            outs.update({"kk": C.dout("kk", [D, T], BF16), "vv": C.dout("vv", [D, T], BF16),
                         "lf": C.dout("lf", [FOX_HEADS, T], F32)})

    hs = C.sb("hs", [128, 16, T], F32)
    hb = C.sb("hb", [128, 16, T], BF16)
    NW = 4
    wt = [C.sb("wt%d" % i, [128, 16, 128], BF16) for i in range(NW)]
    NS = 3
    stf = [C.sb("stf%d" % i, [128, T], F32) for i in range(NS)]
    stb = [C.sb("stb%d" % i, [128, T], BF16) for i in range(NS)]
    tiles = ntiles(T)

    for c in range(16):
        q = "sp" if c % 2 == 0 else "act"
        P.dma(q, lambda e, c=c: e.dma_start(out=hs[:, c, :], in_=ht[c * 128:(c + 1) * 128, :]),
              writes=[("hs", c)])
        eng = "dve" if c % 2 == 0 else "pool"
        P.op(eng, lambda e, c=c: e.tensor_copy(out=hb[:, c, :], in_=hs[:, c, :]),
             reads=[("hs", c)], writes=[("hb", c)])
    hb_tok = [("hb", c) for c in range(16)]

    state = {"wi": 0, "si": 0, "bank": 0}

    def proj_chunk(wsrc, col0, ncols, epilogue):
        wi = state["wi"] % NW
        state["wi"] += 1
        P.dma("pool", lambda e: e.dma_start(
            out=wt[wi][:, :, 0:ncols],
            in_=wsrc[:, col0:col0 + ncols].rearrange("(kc p) j -> p kc j", p=128)),
            writes=[("wt", wi)])
        for (t0, t1) in tiles:
            b = state["bank"] % 4
            state["bank"] += 1
            for kc in range(16):
                P.op("pe", lambda e, kc=kc, b=b, t0=t0, t1=t1: e.matmul(
                    C.ps[b][0:ncols, 0:t1 - t0], lhsT=wt[wi][:, kc, 0:ncols], rhs=hb[:, kc, t0:t1],
                    start=(kc == 0), stop=(kc == 15)),
                    reads=[("wt", wi), ("hb", kc)], excl=[("ps", b)])
            epilogue(b, t0, t1)

    if kind == "gla":
        gl = C.sb("gl", [GLA_RANK, T], BF16)
        wg2f = C.sb("wg2f", [GLA_RANK, GLA_DK], F32)
        wg2b = C.sb("wg2b", [GLA_RANK, GLA_DK], BF16)
        bg = C.sb("bg", [128, 8], F32)
        nbg = C.sb("nbg", [128, 8], F32)
        P.dma("sp", lambda e: e.dma_start(out=wg2f[:], in_=wg2), writes=["wg2f"])
        P.dma("sp", lambda e: e.dma_start(out=bg[:], in_=bg2), writes=["bg"])
        P.op("dve", lambda e: e.tensor_copy(out=wg2b[:], in_=wg2f[:]), reads=["wg2f"], writes=["wg2b"])
        P.op("dve", lambda e: e.tensor_scalar(out=nbg[:], in0=bg[:], scalar1=-1.0, scalar2=None, op0=ALU.mult),
             reads=["bg"], writes=["nbg"])

        def ep_glow(b, t0, t1):
            P.op("act", lambda e: e.activation(out=gl[:, t0:t1], in_=C.ps[b][0:GLA_RANK, 0:t1 - t0], func=AF.Copy),
                 excl=[("ps", b)], writes=[("gl", t0)])
        proj_chunk(w, 2 * GLA_DK + GLA_DV, GLA_RANK, ep_glow)
        for fc in range(8):
            si = state["si"] % NS
            state["si"] += 1
            for (t0, t1) in tiles:
                b = 4 + (state["bank"] % 2)
                state["bank"] += 1
                P.op("pe", lambda e, fc=fc, b=b, t0=t0, t1=t1: e.matmul(
                    C.ps[b][:, 0:t1 - t0], lhsT=wg2b[:, fc * 128:(fc + 1) * 128], rhs=gl[:, t0:t1],
                    start=True, stop=True), reads=["wg2b", ("gl", t0)], excl=[("ps", b)])
                P.op("act", lambda e, fc=fc, b=b, t0=t0, t1=t1, si=si: e.activation(
                    out=stf[si][:, t0:t1], in_=C.ps[b][:, 0:t1 - t0], func=AF.Exp,
                    bias=nbg[:, fc:fc + 1], scale=-1.0),
                    reads=["nbg"], excl=[("ps", b)], writes=[("stf", si)])
            P.op("act", lambda e, si=si: e.activation(out=stf[si][:], in_=stf[si][:], func=AF.Ln, bias=1.0, scale=1.0),
                 reads=[("stf", si)], writes=[("stf", si)])
            P.op("dve", lambda e, si=si: e.tensor_scalar(out=stf[si][:], in0=stf[si][:], scalar1=-1.0 / GLA_TAU,
                                                        scalar2=None, op0=ALU.mult),
                 reads=[("stf", si)], writes=[("stf", si)])
            P.dma("sp", lambda e, fc=fc, si=si: e.dma_start(out=outs["la"][fc * 128:(fc + 1) * 128, :], in_=stf[si][:]),
                  reads=[("stf", si)])

    def run_seg(wsrc, name, col0, ncols, mode, odt, oname=None):
        oname = oname or name
        for ch in range(ncols // 128):
            si = state["si"] % NS
            state["si"] += 1
            dst = stf[si] if odt == F32 else stb[si]
            tok = ("stf", si) if odt == F32 else ("stb", si)

            def ep(b, t0, t1, dst=dst, tok=tok):
                if mode == "scale":
                    P.op("act", lambda e: e.activation(out=dst[:, t0:t1], in_=C.ps[b][:, 0:t1 - t0], func=AF.Copy,
                                                       scale=float(qscale)),
                         excl=[("ps", b)], writes=[tok])
                else:
                    eng = "dve" if (t0 // 8) % 2 == 0 else "act"
                    if eng == "dve":
                        P.op("dve", lambda e: e.tensor_copy(out=dst[:, t0:t1], in_=C.ps[b][:, 0:t1 - t0]),
                             excl=[("ps", b)], writes=[tok])
                    else:
                        P.op("act", lambda e: e.activation(out=dst[:, t0:t1], in_=C.ps[b][:, 0:t1 - t0], func=AF.Copy),
                             excl=[("ps", b)], writes=[tok])
            proj_chunk(wsrc, col0 + ch * 128, 128, ep)
            P.dma("sp" if ch % 2 == 0 else "act",
                  lambda e, ch=ch, dst=dst: e.dma_start(out=outs[oname][ch * 128:(ch + 1) * 128, :], in_=dst[:]),
                  reads=[tok])

    if kind == "gla":
        for (name, col0, ncols, mode, odt) in segs[1:]:
            run_seg(w, name, col0, ncols, mode, odt)
    else:
        for (name, col0, ncols, mode, odt) in segs:
            run_seg(wq, name, col0, ncols, mode, odt)
        if kind == "fox_kv":
            run_seg(wkv, "kk", 0, D, "copy", BF16)
            run_seg(wkv, "vv", D, D, "copy", BF16)
            bfs = C.sb("bfs", [FOX_HEADS, 1], F32)
            nbf = C.sb("nbf", [FOX_HEADS, 1], F32)
            lfs = C.sb("lfs", [FOX_HEADS, T], F32)
            P.dma("sp", lambda e: e.dma_start(out=bfs[:], in_=bf), writes=["bfs"])
            P.op("dve", lambda e: e.tensor_scalar(out=nbf[:], in0=bfs[:], scalar1=-1.0, scalar2=None, op0=ALU.mult),
                 reads=["bfs"], writes=["nbf"])

            def ep_f(b, t0, t1):
                P.op("act", lambda e: e.activation(out=lfs[:, t0:t1], in_=C.ps[b][0:FOX_HEADS, 0:t1 - t0], func=AF.Exp,
                                                   bias=nbf[:, 0:1], scale=-1.0),
                     reads=["nbf"], excl=[("ps", b)], writes=["lfs"])
            proj_chunk(wkv, 2 * D, FOX_HEADS, ep_f)
            P.op("act", lambda e: e.activation(out=lfs[:], in_=lfs[:], func=AF.Ln, bias=1.0, scale=1.0),
                 reads=["lfs"], writes=["lfs"])
            P.op("dve", lambda e: e.tensor_scalar(out=lfs[:], in0=lfs[:], scalar1=-1.0, scalar2=None, op0=ALU.mult),
                 reads=["lfs"], writes=["lfs"])
            P.dma("sp", lambda e: e.dma_start(out=outs["lf"], in_=lfs[:]), reads=["lfs"])

    P.finish()
    P.emit()
    return nc


def build_post(kind):
    C = Ctx("post_" + kind)
    nc, P = C.nc, C.P
    ht = C.din("ht", [D, TP], F32)
    ot = C.din("ot", [D, TP], F32)
    gt = C.din("gt", [D, TP], F32)
    vm = C.din("vmask", [128, TP], F32)
    wo = C.din("wo", [D, D], F32)
    lnp = C.din("lnp", [128, 4, 16], F32)
    wup = C.din("wup", [D, 2 * D_FF], F32)
    cwd = C.din("cw", [128, 88, 4], F32)
    wdn = C.din("wdn", [D_FF, D], F32)
    hout = C.dout("hout", [D, TC], F32)
    if kind == "gla":
        ngd = C.din("ng", [128, 4], F32)

    TW = 343
    TPX = 3 * TW
    hs = C.sb("hs", [128, 16, TP], F32)
    hb = C.sb("hb", [128, 16, TPX + 3], BF16)
    tA = [C.sb("tA%d" % i, [128, TPX + 3], F32) for i in range(2)]
    tB = [C.sb("tB%d" % i, [128, TPX + 3], F32) for i in range(2)]
    tC = [C.sb("tC%d" % i, [128, TPX + 3], F32) for i in range(2)]
    NW = 4
    wt = [C.sb("wt%d" % i, [128, 16, 128], BF16) for i in range(NW)]
    GP = 11
    NG = 4
    act = C.sb("act", [128, GP, TC], BF16)
    NWD = 3
    wd = [C.sb("wd%d" % i, [128, GP, 128], BF16) for i in range(NWD)]
    st = [C.sb("st%d" % i, [128, TP], F32) for i in range(3)]
    ones = C.sb("ones", [128, 128], F32)
    lns = C.sb("lns", [128, 4, 16], F32)
    cws = C.sb("cws", [128, 88, 4], F32)
    vms = C.sb("vms", [128, TP], F32)
    tiles_p = ntiles(TP)
    tiles_c = ntiles(TC)
    cnt = {"w": 0, "wd": 0, "a": 0, "b": 0, "c": 0, "dq": 0}

    def dq():
        cnt["dq"] += 1
        return "sp" if cnt["dq"] % 2 else "act"

    P.op("pool", lambda e: e.memset(ones[:], 1.0), writes=["ones"])
    P.op("pool", lambda e: e.memset(hb[:], 0.0), writes=[("hb", c) for c in range(16)])
    P.dma("sp", lambda e: e.dma_start(out=lns[:], in_=lnp), writes=["lns"])
    P.dma("sp", lambda e: e.dma_start(out=cws[:], in_=cwd), writes=["cws"])
    P.dma("sp", lambda e: e.dma_start(out=vms[:], in_=vm), writes=["vms"])
    if kind == "gla":
        ngs = C.sb("ngs", [128, 4], F32)
        P.dma("sp", lambda e: e.dma_start(out=ngs[:], in_=ngd), writes=["ngs"])
    for c in range(16):
        P.dma(dq(), lambda e, c=c: e.dma_start(out=hs[:, c, :], in_=ht[c * 128:(c + 1) * 128, :]),
              writes=[("hs", c)])

    def load(buf, tokname, src, c):
        i = cnt[tokname] % 2
        cnt[tokname] += 1
        P.dma(dq(), lambda e: e.dma_start(out=buf[i][:, 0:TP], in_=src[c * 128:(c + 1) * 128, :]),
              writes=[(tokname, i)])
        return i

    def rstd_from(src_banks, tiles, scale, dst, dst_tok):
        for ti, (t0, t1) in enumerate(tiles):
            b = src_banks[ti]
            P.op("dve", lambda e, b=b, t0=t0, t1=t1: e.tensor_scalar(
                out=dst[:, t0:t1], in0=C.ps[b][:, 0:t1 - t0], scalar1=float(scale), scalar2=float(LN_EPS),
                op0=ALU.mult, op1=ALU.add), excl=[("ps", b)], writes=[dst_tok])
        P.op("act", lambda e: e.activation(out=dst[:], in_=dst[:], func=AF.Sqrt), reads=[dst_tok],
             writes=[dst_tok])
        P.op("dve", lambda e: e.reciprocal(out=dst[:], in_=dst[:]), reads=[dst_tok],
             writes=[dst_tok])

    if kind == "gla":
        for hd in range(4):
            for cc in range(4):
                c = hd * 4 + cc
                ia = load(tA, "a", ot, c)
                ic = cnt["c"] % 2
                cnt["c"] += 1
                P.op("act", lambda e, ia=ia, ic=ic: e.activation(out=tC[ic][:, 0:TP], in_=tA[ia][:, 0:TP], func=AF.Square),
                     reads=[("a", ia)], writes=[("c", ic)])
                for ti, (t0, t1) in enumerate(tiles_p):
                    P.op("pe", lambda e, ic=ic, ti=ti, t0=t0, t1=t1, cc=cc: e.matmul(
                        C.ps[ti][:, 0:t1 - t0], lhsT=ones[:], rhs=tC[ic][:, t0:t1], start=(cc == 0), stop=(cc == 3)),
                        reads=["ones", ("c", ic)], excl=[("ps", ti)])
            rstd_from([0, 1, 2], tiles_p, 1.0 / GLA_HV, st[0], ("st", 0))
            for cc in range(4):
                c = hd * 4 + cc
                ia = load(tA, "a", ot, c)
                ib = load(tB, "b", gt, c)
                P.op("act", lambda e, ib=ib: e.activation(out=tB[ib][:, 0:TP], in_=tB[ib][:, 0:TP], func=AF.Silu),
                     reads=[("b", ib)], writes=[("b", ib)])
                P.op("dve", lambda e, ia=ia: e.tensor_tensor(out=tA[ia][:, 0:TP], in0=tA[ia][:, 0:TP], in1=st[0][:], op=ALU.mult),
                     reads=[("a", ia), ("st", 0)], writes=[("a", ia)])
                P.op("act", lambda e, ib=ib, cc=cc: e.activation(
                    out=tB[ib][:, 0:TP], in_=tB[ib][:, 0:TP], func=AF.Copy, scale=ngs[:, cc:cc + 1]),
                    reads=[("b", ib), "ngs"], writes=[("b", ib)])
                P.op("pool", lambda e, ia=ia, ib=ib, c=c: e.tensor_tensor(
                    out=hb[:, c, 0:TP], in0=tA[ia][:, 0:TP], in1=tB[ib][:, 0:TP], op=ALU.mult),
                    reads=[("a", ia), ("b", ib)], writes=[("hb", c)])
    else:
        for c in range(16):
            ia = load(tA, "a", ot, c)
            ib = load(tB, "b", gt, c)
            P.op("act", lambda e, ib=ib: e.activation(out=tB[ib][:, 0:TP], in_=tB[ib][:, 0:TP], func=AF.Sigmoid),
                 reads=[("b", ib)], writes=[("b", ib)])
            P.op("dve" if c % 2 == 0 else "pool",
                 lambda e, ia=ia, ib=ib, c=c: e.tensor_tensor(out=hb[:, c, 0:TP], in0=tA[ia][:, 0:TP], in1=tB[ib][:, 0:TP], op=ALU.mult),
                 reads=[("a", ia), ("b", ib)], writes=[("hb", c)])

    def wload(src_ap):
        wi = cnt["w"] % NW
        cnt["w"] += 1
        P.dma("pool", lambda e: e.dma_start(out=wt[wi][:], in_=src_ap.rearrange("(kc p) j -> p kc j", p=128)),
              writes=[("wt", wi)])
        return wi

    obank = [6, 7]
    ob = {"i": 0}
    wow = {}

    def ensure_wo(fc):
        if fc in wow or fc >= 16:
            return
        wow[fc] = wload(wo[:, fc * 128:(fc + 1) * 128])

    for fc in range(16):
        for k in range(3):
            ensure_wo(fc + k)
        wi = wow[fc]
        for (t0, t1) in tiles_p:
            b = obank[ob["i"] % 2]
            ob["i"] += 1
            for kc in range(16):
                P.op("pe", lambda e, kc=kc, b=b, t0=t0, t1=t1, wi=wi: e.matmul(
                    C.ps[b][:, 0:t1 - t0], lhsT=wt[wi][:, kc, :], rhs=hb[:, kc, t0:t1], start=(kc == 0), stop=(kc == 15)),
                    reads=[("wt", wi), ("hb", kc)], excl=[("ps", b)])
            P.op("dve", lambda e, fc=fc, b=b, t0=t0, t1=t1: e.scalar_tensor_tensor(
                out=hs[:, fc, t0:t1], in0=hs[:, fc, t0:t1], scalar=float(ALPHA), in1=C.ps[b][:, 0:t1 - t0],
                op0=ALU.mult, op1=ALU.add), reads=[("hs", fc)], writes=[("hs", fc)], excl=[("ps", b)])

    def layer_norm(lo, gi, bi, make_hb):
        n = TP - lo
        tl = [(lo + a, lo + b_) for (a, b_) in ntiles(n)]
        for c in range(16):
            ic = cnt["c"] % 2
            cnt["c"] += 1
            P.op("act", lambda e, c=c, ic=ic: e.activation(out=tC[ic][:, lo:TP], in_=hs[:, c, lo:TP], func=AF.Square),
                 reads=[("hs", c)], writes=[("c", ic)])
            for ti, (t0, t1) in enumerate(tl):
                P.op("pe", lambda e, c=c, ti=ti, t0=t0, t1=t1: e.matmul(
                    C.ps[ti][:, 0:t1 - t0], lhsT=ones[:], rhs=hs[:, c, t0:t1], start=(c == 0), stop=(c == 15)),
                    reads=["ones", ("hs", c)], excl=[("ps", ti)])
                P.op("pe", lambda e, c=c, ic=ic, ti=ti, t0=t0, t1=t1: e.matmul(
                    C.ps[3 + ti][:, 0:t1 - t0], lhsT=ones[:], rhs=tC[ic][:, t0:t1], start=(c == 0), stop=(c == 15)),
                    reads=["ones", ("c", ic)], excl=[("ps", 3 + ti)])
        mean, var, tmp = st[0], st[1], st[2]
        for ti, (t0, t1) in enumerate(tl):
            P.op("dve", lambda e, ti=ti, t0=t0, t1=t1: e.tensor_scalar(
                out=mean[:, t0:t1], in0=C.ps[ti][:, 0:t1 - t0], scalar1=1.0 / D, scalar2=None, op0=ALU.mult),
                excl=[("ps", ti)], writes=[("st", 0)])
            P.op("dve", lambda e, ti=ti, t0=t0, t1=t1: e.tensor_scalar(
                out=var[:, t0:t1], in0=C.ps[3 + ti][:, 0:t1 - t0], scalar1=1.0 / D, scalar2=float(LN_EPS),
                op0=ALU.mult, op1=ALU.add), excl=[("ps", 3 + ti)], writes=[("st", 1)])
        P.op("dve", lambda e: e.tensor_tensor(out=tmp[:, lo:TP], in0=mean[:, lo:TP], in1=mean[:, lo:TP], op=ALU.mult),
             reads=[("st", 0)], writes=[("st", 2)])
        P.op("dve", lambda e: e.tensor_tensor(out=var[:, lo:TP], in0=var[:, lo:TP], in1=tmp[:, lo:TP], op=ALU.subtract),
             reads=[("st", 1), ("st", 2)], writes=[("st", 1)])
        P.op("act", lambda e: e.activation(out=var[:, lo:TP], in_=var[:, lo:TP], func=AF.Sqrt), reads=[("st", 1)], writes=[("st", 1)])
        P.op("dve", lambda e: e.reciprocal(out=var[:, lo:TP], in_=var[:, lo:TP]), reads=[("st", 1)], writes=[("st", 1)])
        for c in range(16):
            ia = cnt["a"] % 2
            cnt["a"] += 1
            P.op("dve", lambda e, c=c, ia=ia: e.tensor_tensor(out=tA[ia][:, lo:TP], in0=hs[:, c, lo:TP], in1=mean[:, lo:TP],
                                                             op=ALU.subtract),
                 reads=[("hs", c), ("st", 0)], writes=[("a", ia)])
            P.op("pool", lambda e, ia=ia: e.tensor_tensor(out=tA[ia][:, lo:TP], in0=tA[ia][:, lo:TP], in1=var[:, lo:TP],
                                                          op=ALU.mult),
                 reads=[("a", ia), ("st", 1)], writes=[("a", ia)])
            P.op("dve", lambda e, c=c, ia=ia: e.tensor_scalar(
                out=hs[:, c, lo:TP], in0=tA[ia][:, lo:TP], scalar1=lns[:, gi, c:c + 1], scalar2=lns[:, bi, c:c + 1],
                op0=ALU.mult, op1=ALU.add), reads=[("a", ia), "lns"], writes=[("hs", c)])
            if make_hb:
                P.op("pool", lambda e, c=c: e.tensor_tensor(out=hb[:, c, 0:TP], in0=hs[:, c, :], in1=vms[:], op=ALU.mult),
                     reads=[("hs", c), "vms"], writes=[("hb", c)])

    layer_norm(0, 0, 1, True)

    xu, xg, cu, cg = tA, tB, tC[0], tC[1]
    upw = {}
    dnw = {}

    def ensure_up(j):
        if j in upw or j >= NG * GP:
            return
        upw[j] = (wload(wup[:, j * 128:(j + 1) * 128]), wload(wup[:, D_FF + j * 128:D_FF + (j + 1) * 128]))

    def ensure_down(g, fc):
        if (g, fc) in dnw or fc >= 16:
            return
        wi = cnt["wd"] % NWD
        cnt["wd"] += 1
        P.dma("pool", lambda e: e.dma_start(
            out=wd[wi][:], in_=wdn[g * GP * 128:(g + 1) * GP * 128, fc * 128:(fc + 1) * 128].rearrange(
                "(jj p) m -> p jj m", p=128)), writes=[("wd", wi)])
        dnw[(g, fc)] = wi
    for g in range(NG):
        for jj in range(GP):
            j = g * GP + jj
            ensure_up(j)
            ensure_up(j + 1)
            if jj == GP - 1:
                ensure_down(g, 0)
                ensure_down(g, 1)
            wa, wb = upw[j]
            for (wi, b0, X, xt_, col) in ((wa, 0, (tA[0], tA[1], tC[0]), ("a", 0), j), (wb, 3, (tB[0], tB[1], tC[1]), ("b", 0), 44 + j)):
                for ti in range(3):
                    for kc in range(16):
                        P.op("pe", lambda e, kc=kc, wi=wi, b0=b0, ti=ti: e.matmul(
                            C.ps[b0 + ti][:, 0:TW], lhsT=wt[wi][:, kc, :], rhs=hb[:, kc, ti * TW:(ti + 1) * TW],
                            start=(kc == 0), stop=(kc == 15)),
                            reads=[("wt", wi), ("hb", kc)], excl=[("ps", b0 + ti)])
                src = C.psbig[:, b0 * 512:(b0 + 3) * 512].rearrange("p (b c) -> p b c", c=512)[:, :, 0:TW]
                banks = [("ps", b0), ("ps", b0 + 1), ("ps", b0 + 2)]
                for tap, buf, tok in ((2, X[0], (xt_[0], 0)), (1, X[1], (xt_[0], 1)), (0, X[2], ("c", 0 if b0 == 0 else 1))):
                    P.op("act", lambda e, tap=tap, buf=buf, src=src, col=col: e.activation(
                        out=buf[:, 0:TPX].rearrange("p (b c) -> p b c", c=TW), in_=src, func=AF.Identity,
                        scale=cws[:, col, tap:tap + 1], bias=(cws[:, col, 3:4] if tap == 2 else 0.0)),
                        reads=["cws"], excl=banks, writes=[tok])
                t2, t1_, t0_ = (xt_[0], 0), (xt_[0], 1), ("c", 0 if b0 == 0 else 1)
                P.op("dve", lambda e, X=X: e.tensor_tensor(out=X[0][:, 2:TP], in0=X[0][:, 2:TP], in1=X[1][:, 1:TP - 1], op=ALU.add),
                     reads=[t2, t1_], writes=[t2])
                P.op("dve", lambda e, X=X: e.tensor_tensor(out=X[0][:, 2:TP], in0=X[0][:, 2:TP], in1=X[2][:, 0:TC], op=ALU.add),
                     reads=[t2, t0_], writes=[t2])
            P.op("act", lambda e: e.activation(out=tB[0][:, 2:TP], in_=tB[0][:, 2:TP], func=AF.Silu), reads=[("b", 0)], writes=[("b", 0)])
            P.op("dve", lambda e, jj=jj: e.tensor_tensor(out=act[:, jj, :], in0=tA[0][:, 2:TP], in1=tB[0][:, 2:TP], op=ALU.mult),
                 reads=[("a", 0), ("b", 0)], writes=[("act", jj)])
        for fc in range(16):
            ensure_down(g, fc)
            ensure_down(g, fc + 1)
            ensure_down(g, fc + 2)
            wi = dnw[(g, fc)]
            for (t0, t1) in tiles_c:
                b = obank[ob["i"] % 2]
                ob["i"] += 1
                for jj in range(GP):
                    P.op("pe", lambda e, jj=jj, wi=wi, b=b, t0=t0, t1=t1: e.matmul(
                        C.ps[b][:, 0:t1 - t0], lhsT=wd[wi][:, jj, :], rhs=act[:, jj, t0:t1],
                        start=(jj == 0), stop=(jj == GP - 1)),
                        reads=[("wd", wi), ("act", jj)], excl=[("ps", b)])
                P.op("dve", lambda e, fc=fc, b=b, t0=t0, t1=t1, g=g: e.scalar_tensor_tensor(
                    out=hs[:, fc, 2 + t0:2 + t1], in0=hs[:, fc, 2 + t0:2 + t1], scalar=float(ALPHA if g == 0 else 1.0),
                    in1=C.ps[b][:, 0:t1 - t0], op0=ALU.mult, op1=ALU.add),
                    reads=[("hs", fc)], writes=[("hs", fc)], excl=[("ps", b)])

    layer_norm(2, 2, 3, False)
    for c in range(16):
        P.dma(dq(), lambda e, c=c: e.dma_start(out=hout[c * 128:(c + 1) * 128, :], in_=hs[:, c, 2:TP]),
              reads=[("hs", c)])
    P.finish()
    P.emit()
    return nc


def build_scan(debug=False):
    C = Ctx("scan")
    nc, P = C.nc, C.P
    L = LPAD
    qf = C.din("qf", [256, L], BF16)
    kf = C.din("kf", [256, L], BF16)
    ktm = C.din("ktm", [L, 256], BF16)
    vtm = C.din("vtm", [L, 256], BF16)
    latm = C.din("latm", [L, 256], F32)
    cmd = C.din("cm", [128, 2, 128], F32)
    od = C.dout("o", [L, 256], F32)
    GT = 13
    NGRP = NT // GT
    GL = GT * 128
    qin = [C.sb("qin%d" % i, [128, 2, GL], BF16) for i in range(2)]
    kin = [C.sb("kin%d" % i, [128, 2, GL], BF16) for i in range(2)]
    ktin = [C.sb("ktin%d" % i, [128, GT, 256], BF16) for i in range(2)]
    vtin = [C.sb("vtin%d" % i, [128, GT, 256], BF16) for i in range(2)]
    lain = [C.sb("lain%d" % i, [128, GT, 256], F32) for i in range(2)]
    cm = C.sb("cm_sb", [128, 2, 128], F32)
    eb = [C.sb("eb%d" % i, [128, 2, 128], F32) for i in range(2)]
    enb = [C.sb("enb%d" % i, [128, 2, 128], F32) for i in range(2)]
    qg = [C.sb("qg%d" % i, [128, 2, 128], BF16) for i in range(2)]
    kg = [C.sb("kg%d" % i, [128, 2, 128], BF16) for i in range(2)]
    ekd = [C.sb("ekd%d" % i, [128, 256], F32) for i in range(2)]
    kd = [C.sb("kd%d" % i, [128, 256], BF16) for i in range(2)]
    attm = [C.sb("attm%d" % i, [128, 128], BF16) for i in range(2)]
    osb = [C.sb("osb%d" % i, [64, 2, 256], F32) for i in range(2)]
    S = C.sb("S_sb", [128, 2, 256], F32)
    Sb = C.sb("Sb_sb", [128, 2, 256], BF16)
    ps = C.ps

    P.dma("sp", lambda e: e.dma_start(out=cm[:], in_=cmd), writes=["cm"])
    P.op("dve", lambda e: e.memset(S[:], 0.0), writes=[("S", 0), ("S", 1)])
    P.op("pool", lambda e: e.memset(Sb[:], 0.0), writes=[("Sb", 0), ("Sb", 1)])

    def load_group(g):
        i = g % 2
        c0 = g * GL
        P.dma("sp", lambda e: e.dma_start(out=qin[i][:], in_=qf[:, c0:c0 + GL].rearrange("(kc p) t -> p kc t", p=128)),
              writes=[("qin", i)])
        P.dma("act", lambda e: e.dma_start(out=kin[i][:], in_=kf[:, c0:c0 + GL].rearrange("(kc p) t -> p kc t", p=128)),
              writes=[("kin", i)])
        P.dma("sp", lambda e: e.dma_start(out=ktin[i][:], in_=ktm[c0:c0 + GL, :].rearrange("(t p) k -> p t k", p=128)),
              writes=[("ktin", i)])
        P.dma("act", lambda e: e.dma_start(out=vtin[i][:], in_=vtm[c0:c0 + GL, :].rearrange("(t p) k -> p t k", p=128)),
              writes=[("vtin", i)])
        P.dma("sp", lambda e: e.dma_start(out=lain[i][:], in_=latm[c0:c0 + GL, :].rearrange("(t p) k -> p t k", p=128)),
              writes=[("lain", i)])

    def precompute(tt):
        g, tl = divmod(tt, GT)
        gi = g % 2
        pr = tt % 2
        bb = 0 if pr == 0 else 6
        b2 = 1 if pr == 0 else 7
        cs = slice(tl * 128, (tl + 1) * 128)
        for kc in range(2):
            P.op("pe", lambda e, kc=kc: e.matmul(ps[bb][:, kc * 128:(kc + 1) * 128], lhsT=lain[gi][:, tl, kc * 128:(kc + 1) * 128],
                                                rhs=cm[:, 0, :], start=True, stop=True),
                 reads=[("lain", gi), "cm"], excl=[("ps", bb)])
        P.op("act", lambda e: e.activation(out=eb[pr][:].rearrange("p a b -> p (a b)"), in_=ps[bb][:, 0:256], func=AF.Exp),
             excl=[("ps", bb)], writes=[("eb", pr)])
        P.op("act", lambda e: e.activation(out=enb[pr][:].rearrange("p a b -> p (a b)"), in_=ps[bb][:, 0:256], func=AF.Exp, scale=-1.0),
             excl=[("ps", bb)], writes=[("enb", pr)])
        P.op("dve", lambda e: e.tensor_tensor(out=qg[pr][:], in0=qin[gi][:, :, cs], in1=eb[pr][:], op=ALU.mult),
             reads=[("qin", gi), ("eb", pr)], writes=[("qg", pr)])
        P.op("pool", lambda e: e.tensor_tensor(out=kg[pr][:], in0=kin[gi][:, :, cs], in1=enb[pr][:], op=ALU.mult),
             reads=[("kin", gi), ("enb", pr)], writes=[("kg", pr)])
        P.op("pe", lambda e: e.matmul(ps[b2][:, 0:256], lhsT=cm[:, 1, :], rhs=lain[gi][:, tl, :], start=True, stop=True),
             reads=[("lain", gi), "cm"], excl=[("ps", b2)])
        P.op("act", lambda e: e.activation(out=ekd[pr][:], in_=ps[b2][:, 0:256], func=AF.Exp),
             excl=[("ps", b2)], writes=[("ekd", pr)])
        P.op("pool", lambda e: e.tensor_tensor(out=kd[pr][:], in0=ktin[gi][:, tl, :], in1=ekd[pr][:], op=ALU.mult),
             reads=[("ktin", gi), ("ekd", pr)], writes=[("kd", pr)])
        for kc in range(2):
            P.op("pe", lambda e, kc=kc: e.matmul(ps[b2][:, 256:384], lhsT=kg[pr][:, kc, :], rhs=qg[pr][:, kc, :],
                                                start=(kc == 0), stop=(kc == 1)),
                 reads=[("kg", pr), ("qg", pr)], excl=[("ps", b2)])
        P.op("dve", lambda e: e.tensor_tensor(out=attm[pr][:], in0=ps[b2][:, 256:384], in1=cm[:, 0, :], op=ALU.mult),
             reads=["cm"], excl=[("ps", b2)], writes=[("attm", pr)])

    def chain(tt):
        g, tl = divmod(tt, GT)
        gi = g % 2
        pr = tt % 2
        for s in range(2):
            rs = slice(s * 64, (s + 1) * 64)
            ob = 3 + s
            P.op("pe", lambda e, rs=rs, ob=ob: e.matmul(ps[ob][0:64, 0:256], lhsT=attm[pr][rs, rs], rhs=vtin[gi][rs, tl, :],
                                          start=True, stop=False),
                 reads=[("attm", pr), ("vtin", gi)], excl=[("ps", ob)])
            for kc in range(2):
                P.op("pe", lambda e, kc=kc, rs=rs, ob=ob: e.matmul(ps[ob][0:64, 0:256], lhsT=qg[pr][:, kc, rs], rhs=Sb[:, kc, :],
                                                    start=False, stop=(kc == 1)),
                     reads=[("qg", pr), ("Sb", kc)], excl=[("ps", ob)])
            P.op("act", lambda e, s=s, ob=ob: e.activation(out=osb[pr][:, s, :], in_=ps[ob][0:64, 0:256], func=AF.Copy),
                 excl=[("ps", ob)], writes=[("osb", pr)])
            for kc in range(2):
                P.op("pe", lambda e, kc=kc, rs=rs: e.matmul(ps[5][:, kc * 256:(kc + 1) * 256], lhsT=kd[pr][rs, kc * 128:(kc + 1) * 128],
                                                    rhs=vtin[gi][rs, tl, :], start=True, stop=True),
                     reads=[("kd", pr), ("vtin", gi)], excl=[("ps", 5)])
            for kc in range(2):
                P.op("dve", lambda e, kc=kc, s=s: e.scalar_tensor_tensor(
                    out=S[:, kc, :], in0=S[:, kc, :], scalar=eb[pr][:, kc, s * 64 + 63:s * 64 + 64],
                    in1=ps[5][:, kc * 256:(kc + 1) * 256], op0=ALU.mult, op1=ALU.add),
                    reads=[("S", kc), ("eb", pr)], writes=[("S", kc)], excl=[("ps", 5)])
                P.op("pool", lambda e, kc=kc: e.tensor_copy(out=Sb[:, kc, :], in_=S[:, kc, :]),
                     reads=[("S", kc)], writes=[("Sb", kc)])
        P.dma("sp" if tt % 2 == 0 else "act",
              lambda e: e.dma_start(out=od[tt * 128:(tt + 1) * 128, :].rearrange("(s i) v -> i s v", s=2), in_=osb[pr][:]),
              reads=[("osb", pr)])

    load_group(0)
    precompute(0)
    if debug:
        dbg = {n: C.dout("dbg_" + n, shp, dt) for n, shp, dt in (("eb", [128, 256], F32), ("qg", [128, 256], BF16),
               ("kg", [128, 256], BF16), ("kd", [128, 256], BF16), ("attm", [128, 128], BF16), ("ekd", [128, 256], F32))}
        P.dma("sp", lambda e: e.dma_start(out=dbg["eb"], in_=eb[0][:].rearrange("p a b -> p (a b)")), reads=[("eb", 0)])
        P.dma("sp", lambda e: e.dma_start(out=dbg["qg"], in_=qg[0][:].rearrange("p a b -> p (a b)")), reads=[("qg", 0)])
        P.dma("sp", lambda e: e.dma_start(out=dbg["kg"], in_=kg[0][:].rearrange("p a b -> p (a b)")), reads=[("kg", 0)])
        P.dma("sp", lambda e: e.dma_start(out=dbg["kd"], in_=kd[0][:]), reads=[("kd", 0)])
        P.dma("sp", lambda e: e.dma_start(out=dbg["attm"], in_=attm[0][:]), reads=[("attm", 0)])
        P.dma("sp", lambda e: e.dma_start(out=dbg["ekd"], in_=ekd[0][:]), reads=[("ekd", 0)])
    for tt in range(NT if not debug else 2):
        g, tl = divmod(tt, GT)
        if tl == 0 and g + 1 < NGRP:
            load_group(g + 1)
        if tt + 1 < NT:
            precompute(tt + 1)
        chain(tt)
    P.finish()
    P.emit()
    return nc


HPC = FOX_HEADS // NCORE


def build_attn():
    C = Ctx("attn")
    nc, P = C.nc, C.P
    L = LPAD
    qf = C.din("qf", [HPC, 128, L], BF16)
    kf = C.din("kf", [HPC, 128, L], BF16)
    vtm = C.din("vtm", [HPC, L, 128], BF16)
    lft = C.din("lft", [HPC, 128, NT], F32)
    cmd = C.din("cm", [128, 4, 128], F32)
    lmd = C.din("lm", [NT, NT], F32)
    pmd = C.din("pm", [3, 3], F32)
    od = C.dout("o", [HPC, L, 128], F32)
    scr = [nc.dram_tensor("cscr%d" % h, [NT, 128], F32, kind="Internal").ap() for h in range(HPC)]
    ps = C.ps

    cm = C.sb("cm_sb", [128, 4, 128], F32)
    lm = C.sb("lm_sb", [NT, NT], F32)
    pm = C.sb("pm_sb", [3, 3], F32)
    ibf = C.sb("ibf", [128, 128], BF16)
    mneg = C.sb("mneg", [128, 128], BF16)
    ones3 = C.sb("ones3", [3, 128], BF16)
    qh = C.sb("qh", [128, L], BF16)
    kh = C.sb("kh", [128, L], BF16)
    vaug = C.sb("vaug", [128, NT, 129], BF16)
    lfs = C.sb("lfs", [128, NT], F32)
    asb = C.sb("asb", [NT, 128], F32)
    csb = C.sb("csb", [128, NT], F32)
    negc = C.sb("negc", [128, NT], F32)
    ctsb = C.sb("ctsb", [NT, 128], F32)
    cqf = C.sb("cqf", [3, L], F32)
    cr1 = C.sb("cr1", [3, L], F32)
    chi = C.sb("chi", [3, L], BF16)
    cq3 = C.sb("cq3", [3, L], BF16)
    pt = [C.sb("pt%d" % i, [128, 512], BF16) for i in range(2)]
    osb = [C.sb("osb%d" % i, [128, 4, 128], F32) for i in range(2)]
    rec = [C.sb("rec%d" % i, [128, 4], F32) for i in range(2)]

    P.dma("sp", lambda e: e.dma_start(out=cm[:], in_=cmd), writes=["cm"])
    P.dma("sp", lambda e: e.dma_start(out=lm[:], in_=lmd), writes=["lm"])
    P.dma("sp", lambda e: e.dma_start(out=pm[:], in_=pmd), writes=["pm"])
    P.op("dve", lambda e: e.tensor_copy(out=ibf[:], in_=cm[:, 2, :]), reads=["cm"], writes=["ibf"])
    P.op("dve", lambda e: e.tensor_copy(out=mneg[:], in_=cm[:, 3, :]), reads=["cm"], writes=["mneg"])
    P.op("pool", lambda e: e.memset(ones3[:], 1.0), writes=["ones3"])

    cnt = {"sb": 0, "pb": 0, "ob": 0}
    for h in range(HPC):
        P.dma("sp", lambda e, h=h: e.dma_start(out=qh[:], in_=qf[h]), writes=["qh"])
        P.dma("act", lambda e, h=h: e.dma_start(out=kh[:], in_=kf[h]), writes=["kh"])
        P.op("pool", lambda e: e.memset(vaug[:], 1.0), writes=["vaug"])
        P.dma("sp", lambda e, h=h: e.dma_start(out=vaug[:, :, 0:128], in_=vtm[h].rearrange("(t p) d -> p t d", p=128)),
              writes=["vaug"])
        P.dma("act", lambda e, h=h: e.dma_start(out=lfs[:], in_=lft[h]), writes=["lfs"])
        P.op("pe", lambda e: e.matmul(ps[6][0:NT, 0:128], lhsT=lfs[:], rhs=cm[:, 1, :], start=True, stop=True),
             reads=["lfs", "cm"], excl=[("ps", 6)])
        P.op("act", lambda e: e.activation(out=asb[:], in_=ps[6][0:NT, 0:128], func=AF.Copy), excl=[("ps", 6)], writes=["asb"])
        P.op("pe", lambda e: e.matmul(ps[7][:, 0:NT], lhsT=cm[:, 0, :], rhs=lfs[:], start=True, stop=False),
             reads=["lfs", "cm"], excl=[("ps", 7)])
        P.op("pe", lambda e: e.matmul(ps[7][:, 0:NT], lhsT=asb[:], rhs=lm[:], start=False, stop=True),
             reads=["asb", "lm"], excl=[("ps", 7)])
        P.op("dve", lambda e: e.tensor_copy(out=csb[:], in_=ps[7][:, 0:NT]), excl=[("ps", 7)], writes=["csb"])
        P.op("dve", lambda e: e.tensor_scalar(out=negc[:], in0=csb[:], scalar1=-1.0, scalar2=None, op0=ALU.mult),
             reads=["csb"], writes=["negc"])
        P.op("pe", lambda e: e.transpose(ps[6][0:NT, 0:128], csb[:], cm[:, 2, :]), reads=["csb", "cm"], excl=[("ps", 6)])
        P.op("act", lambda e: e.activation(out=ctsb[:], in_=ps[6][0:NT, 0:128], func=AF.Copy), excl=[("ps", 6)], writes=["ctsb"])
        P.dma("sp", lambda e, h=h: e.dma_start(out=scr[h], in_=ctsb[:]), reads=["ctsb"], writes=[("scr", h)])
        for r in range(3):
            P.dma("sp" if r != 1 else "act",
                  lambda e, h=h, r=r: e.dma_start(out=cqf[r:r + 1, :], in_=scr[h].rearrange("(o t) j -> o (t j)", o=1)),
                  reads=[("scr", h)], writes=["cqf"])
        P.op("dve", lambda e: e.tensor_copy(out=chi[:], in_=cqf[:]), reads=["cqf"], writes=["chi"])
        P.op("dve", lambda e: e.tensor_scalar(out=cq3[:], in0=chi[:], scalar1=pm[:, 0:1], scalar2=None, op0=ALU.mult),
             reads=["chi", "pm"], writes=["cq3"])
        P.op("dve", lambda e: e.tensor_tensor(out=cr1[:], in0=cqf[:], in1=chi[:], op=ALU.subtract),
             reads=["cqf", "chi"], writes=["cr1"])
        P.op("dve", lambda e: e.tensor_copy(out=chi[:], in_=cr1[:]), reads=["cr1"], writes=["chi"])
        P.op("dve", lambda e: e.scalar_tensor_tensor(out=cq3[:], in0=chi[:], scalar=pm[:, 1:2], in1=cq3[:],
                                                     op0=ALU.mult, op1=ALU.add), reads=["chi", "pm", "cq3"], writes=["cq3"])
        P.op("dve", lambda e: e.tensor_tensor(out=cr1[:], in0=cr1[:], in1=chi[:], op=ALU.subtract),
             reads=["cr1", "chi"], writes=["cr1"])
        P.op("dve", lambda e: e.tensor_copy(out=chi[:], in_=cr1[:]), reads=["cr1"], writes=["chi"])
        P.op("dve", lambda e: e.scalar_tensor_tensor(out=cq3[:], in0=chi[:], scalar=pm[:, 2:3], in1=cq3[:],
                                                     op0=ALU.mult, op1=ALU.add), reads=["chi", "pm", "cq3"], writes=["cq3"])
        for q0 in range(0, NT, 4):
            nq = min(4, NT - q0)
            oi = cnt["ob"] % 2
            cnt["ob"] += 1
            for kt in range(q0 + nq):
                r0 = max(0, kt - q0)
                n = (nq - r0) * 128
                qa, qb = (q0 + r0) * 128, (q0 + nq) * 128
                sbk = cnt["sb"] % 2
                cnt["sb"] += 1
                pb = cnt["pb"] % 2
                cnt["pb"] += 1
                diag = kt >= q0
                P.op("pe", lambda e, kt=kt, n=n, qa=qa, qb=qb, sbk=sbk: e.matmul(
                    ps[sbk][:, 0:n], lhsT=kh[:, kt * 128:(kt + 1) * 128], rhs=qh[:, qa:qb], start=True, stop=False),
                    reads=["kh", "qh"], excl=[("ps", sbk)])
                P.op("pe", lambda e, n=n, qa=qa, qb=qb, sbk=sbk, diag=diag: e.matmul(
                    ps[sbk][:, 0:n], lhsT=ones3[:], rhs=cq3[:, qa:qb], start=False, stop=(not diag)),
                    reads=["ones3", "cq3"], excl=[("ps", sbk)])
                if diag:
                    P.op("pe", lambda e, sbk=sbk: e.matmul(ps[sbk][:, 0:128], lhsT=ibf[:], rhs=mneg[:], start=False, stop=True),
                         reads=["ibf", "mneg"], excl=[("ps", sbk)])
                P.op("act", lambda e, kt=kt, n=n, sbk=sbk, pb=pb: e.activation(
                    out=pt[pb][:, 0:n], in_=ps[sbk][:, 0:n], func=AF.Exp, bias=negc[:, kt:kt + 1], scale=1.0),
                    reads=["negc"], excl=[("ps", sbk)], writes=[("pt", pb)])
                for r in range(r0, nq):
                    qt = q0 + r
                    P.op("pe", lambda e, kt=kt, r=r, r0=r0, pb=pb, qt=qt: e.matmul(
                        ps[2 + r][:, 0:129], lhsT=pt[pb][:, (r - r0) * 128:(r - r0 + 1) * 128], rhs=vaug[:, kt, :],
                        start=(kt == 0), stop=(kt == qt)),
                        reads=[("pt", pb), "vaug"], excl=[("ps", 2 + r)])
            for r in range(nq):
                P.op("dve", lambda e, r=r, oi=oi: e.reciprocal(out=rec[oi][:, r:r + 1], in_=ps[2 + r][:, 128:129]),
                     excl=[("ps", 2 + r)], writes=[("rec", oi)])
                P.op("dve", lambda e, r=r, oi=oi: e.tensor_scalar(out=osb[oi][:, r, :], in0=ps[2 + r][:, 0:128],
                                                                 scalar1=rec[oi][:, r:r + 1], scalar2=None, op0=ALU.mult),
                     reads=[("rec", oi)], excl=[("ps", 2 + r)], writes=[("osb", oi)])
            P.dma("sp" if oi == 0 else "act",
                  lambda e, h=h, q0=q0, nq=nq, oi=oi: e.dma_start(
                      out=od[h, q0 * 128:(q0 + nq) * 128, :].rearrange("(r i) d -> i r d", i=128), in_=osb[oi][:, 0:nq, :]),
                  reads=[("osb", oi)])
    P.finish()
    P.emit()
    return nc


def attn_consts():
    ii = np.arange(128)
    cm = np.zeros((128, 4, 128), np.float32)
    cm[:, 0, :] = (ii[:, None] <= ii[None, :])
    cm[:, 1, :] = 1.0
    cm[:, 2, :] = np.eye(128)
    cm[:, 3, :] = np.where(ii[:, None] > ii[None, :], NEG, 0.0)
    tt = np.arange(NT)
    lm = (tt[:, None] < tt[None, :]).astype(np.float32)
    return {"cm": cm, "lm": lm, "pm": np.eye(3, dtype=np.float32)}


_PROGS = {}


def _prog(name):
    if name not in _PROGS:
        if name.startswith("pre_"):
            _PROGS[name] = build_pre(name[4:])
        elif name.startswith("post_"):
            _PROGS[name] = build_post(name[5:])
        elif name == "scan":
            _PROGS[name] = build_scan()
        elif name == "attn":
            _PROGS[name] = build_attn()
    return _PROGS[name]


def _run(name, in_maps):
    res = run_bass_kernel_spmd(_prog(name), in_maps, core_ids=list(range(NCORE)))
    return res.results


def _fm(vec, nch):
    return np.ascontiguousarray(np.asarray(vec, np.float32).reshape(nch, 128).T)


def _cat_tokens(results, key, dtype):
    full = np.concatenate([np.asarray(r[key]) for r in results], axis=1)
    out = np.zeros((full.shape[0], LPAD), dtype)
    out[:, :NTOK] = full
    return out


def _scan_consts():
    ii = np.arange(128)
    same = (ii[:, None] // 64) == (ii[None, :] // 64)
    return np.stack([((ii[:, None] <= ii[None, :]) & same), ((ii[:, None] > ii[None, :]) & same)],
                    axis=1).astype(np.float32)


def _post_inputs(h, o_full, g_fm, c):
    lo = c * TC - HALO
    def tok_major(a):
        if lo < 0:
            blk = np.concatenate([np.zeros((HALO, D), np.float32), a[0:TC]], axis=0)
        else:
            blk = a[lo:lo + TP]
        return np.ascontiguousarray(blk.T)
    if lo < 0:
        g = np.concatenate([np.zeros((D, HALO), np.float32), g_fm[:, 0:TC]], axis=1)
    else:
        g = g_fm[:, lo:lo + TP]
    vm = np.ones((128, TP), np.float32)
    if c == 0:
        vm[:, 0:HALO] = 0.0
    return tok_major(h), tok_major(o_full), np.ascontiguousarray(g, dtype=np.float32), vm


def kernel(x, meta, ln_g, ln_b, gla_w_in, gla_w_g2, gla_b_g2, gla_norm_g, gla_w_out,
           kv_w, kv_bf, fox_w_in, fox_w_out, ffn_w_up, ffn_conv_w, ffn_conv_b, ffn_w_down):
    f32 = np.float32
    x = np.asarray(x, f32)
    h = np.concatenate([np.asarray(meta, f32), x[0]], axis=0)
    ln_g, ln_b = np.asarray(ln_g, f32), np.asarray(ln_b, f32)
    kF = vF = lfF = None
    for l in range(DEPTH):
        hts = [np.ascontiguousarray(h[c * TC:(c + 1) * TC].T) for c in range(NCORE)]
        if l < 2:
            w_in = np.asarray(gla_w_in[l], f32)
            wg2 = np.asarray(gla_w_g2[l], f32)
            bg2 = _fm(gla_b_g2[l], 8)
            res = _run("pre_gla", [{"ht": hts[c], "w": w_in, "wg2": wg2, "bg2": bg2} for c in range(NCORE)])
            qF = _cat_tokens(res, "q", NPBF)
            kFg = _cat_tokens(res, "k", NPBF)
            vFg = _cat_tokens(res, "v", NPBF)
            laF = _cat_tokens(res, "la", f32)
            rF = _cat_tokens(res, "r", f32)
            cmc = _scan_consts()
            ims = []
            for u in range(NCORE):
                hd, vh = divmod(u, 2)
                ks = kFg[hd * 256:(hd + 1) * 256]
                ims.append({"qf": np.ascontiguousarray(qF[hd * 256:(hd + 1) * 256]), "kf": np.ascontiguousarray(ks),
                            "ktm": np.ascontiguousarray(ks.T),
                            "vtm": np.ascontiguousarray(vFg[hd * 512 + vh * 256:hd * 512 + (vh + 1) * 256].T),
                            "latm": np.ascontiguousarray(laF[hd * 256:(hd + 1) * 256].T), "cm": cmc})
            res = _run("scan", ims)
            o_full = np.zeros((NTOK, D), f32)
            for u in range(NCORE):
                hd, vh = divmod(u, 2)
                o_full[:, hd * 512 + vh * 256:hd * 512 + (vh + 1) * 256] = np.asarray(res[u]["o"])[:NTOK]
            g_fm = rF
            pname = "post_gla"
            w_out = np.asarray(gla_w_out[l], f32)
        else:
            j = l - 2
            w_in = np.asarray(fox_w_in[j], f32)
            if l == 2:
                wkv = np.asarray(kv_w, f32)
                bfv = np.asarray(kv_bf, f32).reshape(FOX_HEADS, 1)
                res = _run("pre_fox_kv", [{"ht": hts[c], "w": w_in, "wkv": wkv, "bf": bfv} for c in range(NCORE)])
                kF = _cat_tokens(res, "kk", NPBF)
                vF = _cat_tokens(res, "vv", NPBF)
                lfF = _cat_tokens(res, "lf", f32)
            else:
                res = _run("pre_fox", [{"ht": hts[c], "w": w_in} for c in range(NCORE)])
            qF = _cat_tokens(res, "q", NPBF)
            ogF = _cat_tokens(res, "og", f32)
            ac = attn_consts()
            ims = []
            for c in range(NCORE):
                rows = slice(c * HPC * 128, (c + 1) * HPC * 128)
                im = dict(ac)
                im["qf"] = np.ascontiguousarray(qF[rows].reshape(HPC, 128, LPAD))
                im["kf"] = np.ascontiguousarray(kF[rows].reshape(HPC, 128, LPAD))
                im["vtm"] = np.ascontiguousarray(vF[rows].reshape(HPC, 128, LPAD).transpose(0, 2, 1))
                im["lft"] = np.ascontiguousarray(lfF[c * HPC:(c + 1) * HPC].reshape(HPC, NT, 128).transpose(0, 2, 1))
                ims.append(im)
            res = _run("attn", ims)
            o_full = np.zeros((NTOK, D), f32)
            for c in range(NCORE):
                o = np.asarray(res[c]["o"])
                for hh in range(HPC):
                    o_full[:, (c * HPC + hh) * 128:(c * HPC + hh + 1) * 128] = o[hh, :NTOK]
            g_fm = ogF
            pname = "post_fox"
            w_out = np.asarray(fox_w_out[j], f32)
        lnp = np.ascontiguousarray(np.stack([_fm(ln_g[l, 0], 16), _fm(ln_b[l, 0], 16), _fm(ln_g[l, 1], 16),
                                             _fm(ln_b[l, 1], 16)], axis=1))
        cw = np.asarray(ffn_conv_w[l], f32)
        cb = np.asarray(ffn_conv_b[l], f32)
        cwl = np.ascontiguousarray(np.concatenate([cw.reshape(3, 88, 128).transpose(2, 1, 0),
                                                   cb.reshape(88, 128).T[:, :, None]], axis=2))
        wup = np.asarray(ffn_w_up[l], f32)
        wdn = np.asarray(ffn_w_down[l], f32)
        ims = []
        for c in range(NCORE):
            ht_, ot_, gt_, vm_ = _post_inputs(h, o_full, g_fm, c)
            im = {"ht": ht_, "ot": ot_, "gt": gt_, "vmask": vm_, "wo": w_out, "lnp": lnp, "wup": wup, "cw": cwl, "wdn": wdn}
            if pname == "post_gla":
                im["ng"] = _fm(gla_norm_g[l], 4)
            ims.append(im)
        res = _run(pname, ims)
        h = np.concatenate([np.asarray(res[c]["hout"]).T for c in range(NCORE)], axis=0)
    return np.ascontiguousarray(h[N_META:].reshape(1, SEQ, D).astype(f32))
```

```python
import contextlib
import numpy as np
import ml_dtypes
import concourse.bass as bass
import concourse.mybir as mybir
from concourse.bass_utils import run_bass_kernel_spmd

F32 = mybir.dt.float32
BF16 = mybir.dt.bfloat16
AF = mybir.ActivationFunctionType
ALU = mybir.AluOpType
NPBF = ml_dtypes.bfloat16

D = 2048
SEQ = 8192
DEPTH = 4
N_META = 16
NTOK = SEQ + N_META
NCORE = 8
TC = NTOK // NCORE
HALO = 2
TP = TC + HALO
LPAD = 8320
NT = LPAD // 128
ALPHA = (2.0 * DEPTH) ** 0.25
LN_EPS = 1e-5
GLA_HEADS, GLA_HK, GLA_HV, GLA_RANK, GLA_TAU = 4, 256, 512, 16, 16.0
GLA_DK, GLA_DV = 1024, 2048
FOX_HEADS, FOX_HD = 16, 128
D_FF = 5632
NEG = -30000.0

ENGS = ("pe", "act", "dve", "pool", "sp")
NDS = 8


class Prog:
    def __init__(self, nc):
        self.nc = nc
        self.ops = {e: [] for e in ENGS}
        self.cnt = {e: 0 for e in ENGS}
        self.dcnt = {e: 0 for e in ENGS}
        self.lastw = {}
        self.readers = {}
        self.known = {e: {} for e in ENGS}
        self.dknown = {e: {q: set() for q in ENGS} for e in ENGS}
        self.dmax = {e: {q: 0 for q in ENGS} for e in ENGS}

    def _deps(self, eng, reads, writes):
        deps = {}
        ddeps = set()

        def need(ev):
            if ev is None:
                return
            kind, e, idx = ev
            if kind == "d":
                ddeps.add((e, idx))
                return
            if e == eng == "pe":
                return
            if deps.get(e, 0) < idx:
                deps[e] = idx

        for t in reads:
            need(self.lastw.get(t))
        for t in writes:
            need(self.lastw.get(t))
            for r in self.readers.get(t, ()):
                need(r)
        out = []
        kn = self.known[eng]
        for e, idx in deps.items():
            if kn.get(e, 0) >= idx:
                continue
            kn[e] = idx
            out.append(("c", e, idx))
        for (q, idx) in sorted(ddeps):
            if idx in self.dknown[eng][q] or idx <= self.dmax[eng][q] - NDS:
                continue
            self.dknown[eng][q].add(idx)
            if idx > self.dmax[eng][q]:
                self.dmax[eng][q] = idx
            out.append(("d", q, idx))
        return out

    def _commit(self, ev, reads, writes):
        for t in reads:
            self.readers.setdefault(t, []).append(ev)
        for t in writes:
            self.lastw[t] = ev
            self.readers[t] = []

    def op(self, eng, fn, reads=(), writes=(), excl=()):
        if excl:
            reads = list(reads) + list(excl)
            writes = list(writes) + list(excl)
        waits = self._deps(eng, reads, writes)
        self.cnt[eng] += 1
        self.ops[eng].append(("c", fn, waits, 0))
        self._commit(("c", eng, self.cnt[eng]), reads, writes)

    def dma(self, eng, fn, reads=(), writes=()):
        waits = self._deps(eng, reads, writes)
        self.dcnt[eng] += 1
        i = self.dcnt[eng]
        if i > NDS:
            prev = i - NDS
            if not (prev in self.dknown[eng][eng] or prev <= self.dmax[eng][eng] - NDS):
                self.dknown[eng][eng].add(prev)
                self.dmax[eng][eng] = max(self.dmax[eng][eng], prev)
                waits.append(("d", eng, prev))
        self.ops[eng].append(("d", fn, waits, i))
        self._commit(("d", eng, i), reads, writes)

    def finish(self, eng="sp"):
        waits = []
        for e in ENGS:
            for i in range(max(1, self.dcnt[e] - NDS + 1), self.dcnt[e] + 1):
                waits.append(("d", e, i))
            if self.cnt[e]:
                waits.append(("c", e, self.cnt[e]))
        self.ops[eng].append(("w", None, waits, 0))

    def emit(self):
        nc = self.nc
        with contextlib.ExitStack() as st:
            csem = {e: st.enter_context(nc.semaphore("c_" + e)) for e in ENGS}
            dsem = {e: [st.enter_context(nc.semaphore("d_%s%d" % (e, k))) for k in range(NDS)]
                    for e in ("sp", "act", "pool")}
            block = st.enter_context(nc.Block())

            def run(engname):
                def body(eh):
                    for kind, fn, waits, di in self.ops[engname]:
                        for (k, e, idx) in waits:
                            if k == "c":
                                eh.wait_ge(csem[e], idx)
                            else:
                                eh.wait_ge(dsem[e][(idx - 1) % NDS], 16 * ((idx - 1) // NDS + 1))
                        if kind == "c":
                            fn(eh).then_inc(csem[engname], 1)
                        elif kind == "d":
                            fn(eh).then_inc(dsem[engname][(di - 1) % NDS], 16)
                return body

            block.sync(run("sp"))
            block.scalar(run("act"))
            block.vector(run("dve"))
            block.gpsimd(run("pool"))
            block.tensor(run("pe"))


def ntiles(total, maxn=512):
    n = -(-total // maxn)
    base = -(-total // n)
    out = []
    s = 0
    while s < total:
        e = min(total, s + base)
        out.append((s, e))
        s = e
    return out


class Ctx:
    def __init__(self, name):
        self.nc = bass.Bass("TRN2", target_bir_lowering=False)
        self.P = Prog(self.nc)
        self.psbig = self.nc.alloc_psum_tensor("psbig", [128, 4096], F32)
        self.ps = [self.psbig[:, b * 512:(b + 1) * 512] for b in range(8)]
        self.uid = 0

    def sb(self, name, shape, dt):
        return self.nc.alloc_sbuf_tensor(name, list(shape), dt)

    def din(self, name, shape, dt):
        return self.nc.dram_tensor(name, list(shape), dt, kind="ExternalInput").ap()

    def dout(self, name, shape, dt):
        return self.nc.dram_tensor(name, list(shape), dt, kind="ExternalOutput").ap()


def build_pre(kind):
    C = Ctx("pre_" + kind)
    nc, P = C.nc, C.P
    T = TC
    ht = C.din("ht", [D, T], F32)
    if kind == "gla":
        dout_w = 2 * GLA_DK + GLA_DV + GLA_RANK + GLA_DV
        w = C.din("w", [D, dout_w], F32)
        wg2 = C.din("wg2", [GLA_RANK, GLA_DK], F32)
        bg2 = C.din("bg2", [128, GLA_DK // 128], F32)
        segs = [("g", 2 * GLA_DK + GLA_DV, 16, "glow", None),
                ("q", 0, GLA_DK, "scale", BF16), ("k", GLA_DK, GLA_DK, "copy", BF16),
                ("v", 2 * GLA_DK, GLA_DV, "copy", BF16),
                ("r", 2 * GLA_DK + GLA_DV + GLA_RANK, GLA_DV, "copy", F32)]
        qscale = GLA_HK ** -0.5
        outs = {"q": C.dout("q", [GLA_DK, T], BF16), "k": C.dout("k", [GLA_DK, T], BF16),
                "v": C.dout("v", [GLA_DV, T], BF16), "r": C.dout("r", [GLA_DV, T], F32),
                "la": C.dout("la", [GLA_DK, T], F32)}
    else:
        wq = C.din("w", [D, 2 * D], F32)
        segs = [("q", 0, D, "scale", BF16), ("og", D, D, "copy", F32)]
        qscale = FOX_HD ** -0.5
        outs = {"q": C.dout("q", [D, T], BF16), "og": C.dout("og", [D, T], F32)}
        if kind == "fox_kv":
            wkv = C.din("wkv", [D, 2 * D + FOX_HEADS], F32)
            bf = C.din("bf", [FOX_HEADS, 1], F32)
            outs.update({"kk": C.dout("kk", [D, T], BF16), "vv": C.dout("vv", [D, T], BF16),
                         "lf": C.dout("lf", [FOX_HEADS, T], F32)})

    hs = C.sb("hs", [128, 16, T], F32)
    hb = C.sb("hb", [128, 16, T], BF16)
    NW = 4
    wt = [C.sb("wt%d" % i, [128, 16, 128], BF16) for i in range(NW)]
    NS = 3
    stf = [C.sb("stf%d" % i, [128, T], F32) for i in range(NS)]
    stb = [C.sb("stb%d" % i, [128, T], BF16) for i in range(NS)]
    tiles = ntiles(T)

    for c in range(16):
        q = "sp" if c % 2 == 0 else "act"
        P.dma(q, lambda e, c=c: e.dma_start(out=hs[:, c, :], in_=ht[c * 128:(c + 1) * 128, :]),
              writes=[("hs", c)])
        eng = "dve" if c % 2 == 0 else "pool"
        P.op(eng, lambda e, c=c: e.tensor_copy(out=hb[:, c, :], in_=hs[:, c, :]),
             reads=[("hs", c)], writes=[("hb", c)])
    hb_tok = [("hb", c) for c in range(16)]

    state = {"wi": 0, "si": 0, "bank": 0}

    def proj_chunk(wsrc, col0, ncols, epilogue):
        wi = state["wi"] % NW
        state["wi"] += 1
        P.dma("pool", lambda e: e.dma_start(
            out=wt[wi][:, :, 0:ncols],
            in_=wsrc[:, col0:col0 + ncols].rearrange("(kc p) j -> p kc j", p=128)),
            writes=[("wt", wi)])
        for (t0, t1) in tiles:
            b = state["bank"] % 4
            state["bank"] += 1
            for kc in range(16):
                P.op("pe", lambda e, kc=kc, b=b, t0=t0, t1=t1: e.matmul(
                    C.ps[b][0:ncols, 0:t1 - t0], lhsT=wt[wi][:, kc, 0:ncols], rhs=hb[:, kc, t0:t1],
                    start=(kc == 0), stop=(kc == 15)),
                    reads=[("wt", wi), ("hb", kc)], excl=[("ps", b)])
            epilogue(b, t0, t1)

    if kind == "gla":
        gl = C.sb("gl", [GLA_RANK, T], BF16)
        wg2f = C.sb("wg2f", [GLA_RANK, GLA_DK], F32)
        wg2b = C.sb("wg2b", [GLA_RANK, GLA_DK], BF16)
        bg = C.sb("bg", [128, 8], F32)
        nbg = C.sb("nbg", [128, 8], F32)
        P.dma("sp", lambda e: e.dma_start(out=wg2f[:], in_=wg2), writes=["wg2f"])
        P.dma("sp", lambda e: e.dma_start(out=bg[:], in_=bg2), writes=["bg"])
        P.op("dve", lambda e: e.tensor_copy(out=wg2b[:], in_=wg2f[:]), reads=["wg2f"], writes=["wg2b"])
        P.op("dve", lambda e: e.tensor_scalar(out=nbg[:], in0=bg[:], scalar1=-1.0, scalar2=None, op0=ALU.mult),
             reads=["bg"], writes=["nbg"])

        def ep_glow(b, t0, t1):
            P.op("act", lambda e: e.activation(out=gl[:, t0:t1], in_=C.ps[b][0:GLA_RANK, 0:t1 - t0], func=AF.Copy),
                 excl=[("ps", b)], writes=[("gl", t0)])
        proj_chunk(w, 2 * GLA_DK + GLA_DV, GLA_RANK, ep_glow)
        for fc in range(8):
            si = state["si"] % NS
            state["si"] += 1
            for (t0, t1) in tiles:
                b = 4 + (state["bank"] % 2)
                state["bank"] += 1
                P.op("pe", lambda e, fc=fc, b=b, t0=t0, t1=t1: e.matmul(
                    C.ps[b][:, 0:t1 - t0], lhsT=wg2b[:, fc * 128:(fc + 1) * 128], rhs=gl[:, t0:t1],
                    start=True, stop=True), reads=["wg2b", ("gl", t0)], excl=[("ps", b)])
                P.op("act", lambda e, fc=fc, b=b, t0=t0, t1=t1, si=si: e.activation(
                    out=stf[si][:, t0:t1], in_=C.ps[b][:, 0:t1 - t0], func=AF.Exp,
                    bias=nbg[:, fc:fc + 1], scale=-1.0),
                    reads=["nbg"], excl=[("ps", b)], writes=[("stf", si)])
            P.op("act", lambda e, si=si: e.activation(out=stf[si][:], in_=stf[si][:], func=AF.Ln, bias=1.0, scale=1.0),
                 reads=[("stf", si)], writes=[("stf", si)])
            P.op("dve", lambda e, si=si: e.tensor_scalar(out=stf[si][:], in0=stf[si][:], scalar1=-1.0 / GLA_TAU,
                                                        scalar2=None, op0=ALU.mult),
                 reads=[("stf", si)], writes=[("stf", si)])
            P.dma("sp", lambda e, fc=fc, si=si: e.dma_start(out=outs["la"][fc * 128:(fc + 1) * 128, :], in_=stf[si][:]),
                  reads=[("stf", si)])

    def run_seg(wsrc, name, col0, ncols, mode, odt, oname=None):
        oname = oname or name
        for ch in range(ncols // 128):
            si = state["si"] % NS
            state["si"] += 1
            dst = stf[si] if odt == F32 else stb[si]
            tok = ("stf", si) if odt == F32 else ("stb", si)

            def ep(b, t0, t1, dst=dst, tok=tok):
                if mode == "scale":
                    P.op("act", lambda e: e.activation(out=dst[:, t0:t1], in_=C.ps[b][:, 0:t1 - t0], func=AF.Copy,
                                                       scale=float(qscale)),
                         excl=[("ps", b)], writes=[tok])
                else:
                    eng = "dve" if (t0 // 8) % 2 == 0 else "act"
                    if eng == "dve":
                        P.op("dve", lambda e: e.tensor_copy(out=dst[:, t0:t1], in_=C.ps[b][:, 0:t1 - t0]),
                             excl=[("ps", b)], writes=[tok])
                    else:
                        P.op("act", lambda e: e.activation(out=dst[:, t0:t1], in_=C.ps[b][:, 0:t1 - t0], func=AF.Copy),
                             excl=[("ps", b)], writes=[tok])
            proj_chunk(wsrc, col0 + ch * 128, 128, ep)
            P.dma("sp" if ch % 2 == 0 else "act",
                  lambda e, ch=ch, dst=dst: e.dma_start(out=outs[oname][ch * 128:(ch + 1) * 128, :], in_=dst[:]),
                  reads=[tok])

    if kind == "gla":
        for (name, col0, ncols, mode, odt) in segs[1:]:
            run_seg(w, name, col0, ncols, mode, odt)
    else:
        for (name, col0, ncols, mode, odt) in segs:
            run_seg(wq, name, col0, ncols, mode, odt)
        if kind == "fox_kv":
            run_seg(wkv, "kk", 0, D, "copy", BF16)
            run_seg(wkv, "vv", D, D, "copy", BF16)
            bfs = C.sb("bfs", [FOX_HEADS, 1], F32)
            nbf = C.sb("nbf", [FOX_HEADS, 1], F32)
            lfs = C.sb("lfs", [FOX_HEADS, T], F32)
            P.dma("sp", lambda e: e.dma_start(out=bfs[:], in_=bf), writes=["bfs"])
            P.op("dve", lambda e: e.tensor_scalar(out=nbf[:], in0=bfs[:], scalar1=-1.0, scalar2=None, op0=ALU.mult),
                 reads=["bfs"], writes=["nbf"])

            def ep_f(b, t0, t1):
                P.op("act", lambda e: e.activation(out=lfs[:, t0:t1], in_=C.ps[b][0:FOX_HEADS, 0:t1 - t0], func=AF.Exp,
                                                   bias=nbf[:, 0:1], scale=-1.0),
                     reads=["nbf"], excl=[("ps", b)], writes=["lfs"])
            proj_chunk(wkv, 2 * D, FOX_HEADS, ep_f)
            P.op("act", lambda e: e.activation(out=lfs[:], in_=lfs[:], func=AF.Ln, bias=1.0, scale=1.0),
                 reads=["lfs"], writes=["lfs"])
            P.op("dve", lambda e: e.tensor_scalar(out=lfs[:], in0=lfs[:], scalar1=-1.0, scalar2=None, op0=ALU.mult),
                 reads=["lfs"], writes=["lfs"])
            P.dma("sp", lambda e: e.dma_start(out=outs["lf"], in_=lfs[:]), reads=["lfs"])

    P.finish()
    P.emit()
    return nc


def build_post(kind):
    C = Ctx("post_" + kind)
    nc, P = C.nc, C.P
    ht = C.din("ht", [D, TP], F32)
    ot = C.din("ot", [D, TP], F32)
    gt = C.din("gt", [D, TP], F32)
    vm = C.din("vmask", [128, TP], F32)
    wo = C.din("wo", [D, D], F32)
    lnp = C.din("lnp", [128, 4, 16], F32)
    wup = C.din("wup", [D, 2 * D_FF], F32)
    cwd = C.din("cw", [128, 88, 4], F32)
    wdn = C.din("wdn", [D_FF, D], F32)
    hout = C.dout("hout", [D, TC], F32)
    if kind == "gla":
        ngd = C.din("ng", [128, 4], F32)

    TW = 343
    TPX = 3 * TW
    hs = C.sb("hs", [128, 16, TP], F32)
    hb = C.sb("hb", [128, 16, TPX + 3], BF16)
    tA = [C.sb("tA%d" % i, [128, TPX + 3], F32) for i in range(2)]
    tB = [C.sb("tB%d" % i, [128, TPX + 3], F32) for i in range(2)]
    tC = [C.sb("tC%d" % i, [128, TPX + 3], F32) for i in range(2)]
    NW = 4
    wt = [C.sb("wt%d" % i, [128, 16, 128], BF16) for i in range(NW)]
    GP = 11
    NG = 4
    act = C.sb("act", [128, GP, TC], BF16)
    NWD = 3
    wd = [C.sb("wd%d" % i, [128, GP, 128], BF16) for i in range(NWD)]
    st = [C.sb("st%d" % i, [128, TP], F32) for i in range(3)]
    ones = C.sb("ones", [128, 128], F32)
    lns = C.sb("lns", [128, 4, 16], F32)
    cws = C.sb("cws", [128, 88, 4], F32)
    vms = C.sb("vms", [128, TP], F32)
    tiles_p = ntiles(TP)
    tiles_c = ntiles(TC)
    cnt = {"w": 0, "wd": 0, "a": 0, "b": 0, "c": 0, "dq": 0}

    def dq():
        cnt["dq"] += 1
        return "sp" if cnt["dq"] % 2 else "act"

    P.op("pool", lambda e: e.memset(ones[:], 1.0), writes=["ones"])
    P.op("pool", lambda e: e.memset(hb[:], 0.0), writes=[("hb", c) for c in range(16)])
    P.dma("sp", lambda e: e.dma_start(out=lns[:], in_=lnp), writes=["lns"])
    P.dma("sp", lambda e: e.dma_start(out=cws[:], in_=cwd), writes=["cws"])
    P.dma("sp", lambda e: e.dma_start(out=vms[:], in_=vm), writes=["vms"])
    if kind == "gla":
        ngs = C.sb("ngs", [128, 4], F32)
        P.dma("sp", lambda e: e.dma_start(out=ngs[:], in_=ngd), writes=["ngs"])
    for c in range(16):
        P.dma(dq(), lambda e, c=c: e.dma_start(out=hs[:, c, :], in_=ht[c * 128:(c + 1) * 128, :]),
              writes=[("hs", c)])

    def load(buf, tokname, src, c):
        i = cnt[tokname] % 2
        cnt[tokname] += 1
        P.dma(dq(), lambda e: e.dma_start(out=buf[i][:, 0:TP], in_=src[c * 128:(c + 1) * 128, :]),
              writes=[(tokname, i)])
        return i

    def rstd_from(src_banks, tiles, scale, dst, dst_tok):
        for ti, (t0, t1) in enumerate(tiles):
            b = src_banks[ti]
            P.op("dve", lambda e, b=b, t0=t0, t1=t1: e.tensor_scalar(
                out=dst[:, t0:t1], in0=C.ps[b][:, 0:t1 - t0], scalar1=float(scale), scalar2=float(LN_EPS),
                op0=ALU.mult, op1=ALU.add), excl=[("ps", b)], writes=[dst_tok])
        P.op("act", lambda e: e.activation(out=dst[:], in_=dst[:], func=AF.Sqrt), reads=[dst_tok],
             writes=[dst_tok])
        P.op("dve", lambda e: e.reciprocal(out=dst[:], in_=dst[:]), reads=[dst_tok],
             writes=[dst_tok])

    if kind == "gla":
        for hd in range(4):
            for cc in range(4):
                c = hd * 4 + cc
                ia = load(tA, "a", ot, c)
                ic = cnt["c"] % 2
                cnt["c"] += 1
                P.op("act", lambda e, ia=ia, ic=ic: e.activation(out=tC[ic][:, 0:TP], in_=tA[ia][:, 0:TP], func=AF.Square),
                     reads=[("a", ia)], writes=[("c", ic)])
                for ti, (t0, t1) in enumerate(tiles_p):
                    P.op("pe", lambda e, ic=ic, ti=ti, t0=t0, t1=t1, cc=cc: e.matmul(
                        C.ps[ti][:, 0:t1 - t0], lhsT=ones[:], rhs=tC[ic][:, t0:t1], start=(cc == 0), stop=(cc == 3)),
                        reads=["ones", ("c", ic)], excl=[("ps", ti)])
            rstd_from([0, 1, 2], tiles_p, 1.0 / GLA_HV, st[0], ("st", 0))
            for cc in range(4):
                c = hd * 4 + cc
                ia = load(tA, "a", ot, c)
                ib = load(tB, "b", gt, c)
                P.op("act", lambda e, ib=ib: e.activation(out=tB[ib][:, 0:TP], in_=tB[ib][:, 0:TP], func=AF.Silu),
                     reads=[("b", ib)], writes=[("b", ib)])
                P.op("dve", lambda e, ia=ia: e.tensor_tensor(out=tA[ia][:, 0:TP], in0=tA[ia][:, 0:TP], in1=st[0][:], op=ALU.mult),
                     reads=[("a", ia), ("st", 0)], writes=[("a", ia)])
                P.op("act", lambda e, ib=ib, cc=cc: e.activation(
                    out=tB[ib][:, 0:TP], in_=tB[ib][:, 0:TP], func=AF.Copy, scale=ngs[:, cc:cc + 1]),
                    reads=[("b", ib), "ngs"], writes=[("b", ib)])
                P.op("pool", lambda e, ia=ia, ib=ib, c=c: e.tensor_tensor(
                    out=hb[:, c, 0:TP], in0=tA[ia][:, 0:TP], in1=tB[ib][:, 0:TP], op=ALU.mult),
                    reads=[("a", ia), ("b", ib)], writes=[("hb", c)])
    else:
        for c in range(16):
            ia = load(tA, "a", ot, c)
            ib = load(tB, "b", gt, c)
            P.op("act", lambda e, ib=ib: e.activation(out=tB[ib][:, 0:TP], in_=tB[ib][:, 0:TP], func=AF.Sigmoid),
                 reads=[("b", ib)], writes=[("b", ib)])
            P.op("dve" if c % 2 == 0 else "pool",
                 lambda e, ia=ia, ib=ib, c=c: e.tensor_tensor(out=hb[:, c, 0:TP], in0=tA[ia][:, 0:TP], in1=tB[ib][:, 0:TP], op=ALU.mult),
                 reads=[("a", ia), ("b", ib)], writes=[("hb", c)])

    def wload(src_ap):
        wi = cnt["w"] % NW
        cnt["w"] += 1
        P.dma("pool", lambda e: e.dma_start(out=wt[wi][:], in_=src_ap.rearrange("(kc p) j -> p kc j", p=128)),
              writes=[("wt", wi)])
        return wi

    obank = [6, 7]
    ob = {"i": 0}
    wow = {}

    def ensure_wo(fc):
        if fc in wow or fc >= 16:
            return
        wow[fc] = wload(wo[:, fc * 128:(fc + 1) * 128])

    for fc in range(16):
        for k in range(3):
            ensure_wo(fc + k)
        wi = wow[fc]
        for (t0, t1) in tiles_p:
            b = obank[ob["i"] % 2]
            ob["i"] += 1
            for kc in range(16):
                P.op("pe", lambda e, kc=kc, b=b, t0=t0, t1=t1, wi=wi: e.matmul(
                    C.ps[b][:, 0:t1 - t0], lhsT=wt[wi][:, kc, :], rhs=hb[:, kc, t0:t1], start=(kc == 0), stop=(kc == 15)),
                    reads=[("wt", wi), ("hb", kc)], excl=[("ps", b)])
            P.op("dve", lambda e, fc=fc, b=b, t0=t0, t1=t1: e.scalar_tensor_tensor(
                out=hs[:, fc, t0:t1], in0=hs[:, fc, t0:t1], scalar=float(ALPHA), in1=C.ps[b][:, 0:t1 - t0],
                op0=ALU.mult, op1=ALU.add), reads=[("hs", fc)], writes=[("hs", fc)], excl=[("ps", b)])

    def layer_norm(lo, gi, bi, make_hb):
        n = TP - lo
        tl = [(lo + a, lo + b_) for (a, b_) in ntiles(n)]
        for c in range(16):
            ic = cnt["c"] % 2
            cnt["c"] += 1
            P.op("act", lambda e, c=c, ic=ic: e.activation(out=tC[ic][:, lo:TP], in_=hs[:, c, lo:TP], func=AF.Square),
                 reads=[("hs", c)], writes=[("c", ic)])
            for ti, (t0, t1) in enumerate(tl):
                P.op("pe", lambda e, c=c, ti=ti, t0=t0, t1=t1: e.matmul(
                    C.ps[ti][:, 0:t1 - t0], lhsT=ones[:], rhs=hs[:, c, t0:t1], start=(c == 0), stop=(c == 15)),
                    reads=["ones", ("hs", c)], excl=[("ps", ti)])
                P.op("pe", lambda e, c=c, ic=ic, ti=ti, t0=t0, t1=t1: e.matmul(
                    C.ps[3 + ti][:, 0:t1 - t0], lhsT=ones[:], rhs=tC[ic][:, t0:t1], start=(c == 0), stop=(c == 15)),
                    reads=["ones", ("c", ic)], excl=[("ps", 3 + ti)])
        mean, var, tmp = st[0], st[1], st[2]
        for ti, (t0, t1) in enumerate(tl):
            P.op("dve", lambda e, ti=ti, t0=t0, t1=t1: e.tensor_scalar(
                out=mean[:, t0:t1], in0=C.ps[ti][:, 0:t1 - t0], scalar1=1.0 / D, scalar2=None, op0=ALU.mult),
                excl=[("ps", ti)], writes=[("st", 0)])
            P.op("dve", lambda e, ti=ti, t0=t0, t1=t1: e.tensor_scalar(
                out=var[:, t0:t1], in0=C.ps[3 + ti][:, 0:t1 - t0], scalar1=1.0 / D, scalar2=float(LN_EPS),
                op0=ALU.mult, op1=ALU.add), excl=[("ps", 3 + ti)], writes=[("st", 1)])
        P.op("dve", lambda e: e.tensor_tensor(out=tmp[:, lo:TP], in0=mean[:, lo:TP], in1=mean[:, lo:TP], op=ALU.mult),
             reads=[("st", 0)], writes=[("st", 2)])
        P.op("dve", lambda e: e.tensor_tensor(out=var[:, lo:TP], in0=var[:, lo:TP], in1=tmp[:, lo:TP], op=ALU.subtract),
             reads=[("st", 1), ("st", 2)], writes=[("st", 1)])
        P.op("act", lambda e: e.activation(out=var[:, lo:TP], in_=var[:, lo:TP], func=AF.Sqrt), reads=[("st", 1)], writes=[("st", 1)])
        P.op("dve", lambda e: e.reciprocal(out=var[:, lo:TP], in_=var[:, lo:TP]), reads=[("st", 1)], writes=[("st", 1)])
        for c in range(16):
            ia = cnt["a"] % 2
            cnt["a"] += 1
            P.op("dve", lambda e, c=c, ia=ia: e.tensor_tensor(out=tA[ia][:, lo:TP], in0=hs[:, c, lo:TP], in1=mean[:, lo:TP],
                                                             op=ALU.subtract),
                 reads=[("hs", c), ("st", 0)], writes=[("a", ia)])
            P.op("pool", lambda e, ia=ia: e.tensor_tensor(out=tA[ia][:, lo:TP], in0=tA[ia][:, lo:TP], in1=var[:, lo:TP],
                                                          op=ALU.mult),
                 reads=[("a", ia), ("st", 1)], writes=[("a", ia)])
            P.op("dve", lambda e, c=c, ia=ia: e.tensor_scalar(
                out=hs[:, c, lo:TP], in0=tA[ia][:, lo:TP], scalar1=lns[:, gi, c:c + 1], scalar2=lns[:, bi, c:c + 1],
                op0=ALU.mult, op1=ALU.add), reads=[("a", ia), "lns"], writes=[("hs", c)])
            if make_hb:
                P.op("pool", lambda e, c=c: e.tensor_tensor(out=hb[:, c, 0:TP], in0=hs[:, c, :], in1=vms[:], op=ALU.mult),
                     reads=[("hs", c), "vms"], writes=[("hb", c)])

    layer_norm(0, 0, 1, True)

    xu, xg, cu, cg = tA, tB, tC[0], tC[1]
    upw = {}
    dnw = {}

    def ensure_up(j):
        if j in upw or j >= NG * GP:
            return
        upw[j] = (wload(wup[:, j * 128:(j + 1) * 128]), wload(wup[:, D_FF + j * 128:D_FF + (j + 1) * 128]))

    def ensure_down(g, fc):
        if (g, fc) in dnw or fc >= 16:
            return
        wi = cnt["wd"] % NWD
        cnt["wd"] += 1
        P.dma("pool", lambda e: e.dma_start(
            out=wd[wi][:], in_=wdn[g * GP * 128:(g + 1) * GP * 128, fc * 128:(fc + 1) * 128].rearrange(
                "(jj p) m -> p jj m", p=128)), writes=[("wd", wi)])
        dnw[(g, fc)] = wi
    for g in range(NG):
        for jj in range(GP):
            j = g * GP + jj
            ensure_up(j)
            ensure_up(j + 1)
            if jj == GP - 1:
                ensure_down(g, 0)
                ensure_down(g, 1)
            wa, wb = upw[j]
            for (wi, b0, X, xt_, col) in ((wa, 0, (tA[0], tA[1], tC[0]), ("a", 0), j), (wb, 3, (tB[0], tB[1], tC[1]), ("b", 0), 44 + j)):
                for ti in range(3):
                    for kc in range(16):
                        P.op("pe", lambda e, kc=kc, wi=wi, b0=b0, ti=ti: e.matmul(
                            C.ps[b0 + ti][:, 0:TW], lhsT=wt[wi][:, kc, :], rhs=hb[:, kc, ti * TW:(ti + 1) * TW],
                            start=(kc == 0), stop=(kc == 15)),
                            reads=[("wt", wi), ("hb", kc)], excl=[("ps", b0 + ti)])
                src = C.psbig[:, b0 * 512:(b0 + 3) * 512].rearrange("p (b c) -> p b c", c=512)[:, :, 0:TW]
                banks = [("ps", b0), ("ps", b0 + 1), ("ps", b0 + 2)]
                for tap, buf, tok in ((2, X[0], (xt_[0], 0)), (1, X[1], (xt_[0], 1)), (0, X[2], ("c", 0 if b0 == 0 else 1))):
                    P.op("act", lambda e, tap=tap, buf=buf, src=src, col=col: e.activation(
                        out=buf[:, 0:TPX].rearrange("p (b c) -> p b c", c=TW), in_=src, func=AF.Identity,
                        scale=cws[:, col, tap:tap + 1], bias=(cws[:, col, 3:4] if tap == 2 else 0.0)),
                        reads=["cws"], excl=banks, writes=[tok])
                t2, t1_, t0_ = (xt_[0], 0), (xt_[0], 1), ("c", 0 if b0 == 0 else 1)
                P.op("dve", lambda e, X=X: e.tensor_tensor(out=X[0][:, 2:TP], in0=X[0][:, 2:TP], in1=X[1][:, 1:TP - 1], op=ALU.add),
                     reads=[t2, t1_], writes=[t2])
                P.op("dve", lambda e, X=X: e.tensor_tensor(out=X[0][:, 2:TP], in0=X[0][:, 2:TP], in1=X[2][:, 0:TC], op=ALU.add),
                     reads=[t2, t0_], writes=[t2])
            P.op("act", lambda e: e.activation(out=tB[0][:, 2:TP], in_=tB[0][:, 2:TP], func=AF.Silu), reads=[("b", 0)], writes=[("b", 0)])
            P.op("dve", lambda e, jj=jj: e.tensor_tensor(out=act[:, jj, :], in0=tA[0][:, 2:TP], in1=tB[0][:, 2:TP], op=ALU.mult),
                 reads=[("a", 0), ("b", 0)], writes=[("act", jj)])
        for fc in range(16):
            ensure_down(g, fc)
            ensure_down(g, fc + 1)
            ensure_down(g, fc + 2)
            wi = dnw[(g, fc)]
            for (t0, t1) in tiles_c:
                b = obank[ob["i"] % 2]
                ob["i"] += 1
                for jj in range(GP):
                    P.op("pe", lambda e, jj=jj, wi=wi, b=b, t0=t0, t1=t1: e.matmul(
                        C.ps[b][:, 0:t1 - t0], lhsT=wd[wi][:, jj, :], rhs=act[:, jj, t0:t1],
                        start=(jj == 0), stop=(jj == GP - 1)),
                        reads=[("wd", wi), ("act", jj)], excl=[("ps", b)])
                P.op("dve", lambda e, fc=fc, b=b, t0=t0, t1=t1, g=g: e.scalar_tensor_tensor(
                    out=hs[:, fc, 2 + t0:2 + t1], in0=hs[:, fc, 2 + t0:2 + t1], scalar=float(ALPHA if g == 0 else 1.0),
                    in1=C.ps[b][:, 0:t1 - t0], op0=ALU.mult, op1=ALU.add),
                    reads=[("hs", fc)], writes=[("hs", fc)], excl=[("ps", b)])

    layer_norm(2, 2, 3, False)
    for c in range(16):
        P.dma(dq(), lambda e, c=c: e.dma_start(out=hout[c * 128:(c + 1) * 128, :], in_=hs[:, c, 2:TP]),
              reads=[("hs", c)])
    P.finish()
    P.emit()
    return nc


def build_scan(debug=False):
    C = Ctx("scan")
    nc, P = C.nc, C.P
    L = LPAD
    qf = C.din("qf", [256, L], BF16)
    kf = C.din("kf", [256, L], BF16)
    ktm = C.din("ktm", [L, 256], BF16)
    vtm = C.din("vtm", [L, 256], BF16)
    latm = C.din("latm", [L, 256], F32)
    cmd = C.din("cm", [128, 2, 128], F32)
    od = C.dout("o", [L, 256], F32)
    GT = 13
    NGRP = NT // GT
    GL = GT * 128
    qin = [C.sb("qin%d" % i, [128, 2, GL], BF16) for i in range(2)]
    kin = [C.sb("kin%d" % i, [128, 2, GL], BF16) for i in range(2)]
    ktin = [C.sb("ktin%d" % i, [128, GT, 256], BF16) for i in range(2)]
    vtin = [C.sb("vtin%d" % i, [128, GT, 256], BF16) for i in range(2)]
    lain = [C.sb("lain%d" % i, [128, GT, 256], F32) for i in range(2)]
    cm = C.sb("cm_sb", [128, 2, 128], F32)
    eb = [C.sb("eb%d" % i, [128, 2, 128], F32) for i in range(2)]
    enb = [C.sb("enb%d" % i, [128, 2, 128], F32) for i in range(2)]
    qg = [C.sb("qg%d" % i, [128, 2, 128], BF16) for i in range(2)]
    kg = [C.sb("kg%d" % i, [128, 2, 128], BF16) for i in range(2)]
    ekd = [C.sb("ekd%d" % i, [128, 256], F32) for i in range(2)]
    kd = [C.sb("kd%d" % i, [128, 256], BF16) for i in range(2)]
    attm = [C.sb("attm%d" % i, [128, 128], BF16) for i in range(2)]
    osb = [C.sb("osb%d" % i, [64, 2, 256], F32) for i in range(2)]
    S = C.sb("S_sb", [128, 2, 256], F32)
    Sb = C.sb("Sb_sb", [128, 2, 256], BF16)
    ps = C.ps

    P.dma("sp", lambda e: e.dma_start(out=cm[:], in_=cmd), writes=["cm"])
    P.op("dve", lambda e: e.memset(S[:], 0.0), writes=[("S", 0), ("S", 1)])
    P.op("pool", lambda e: e.memset(Sb[:], 0.0), writes=[("Sb", 0), ("Sb", 1)])

    def load_group(g):
        i = g % 2
        c0 = g * GL
        P.dma("sp", lambda e: e.dma_start(out=qin[i][:], in_=qf[:, c0:c0 + GL].rearrange("(kc p) t -> p kc t", p=128)),
              writes=[("qin", i)])
        P.dma("act", lambda e: e.dma_start(out=kin[i][:], in_=kf[:, c0:c0 + GL].rearrange("(kc p) t -> p kc t", p=128)),
              writes=[("kin", i)])
        P.dma("sp", lambda e: e.dma_start(out=ktin[i][:], in_=ktm[c0:c0 + GL, :].rearrange("(t p) k -> p t k", p=128)),
              writes=[("ktin", i)])
        P.dma("act", lambda e: e.dma_start(out=vtin[i][:], in_=vtm[c0:c0 + GL, :].rearrange("(t p) k -> p t k", p=128)),
              writes=[("vtin", i)])
        P.dma("sp", lambda e: e.dma_start(out=lain[i][:], in_=latm[c0:c0 + GL, :].rearrange("(t p) k -> p t k", p=128)),
              writes=[("lain", i)])

    def precompute(tt):
        g, tl = divmod(tt, GT)
        gi = g % 2
        pr = tt % 2
        bb = 0 if pr == 0 else 6
        b2 = 1 if pr == 0 else 7
        cs = slice(tl * 128, (tl + 1) * 128)
        for kc in range(2):
            P.op("pe", lambda e, kc=kc: e.matmul(ps[bb][:, kc * 128:(kc + 1) * 128], lhsT=lain[gi][:, tl, kc * 128:(kc + 1) * 128],
                                                rhs=cm[:, 0, :], start=True, stop=True),
                 reads=[("lain", gi), "cm"], excl=[("ps", bb)])
        P.op("act", lambda e: e.activation(out=eb[pr][:].rearrange("p a b -> p (a b)"), in_=ps[bb][:, 0:256], func=AF.Exp),
             excl=[("ps", bb)], writes=[("eb", pr)])
        P.op("act", lambda e: e.activation(out=enb[pr][:].rearrange("p a b -> p (a b)"), in_=ps[bb][:, 0:256], func=AF.Exp, scale=-1.0),
             excl=[("ps", bb)], writes=[("enb", pr)])
        P.op("dve", lambda e: e.tensor_tensor(out=qg[pr][:], in0=qin[gi][:, :, cs], in1=eb[pr][:], op=ALU.mult),
             reads=[("qin", gi), ("eb", pr)], writes=[("qg", pr)])
        P.op("pool", lambda e: e.tensor_tensor(out=kg[pr][:], in0=kin[gi][:, :, cs], in1=enb[pr][:], op=ALU.mult),
             reads=[("kin", gi), ("enb", pr)], writes=[("kg", pr)])
        P.op("pe", lambda e: e.matmul(ps[b2][:, 0:256], lhsT=cm[:, 1, :], rhs=lain[gi][:, tl, :], start=True, stop=True),
             reads=[("lain", gi), "cm"], excl=[("ps", b2)])
        P.op("act", lambda e: e.activation(out=ekd[pr][:], in_=ps[b2][:, 0:256], func=AF.Exp),
             excl=[("ps", b2)], writes=[("ekd", pr)])
        P.op("pool", lambda e: e.tensor_tensor(out=kd[pr][:], in0=ktin[gi][:, tl, :], in1=ekd[pr][:], op=ALU.mult),
             reads=[("ktin", gi), ("ekd", pr)], writes=[("kd", pr)])
        for kc in range(2):
            P.op("pe", lambda e, kc=kc: e.matmul(ps[b2][:, 256:384], lhsT=kg[pr][:, kc, :], rhs=qg[pr][:, kc, :],
                                                start=(kc == 0), stop=(kc == 1)),
                 reads=[("kg", pr), ("qg", pr)], excl=[("ps", b2)])
        P.op("dve", lambda e: e.tensor_tensor(out=attm[pr][:], in0=ps[b2][:, 256:384], in1=cm[:, 0, :], op=ALU.mult),
             reads=["cm"], excl=[("ps", b2)], writes=[("attm", pr)])

    def chain(tt):
        g, tl = divmod(tt, GT)
        gi = g % 2
        pr = tt % 2
        for s in range(2):
            rs = slice(s * 64, (s + 1) * 64)
            ob = 3 + s
            P.op("pe", lambda e, rs=rs, ob=ob: e.matmul(ps[ob][0:64, 0:256], lhsT=attm[pr][rs, rs], rhs=vtin[gi][rs, tl, :],
                                          start=True, stop=False),
                 reads=[("attm", pr), ("vtin", gi)], excl=[("ps", ob)])
            for kc in range(2):
                P.op("pe", lambda e, kc=kc, rs=rs, ob=ob: e.matmul(ps[ob][0:64, 0:256], lhsT=qg[pr][:, kc, rs], rhs=Sb[:, kc, :],
                                                    start=False, stop=(kc == 1)),
                     reads=[("qg", pr), ("Sb", kc)], excl=[("ps", ob)])
            P.op("act", lambda e, s=s, ob=ob: e.activation(out=osb[pr][:, s, :], in_=ps[ob][0:64, 0:256], func=AF.Copy),
                 excl=[("ps", ob)], writes=[("osb", pr)])
            for kc in range(2):
                P.op("pe", lambda e, kc=kc, rs=rs: e.matmul(ps[5][:, kc * 256:(kc + 1) * 256], lhsT=kd[pr][rs, kc * 128:(kc + 1) * 128],
                                                    rhs=vtin[gi][rs, tl, :], start=True, stop=True),
                     reads=[("kd", pr), ("vtin", gi)], excl=[("ps", 5)])
            for kc in range(2):
                P.op("dve", lambda e, kc=kc, s=s: e.scalar_tensor_tensor(
                    out=S[:, kc, :], in0=S[:, kc, :], scalar=eb[pr][:, kc, s * 64 + 63:s * 64 + 64],
                    in1=ps[5][:, kc * 256:(kc + 1) * 256], op0=ALU.mult, op1=ALU.add),
                    reads=[("S", kc), ("eb", pr)], writes=[("S", kc)], excl=[("ps", 5)])
                P.op("pool", lambda e, kc=kc: e.tensor_copy(out=Sb[:, kc, :], in_=S[:, kc, :]),
                     reads=[("S", kc)], writes=[("Sb", kc)])
        P.dma("sp" if tt % 2 == 0 else "act",
              lambda e: e.dma_start(out=od[tt * 128:(tt + 1) * 128, :].rearrange("(s i) v -> i s v", s=2), in_=osb[pr][:]),
              reads=[("osb", pr)])

    load_group(0)
    precompute(0)
    if debug:
        dbg = {n: C.dout("dbg_" + n, shp, dt) for n, shp, dt in (("eb", [128, 256], F32), ("qg", [128, 256], BF16),
               ("kg", [128, 256], BF16), ("kd", [128, 256], BF16), ("attm", [128, 128], BF16), ("ekd", [128, 256], F32))}
        P.dma("sp", lambda e: e.dma_start(out=dbg["eb"], in_=eb[0][:].rearrange("p a b -> p (a b)")), reads=[("eb", 0)])
        P.dma("sp", lambda e: e.dma_start(out=dbg["qg"], in_=qg[0][:].rearrange("p a b -> p (a b)")), reads=[("qg", 0)])
        P.dma("sp", lambda e: e.dma_start(out=dbg["kg"], in_=kg[0][:].rearrange("p a b -> p (a b)")), reads=[("kg", 0)])
        P.dma("sp", lambda e: e.dma_start(out=dbg["kd"], in_=kd[0][:]), reads=[("kd", 0)])
        P.dma("sp", lambda e: e.dma_start(out=dbg["attm"], in_=attm[0][:]), reads=[("attm", 0)])
        P.dma("sp", lambda e: e.dma_start(out=dbg["ekd"], in_=ekd[0][:]), reads=[("ekd", 0)])
    for tt in range(NT if not debug else 2):
        g, tl = divmod(tt, GT)
        if tl == 0 and g + 1 < NGRP:
            load_group(g + 1)
        if tt + 1 < NT:
            precompute(tt + 1)
        chain(tt)
    P.finish()
    P.emit()
    return nc


HPC = FOX_HEADS // NCORE


def build_attn():
    C = Ctx("attn")
    nc, P = C.nc, C.P
    L = LPAD
    qf = C.din("qf", [HPC, 128, L], BF16)
    kf = C.din("kf", [HPC, 128, L], BF16)
    vtm = C.din("vtm", [HPC, L, 128], BF16)
    lft = C.din("lft", [HPC, 128, NT], F32)
    cmd = C.din("cm", [128, 4, 128], F32)
    lmd = C.din("lm", [NT, NT], F32)
    pmd = C.din("pm", [3, 3], F32)
    od = C.dout("o", [HPC, L, 128], F32)
    scr = [nc.dram_tensor("cscr%d" % h, [NT, 128], F32, kind="Internal").ap() for h in range(HPC)]
    ps = C.ps

    cm = C.sb("cm_sb", [128, 4, 128], F32)
    lm = C.sb("lm_sb", [NT, NT], F32)
    pm = C.sb("pm_sb", [3, 3], F32)
    ibf = C.sb("ibf", [128, 128], BF16)
    mneg = C.sb("mneg", [128, 128], BF16)
    ones3 = C.sb("ones3", [3, 128], BF16)
    qh = C.sb("qh", [128, L], BF16)
    kh = C.sb("kh", [128, L], BF16)
    vaug = C.sb("vaug", [128, NT, 129], BF16)
    lfs = C.sb("lfs", [128, NT], F32)
    asb = C.sb("asb", [NT, 128], F32)
    csb = C.sb("csb", [128, NT], F32)
    negc = C.sb("negc", [128, NT], F32)
    ctsb = C.sb("ctsb", [NT, 128], F32)
    cqf = C.sb("cqf", [3, L], F32)
    cr1 = C.sb("cr1", [3, L], F32)
    chi = C.sb("chi", [3, L], BF16)
    cq3 = C.sb("cq3", [3, L], BF16)
    pt = [C.sb("pt%d" % i, [128, 512], BF16) for i in range(3)]
    osb = [C.sb("osb%d" % i, [128, 4, 128], F32) for i in range(2)]
    rec = [C.sb("rec%d" % i, [128, 4], F32) for i in range(2)]

    P.dma("sp", lambda e: e.dma_start(out=cm[:], in_=cmd), writes=["cm"])
    P.dma("sp", lambda e: e.dma_start(out=lm[:], in_=lmd), writes=["lm"])
    P.dma("sp", lambda e: e.dma_start(out=pm[:], in_=pmd), writes=["pm"])
    P.op("dve", lambda e: e.tensor_copy(out=ibf[:], in_=cm[:, 2, :]), reads=["cm"], writes=["ibf"])
    P.op("dve", lambda e: e.tensor_copy(out=mneg[:], in_=cm[:, 3, :]), reads=["cm"], writes=["mneg"])
    P.op("pool", lambda e: e.memset(ones3[:], 1.0), writes=["ones3"])

    cnt = {"sb": 0, "pb": 0, "ob": 0}
    for h in range(HPC):
        P.dma("sp", lambda e, h=h: e.dma_start(out=qh[:], in_=qf[h]), writes=["qh"])
        P.dma("act", lambda e, h=h: e.dma_start(out=kh[:], in_=kf[h]), writes=["kh"])
        P.op("pool", lambda e: e.memset(vaug[:], 1.0), writes=["vaug"])
        P.dma("sp", lambda e, h=h: e.dma_start(out=vaug[:, :, 0:128], in_=vtm[h].rearrange("(t p) d -> p t d", p=128)),
              writes=["vaug"])
        P.dma("act", lambda e, h=h: e.dma_start(out=lfs[:], in_=lft[h]), writes=["lfs"])
        P.op("pe", lambda e: e.matmul(ps[6][0:NT, 0:128], lhsT=lfs[:], rhs=cm[:, 1, :], start=True, stop=True),
             reads=["lfs", "cm"], excl=[("ps", 6)])
        P.op("act", lambda e: e.activation(out=asb[:], in_=ps[6][0:NT, 0:128], func=AF.Copy), excl=[("ps", 6)], writes=["asb"])
        P.op("pe", lambda e: e.matmul(ps[7][:, 0:NT], lhsT=cm[:, 0, :], rhs=lfs[:], start=True, stop=False),
             reads=["lfs", "cm"], excl=[("ps", 7)])
        P.op("pe", lambda e: e.matmul(ps[7][:, 0:NT], lhsT=asb[:], rhs=lm[:], start=False, stop=True),
             reads=["asb", "lm"], excl=[("ps", 7)])
        P.op("dve", lambda e: e.tensor_copy(out=csb[:], in_=ps[7][:, 0:NT]), excl=[("ps", 7)], writes=["csb"])
        P.op("dve", lambda e: e.tensor_scalar(out=negc[:], in0=csb[:], scalar1=-1.0, scalar2=None, op0=ALU.mult),
             reads=["csb"], writes=["negc"])
        P.op("pe", lambda e: e.transpose(ps[6][0:NT, 0:128], csb[:], cm[:, 2, :]), reads=["csb", "cm"], excl=[("ps", 6)])
        P.op("act", lambda e: e.activation(out=ctsb[:], in_=ps[6][0:NT, 0:128], func=AF.Copy), excl=[("ps", 6)], writes=["ctsb"])
        P.dma("sp", lambda e, h=h: e.dma_start(out=scr[h], in_=ctsb[:]), reads=["ctsb"], writes=[("scr", h)])
        for r in range(3):
            P.dma("sp" if r != 1 else "act",
                  lambda e, h=h, r=r: e.dma_start(out=cqf[r:r + 1, :], in_=scr[h].rearrange("(o t) j -> o (t j)", o=1)),
                  reads=[("scr", h)], writes=["cqf"])
        P.op("dve", lambda e: e.tensor_copy(out=chi[:], in_=cqf[:]), reads=["cqf"], writes=["chi"])
        P.op("dve", lambda e: e.tensor_scalar(out=cq3[:], in0=chi[:], scalar1=pm[:, 0:1], scalar2=None, op0=ALU.mult),
             reads=["chi", "pm"], writes=["cq3"])
        P.op("dve", lambda e: e.tensor_tensor(out=cr1[:], in0=cqf[:], in1=chi[:], op=ALU.subtract),
             reads=["cqf", "chi"], writes=["cr1"])
        P.op("dve", lambda e: e.tensor_copy(out=chi[:], in_=cr1[:]), reads=["cr1"], writes=["chi"])
        P.op("dve", lambda e: e.scalar_tensor_tensor(out=cq3[:], in0=chi[:], scalar=pm[:, 1:2], in1=cq3[:],
                                                     op0=ALU.mult, op1=ALU.add), reads=["chi", "pm", "cq3"], writes=["cq3"])
        P.op("dve", lambda e: e.tensor_tensor(out=cr1[:], in0=cr1[:], in1=chi[:], op=ALU.subtract),
             reads=["cr1", "chi"], writes=["cr1"])
        P.op("dve", lambda e: e.tensor_copy(out=chi[:], in_=cr1[:]), reads=["cr1"], writes=["chi"])
        P.op("dve", lambda e: e.scalar_tensor_tensor(out=cq3[:], in0=chi[:], scalar=pm[:, 2:3], in1=cq3[:],
                                                     op0=ALU.mult, op1=ALU.add), reads=["chi", "pm", "cq3"], writes=["cq3"])
        steps = []
        for q0 in range(0, NT, 4):
            nq = min(4, NT - q0)
            for kt in range(q0 + nq):
                steps.append((q0, nq, kt))
        LA = 2
        stb = [0, 1, 6]
        info = {}

        def emit_st(i):
            q0, nq, kt = steps[i]
            r0 = max(0, kt - q0)
            n = (nq - r0) * 128
            qa, qb = (q0 + r0) * 128, (q0 + nq) * 128
            sbk = stb[cnt["sb"] % 3]
            cnt["sb"] += 1
            diag = kt >= q0
            info[i] = (sbk, r0, n)
            P.op("pe", lambda e: e.matmul(ps[sbk][:, 0:n], lhsT=kh[:, kt * 128:(kt + 1) * 128], rhs=qh[:, qa:qb],
                                          start=True, stop=False), reads=["kh", "qh"], excl=[("ps", sbk)])
            P.op("pe", lambda e: e.matmul(ps[sbk][:, 0:n], lhsT=ones3[:], rhs=cq3[:, qa:qb], start=False, stop=(not diag)),
                 reads=["ones3", "cq3"], excl=[("ps", sbk)])
            if diag:
                P.op("pe", lambda e: e.matmul(ps[sbk][:, 0:128], lhsT=ibf[:], rhs=mneg[:], start=False, stop=True),
                     reads=["ibf", "mneg"], excl=[("ps", sbk)])

        def emit_exp_pv(i):
            q0, nq, kt = steps[i]
            sbk, r0, n = info[i]
            pb = cnt["pb"] % 3
            cnt["pb"] += 1
            P.op("act", lambda e: e.activation(out=pt[pb][:, 0:n], in_=ps[sbk][:, 0:n], func=AF.Exp,
                                               bias=negc[:, kt:kt + 1], scale=1.0),
                 reads=["negc"], excl=[("ps", sbk)], writes=[("pt", pb)])
            for r in range(r0, nq):
                qt = q0 + r
                P.op("pe", lambda e, r=r, qt=qt: e.matmul(
                    ps[2 + r][:, 0:129], lhsT=pt[pb][:, (r - r0) * 128:(r - r0 + 1) * 128], rhs=vaug[:, kt, :],
                    start=(kt == 0), stop=(kt == qt)),
                    reads=[("pt", pb), "vaug"], excl=[("ps", 2 + r)])

        def finalize(q0, nq):
            oi = cnt["ob"] % 2
            cnt["ob"] += 1
            for r in range(nq):
                P.op("dve", lambda e, r=r: e.reciprocal(out=rec[oi][:, r:r + 1], in_=ps[2 + r][:, 128:129]),
                     excl=[("ps", 2 + r)], writes=[("rec", oi)])
                P.op("dve", lambda e, r=r: e.tensor_scalar(out=osb[oi][:, r, :], in0=ps[2 + r][:, 0:128],
                                                          scalar1=rec[oi][:, r:r + 1], scalar2=None, op0=ALU.mult),
                     reads=[("rec", oi)], excl=[("ps", 2 + r)], writes=[("osb", oi)])
            P.dma("sp" if oi == 0 else "act",
                  lambda e, h=h: e.dma_start(
                      out=od[h, q0 * 128:(q0 + nq) * 128, :].rearrange("(r i) d -> i r d", i=128), in_=osb[oi][:, 0:nq, :]),
                  reads=[("osb", oi)])

        nst = len(steps)
        for i in range(min(LA, nst)):
            emit_st(i)
        for i in range(nst):
            if i + LA < nst:
                emit_st(i + LA)
            emit_exp_pv(i)
            q0, nq, kt = steps[i]
            if kt == q0 + nq - 1:
                finalize(q0, nq)
    P.finish()
    P.emit()
    return nc


def attn_consts():
    ii = np.arange(128)
    cm = np.zeros((128, 4, 128), np.float32)
    cm[:, 0, :] = (ii[:, None] <= ii[None, :])
    cm[:, 1, :] = 1.0
    cm[:, 2, :] = np.eye(128)
    cm[:, 3, :] = np.where(ii[:, None] > ii[None, :], NEG, 0.0)
    tt = np.arange(NT)
    lm = (tt[:, None] < tt[None, :]).astype(np.float32)
    return {"cm": cm, "lm": lm, "pm": np.eye(3, dtype=np.float32)}


_PROGS = {}


def _prog(name):
    if name not in _PROGS:
        if name.startswith("pre_"):
            _PROGS[name] = build_pre(name[4:])
        elif name.startswith("post_"):
            _PROGS[name] = build_post(name[5:])
        elif name == "scan":
            _PROGS[name] = build_scan()
        elif name == "attn":
            _PROGS[name] = build_attn()
    return _PROGS[name]


def _run(name, in_maps):
    res = run_bass_kernel_spmd(_prog(name), in_maps, core_ids=list(range(NCORE)))
    return res.results


def _fm(vec, nch):
    return np.ascontiguousarray(np.asarray(vec, np.float32).reshape(nch, 128).T)


def _cat_tokens(results, key, dtype):
    full = np.concatenate([np.asarray(r[key]) for r in results], axis=1)
    out = np.zeros((full.shape[0], LPAD), dtype)
    out[:, :NTOK] = full
    return out


def _scan_consts():
    ii = np.arange(128)
    same = (ii[:, None] // 64) == (ii[None, :] // 64)
    return np.stack([((ii[:, None] <= ii[None, :]) & same), ((ii[:, None] > ii[None, :]) & same)],
                    axis=1).astype(np.float32)


def _post_inputs(h, o_full, g_fm, c):
    lo = c * TC - HALO
    def tok_major(a):
        if lo < 0:
            blk = np.concatenate([np.zeros((HALO, D), np.float32), a[0:TC]], axis=0)
        else:
            blk = a[lo:lo + TP]
        return np.ascontiguousarray(blk.T)
    if lo < 0:
        g = np.concatenate([np.zeros((D, HALO), np.float32), g_fm[:, 0:TC]], axis=1)
    else:
        g = g_fm[:, lo:lo + TP]
    vm = np.ones((128, TP), np.float32)
    if c == 0:
        vm[:, 0:HALO] = 0.0
    return tok_major(h), tok_major(o_full), np.ascontiguousarray(g, dtype=np.float32), vm


def kernel(x, meta, ln_g, ln_b, gla_w_in, gla_w_g2, gla_b_g2, gla_norm_g, gla_w_out,
           kv_w, kv_bf, fox_w_in, fox_w_out, ffn_w_up, ffn_conv_w, ffn_conv_b, ffn_w_down):
    f32 = np.float32
    x = np.asarray(x, f32)
    h = np.concatenate([np.asarray(meta, f32), x[0]], axis=0)
    ln_g, ln_b = np.asarray(ln_g, f32), np.asarray(ln_b, f32)
    kF = vF = lfF = None
    for l in range(DEPTH):
        hts = [np.ascontiguousarray(h[c * TC:(c + 1) * TC].T) for c in range(NCORE)]
        if l < 2:
            w_in = np.asarray(gla_w_in[l], f32)
            wg2 = np.asarray(gla_w_g2[l], f32)
            bg2 = _fm(gla_b_g2[l], 8)
            res = _run("pre_gla", [{"ht": hts[c], "w": w_in, "wg2": wg2, "bg2": bg2} for c in range(NCORE)])
            qF = _cat_tokens(res, "q", NPBF)
            kFg = _cat_tokens(res, "k", NPBF)
            vFg = _cat_tokens(res, "v", NPBF)
            laF = _cat_tokens(res, "la", f32)
            rF = _cat_tokens(res, "r", f32)
            cmc = _scan_consts()
            ims = []
            for u in range(NCORE):
                hd, vh = divmod(u, 2)
                ks = kFg[hd * 256:(hd + 1) * 256]
                ims.append({"qf": np.ascontiguousarray(qF[hd * 256:(hd + 1) * 256]), "kf": np.ascontiguousarray(ks),
                            "ktm": np.ascontiguousarray(ks.T),
                            "vtm": np.ascontiguousarray(vFg[hd * 512 + vh * 256:hd * 512 + (vh + 1) * 256].T),
                            "latm": np.ascontiguousarray(laF[hd * 256:(hd + 1) * 256].T), "cm": cmc})
            res = _run("scan", ims)
            o_full = np.zeros((NTOK, D), f32)
            for u in range(NCORE):
                hd, vh = divmod(u, 2)
                o_full[:, hd * 512 + vh * 256:hd * 512 + (vh + 1) * 256] = np.asarray(res[u]["o"])[:NTOK]
            g_fm = rF
            pname = "post_gla"
            w_out = np.asarray(gla_w_out[l], f32)
        else:
            j = l - 2
            w_in = np.asarray(fox_w_in[j], f32)
            if l == 2:
                wkv = np.asarray(kv_w, f32)
                bfv = np.asarray(kv_bf, f32).reshape(FOX_HEADS, 1)
                res = _run("pre_fox_kv", [{"ht": hts[c], "w": w_in, "wkv": wkv, "bf": bfv} for c in range(NCORE)])
                kF = _cat_tokens(res, "kk", NPBF)
                vF = _cat_tokens(res, "vv", NPBF)
                lfF = _cat_tokens(res, "lf", f32)
            else:
                res = _run("pre_fox", [{"ht": hts[c], "w": w_in} for c in range(NCORE)])
            qF = _cat_tokens(res, "q", NPBF)
            ogF = _cat_tokens(res, "og", f32)
            ac = attn_consts()
            ims = []
            for c in range(NCORE):
                rows = slice(c * HPC * 128, (c + 1) * HPC * 128)
                im = dict(ac)
                im["qf"] = np.ascontiguousarray(qF[rows].reshape(HPC, 128, LPAD))
                im["kf"] = np.ascontiguousarray(kF[rows].reshape(HPC, 128, LPAD))
                im["vtm"] = np.ascontiguousarray(vF[rows].reshape(HPC, 128, LPAD).transpose(0, 2, 1))
                im["lft"] = np.ascontiguousarray(lfF[c * HPC:(c + 1) * HPC].reshape(HPC, NT, 128).transpose(0, 2, 1))
                ims.append(im)
            res = _run("attn", ims)
            o_full = np.zeros((NTOK, D), f32)
            for c in range(NCORE):
                o = np.asarray(res[c]["o"])
                for hh in range(HPC):
                    o_full[:, (c * HPC + hh) * 128:(c * HPC + hh + 1) * 128] = o[hh, :NTOK]
            g_fm = ogF
            pname = "post_fox"
            w_out = np.asarray(fox_w_out[j], f32)
        lnp = np.ascontiguousarray(np.stack([_fm(ln_g[l, 0], 16), _fm(ln_b[l, 0], 16), _fm(ln_g[l, 1], 16),
                                             _fm(ln_b[l, 1], 16)], axis=1))
        cw = np.asarray(ffn_conv_w[l], f32)
        cb = np.asarray(ffn_conv_b[l], f32)
        cwl = np.ascontiguousarray(np.concatenate([cw.reshape(3, 88, 128).transpose(2, 1, 0),
                                                   cb.reshape(88, 128).T[:, :, None]], axis=2))
        wup = np.asarray(ffn_w_up[l], f32)
        wdn = np.asarray(ffn_w_down[l], f32)
        ims = []
        for c in range(NCORE):
            ht_, ot_, gt_, vm_ = _post_inputs(h, o_full, g_fm, c)
            im = {"ht": ht_, "ot": ot_, "gt": gt_, "vmask": vm_, "wo": w_out, "lnp": lnp, "wup": wup, "cw": cwl, "wdn": wdn}
            if pname == "post_gla":
                im["ng"] = _fm(gla_norm_g[l], 4)
            ims.append(im)
        res = _run(pname, ims)
        h = np.concatenate([np.asarray(res[c]["hout"]).T for c in range(NCORE)], axis=0)
    return np.ascontiguousarray(h[N_META:].reshape(1, SEQ, D).astype(f32))
```

```python
import contextlib
import numpy as np
import ml_dtypes
import concourse.bass as bass
import concourse.mybir as mybir
from concourse.bass_utils import run_bass_kernel_spmd

F32 = mybir.dt.float32
BF16 = mybir.dt.bfloat16
AF = mybir.ActivationFunctionType
ALU = mybir.AluOpType
NPBF = ml_dtypes.bfloat16

D = 2048
SEQ = 8192
DEPTH = 4
N_META = 16
NTOK = SEQ + N_META
NCORE = 8
TC = NTOK // NCORE
HALO = 2
TP = TC + HALO
LPAD = 8320
NT = LPAD // 128
ALPHA = (2.0 * DEPTH) ** 0.25
LN_EPS = 1e-5
GLA_HEADS, GLA_HK, GLA_HV, GLA_RANK, GLA_TAU = 4, 256, 512, 16, 16.0
GLA_DK, GLA_DV = 1024, 2048
FOX_HEADS, FOX_HD = 16, 128
D_FF = 5632
NEG = -30000.0

ENGS = ("pe", "act", "dve", "pool", "sp")
NDS = 8


class Prog:
    def __init__(self, nc):
        self.nc = nc
        self.ops = {e: [] for e in ENGS}
        self.cnt = {e: 0 for e in ENGS}
        self.dcnt = {e: 0 for e in ENGS}
        self.lastw = {}
        self.readers = {}
        self.known = {e: {} for e in ENGS}
        self.dknown = {e: {q: set() for q in ENGS} for e in ENGS}
        self.dmax = {e: {q: 0 for q in ENGS} for e in ENGS}

    def _deps(self, eng, reads, writes):
        deps = {}
        ddeps = set()

        def need(ev):
            if ev is None:
                return
            kind, e, idx = ev
            if kind == "d":
                ddeps.add((e, idx))
                return
            if e == eng == "pe":
                return
            if deps.get(e, 0) < idx:
                deps[e] = idx

        for t in reads:
            need(self.lastw.get(t))
        for t in writes:
            need(self.lastw.get(t))
            for r in self.readers.get(t, ()):
                need(r)
        out = []
        kn = self.known[eng]
        for e, idx in deps.items():
            if kn.get(e, 0) >= idx:
                continue
            kn[e] = idx
            out.append(("c", e, idx))
        for (q, idx) in sorted(ddeps):
            if idx in self.dknown[eng][q] or idx <= self.dmax[eng][q] - NDS:
                continue
            self.dknown[eng][q].add(idx)
            if idx > self.dmax[eng][q]:
                self.dmax[eng][q] = idx
            out.append(("d", q, idx))
        return out

    def _commit(self, ev, reads, writes):
        for t in reads:
            self.readers.setdefault(t, []).append(ev)
        for t in writes:
            self.lastw[t] = ev
            self.readers[t] = []

    def op(self, eng, fn, reads=(), writes=(), excl=()):
        if excl:
            reads = list(reads) + list(excl)
            writes = list(writes) + list(excl)
        waits = self._deps(eng, reads, writes)
        self.cnt[eng] += 1
        self.ops[eng].append(("c", fn, waits, 0))
        self._commit(("c", eng, self.cnt[eng]), reads, writes)

    def dma(self, eng, fn, reads=(), writes=()):
        waits = self._deps(eng, reads, writes)
        self.dcnt[eng] += 1
        i = self.dcnt[eng]
        if i > NDS:
            prev = i - NDS
            if not (prev in self.dknown[eng][eng] or prev <= self.dmax[eng][eng] - NDS):
                self.dknown[eng][eng].add(prev)
                self.dmax[eng][eng] = max(self.dmax[eng][eng], prev)
                waits.append(("d", eng, prev))
        self.ops[eng].append(("d", fn, waits, i))
        self._commit(("d", eng, i), reads, writes)

    def finish(self, eng="sp"):
        waits = []
        for e in ENGS:
            for i in range(max(1, self.dcnt[e] - NDS + 1), self.dcnt[e] + 1):
                waits.append(("d", e, i))
            if self.cnt[e]:
                waits.append(("c", e, self.cnt[e]))
        self.ops[eng].append(("w", None, waits, 0))

    def emit(self):
        nc = self.nc
        with contextlib.ExitStack() as st:
            csem = {e: st.enter_context(nc.semaphore("c_" + e)) for e in ENGS}
            dsem = {e: [st.enter_context(nc.semaphore("d_%s%d" % (e, k))) for k in range(NDS)]
                    for e in ("sp", "act", "pool")}
            block = st.enter_context(nc.Block())

            def run(engname):
                def body(eh):
                    for kind, fn, waits, di in self.ops[engname]:
                        for (k, e, idx) in waits:
                            if k == "c":
                                eh.wait_ge(csem[e], idx)
                            else:
                                eh.wait_ge(dsem[e][(idx - 1) % NDS], 16 * ((idx - 1) // NDS + 1))
                        if kind == "c":
                            fn(eh).then_inc(csem[engname], 1)
                        elif kind == "d":
                            fn(eh).then_inc(dsem[engname][(di - 1) % NDS], 16)
                return body

            block.sync(run("sp"))
            block.scalar(run("act"))
            block.vector(run("dve"))
            block.gpsimd(run("pool"))
            block.tensor(run("pe"))


def ntiles(total, maxn=512):
    n = -(-total // maxn)
    base = -(-total // n)
    out = []
    s = 0
    while s < total:
        e = min(total, s + base)
        out.append((s, e))
        s = e
    return out


class Ctx:
    def __init__(self, name):
        self.nc = bass.Bass("TRN2", target_bir_lowering=False)
        self.P = Prog(self.nc)
        self.psbig = self.nc.alloc_psum_tensor("psbig", [128, 4096], F32)
        self.ps = [self.psbig[:, b * 512:(b + 1) * 512] for b in range(8)]
        self.uid = 0

    def sb(self, name, shape, dt):
        return self.nc.alloc_sbuf_tensor(name, list(shape), dt)

    def din(self, name, shape, dt):
        return self.nc.dram_tensor(name, list(shape), dt, kind="ExternalInput").ap()

    def dout(self, name, shape, dt):
        return self.nc.dram_tensor(name, list(shape), dt, kind="ExternalOutput").ap()


def build_pre(kind):
    C = Ctx("pre_" + kind)
    nc, P = C.nc, C.P
    T = TC
    ht = C.din("ht", [D, T], F32)
    if kind == "gla":
        dout_w = 2 * GLA_DK + GLA_DV + GLA_RANK + GLA_DV
        w = C.din("w", [D, dout_w], F32)
        wg2 = C.din("wg2", [GLA_RANK, GLA_DK], F32)
        bg2 = C.din("bg2", [128, GLA_DK // 128], F32)
        segs = [("g", 2 * GLA_DK + GLA_DV, 16, "glow", None),
                ("q", 0, GLA_DK, "scale", BF16), ("k", GLA_DK, GLA_DK, "copy", BF16),
                ("v", 2 * GLA_DK, GLA_DV, "copy", BF16),
                ("r", 2 * GLA_DK + GLA_DV + GLA_RANK, GLA_DV, "copy", F32)]
        qscale = GLA_HK ** -0.5
        outs = {"q": C.dout("q", [GLA_DK, T], BF16), "k": C.dout("k", [GLA_DK, T], BF16),
                "v": C.dout("v", [GLA_DV, T], BF16), "r": C.dout("r", [GLA_DV, T], F32),
                "la": C.dout("la", [GLA_DK, T], F32)}
    else:
        wq = C.din("w", [D, 2 * D], F32)
        segs = [("q", 0, D, "scale", BF16), ("og", D, D, "copy", F32)]
        qscale = FOX_HD ** -0.5
        outs = {"q": C.dout("q", [D, T], BF16), "og": C.dout("og", [D, T], F32)}
        if kind == "fox_kv":
            wkv = C.din("wkv", [D, 2 * D + FOX_HEADS], F32)
            bf = C.din("bf", [FOX_HEADS, 1], F32)
            outs.update({"kk": C.dout("kk", [D, T], BF16), "vv": C.dout("vv", [D, T], BF16),
                         "lf": C.dout("lf", [FOX_HEADS, T], F32)})

    hs = C.sb("hs", [128, 16, T], F32)
    hb = C.sb("hb", [128, 16, T], BF16)
    NW = 4
    wt = [C.sb("wt%d" % i, [128, 16, 128], BF16) for i in range(NW)]
    NS = 3
    stf = [C.sb("stf%d" % i, [128, T], F32) for i in range(NS)]
    stb = [C.sb("stb%d" % i, [128, T], BF16) for i in range(NS)]
    tiles = ntiles(T)

    for c in range(16):
        q = "sp" if c % 2 == 0 else "act"
        P.dma(q, lambda e, c=c: e.dma_start(out=hs[:, c, :], in_=ht[c * 128:(c + 1) * 128, :]),
              writes=[("hs", c)])
        eng = "dve" if c % 2 == 0 else "pool"
        P.op(eng, lambda e, c=c: e.tensor_copy(out=hb[:, c, :], in_=hs[:, c, :]),
             reads=[("hs", c)], writes=[("hb", c)])
    hb_tok = [("hb", c) for c in range(16)]

    state = {"wi": 0, "si": 0, "bank": 0}

    def proj_chunk(wsrc, col0, ncols, epilogue):
        wi = state["wi"] % NW
        state["wi"] += 1
        P.dma("pool", lambda e: e.dma_start(
            out=wt[wi][:, :, 0:ncols],
            in_=wsrc[:, col0:col0 + ncols].rearrange("(kc p) j -> p kc j", p=128)),
            writes=[("wt", wi)])
        for (t0, t1) in tiles:
            b = state["bank"] % 4
            state["bank"] += 1
            for kc in range(16):
                P.op("pe", lambda e, kc=kc, b=b, t0=t0, t1=t1: e.matmul(
                    C.ps[b][0:ncols, 0:t1 - t0], lhsT=wt[wi][:, kc, 0:ncols], rhs=hb[:, kc, t0:t1],
                    start=(kc == 0), stop=(kc == 15)),
                    reads=[("wt", wi), ("hb", kc)], excl=[("ps", b)])
            epilogue(b, t0, t1)

    if kind == "gla":
        gl = C.sb("gl", [GLA_RANK, T], BF16)
        wg2f = C.sb("wg2f", [GLA_RANK, GLA_DK], F32)
        wg2b = C.sb("wg2b", [GLA_RANK, GLA_DK], BF16)
        bg = C.sb("bg", [128, 8], F32)
        nbg = C.sb("nbg", [128, 8], F32)
        P.dma("sp", lambda e: e.dma_start(out=wg2f[:], in_=wg2), writes=["wg2f"])
        P.dma("sp", lambda e: e.dma_start(out=bg[:], in_=bg2), writes=["bg"])
        P.op("dve", lambda e: e.tensor_copy(out=wg2b[:], in_=wg2f[:]), reads=["wg2f"], writes=["wg2b"])
        P.op("dve", lambda e: e.tensor_scalar(out=nbg[:], in0=bg[:], scalar1=-1.0, scalar2=None, op0=ALU.mult),
             reads=["bg"], writes=["nbg"])

        def ep_glow(b, t0, t1):
            P.op("act", lambda e: e.activation(out=gl[:, t0:t1], in_=C.ps[b][0:GLA_RANK, 0:t1 - t0], func=AF.Copy),
                 excl=[("ps", b)], writes=[("gl", t0)])
        proj_chunk(w, 2 * GLA_DK + GLA_DV, GLA_RANK, ep_glow)
        for fc in range(8):
            si = state["si"] % NS
            state["si"] += 1
            for (t0, t1) in tiles:
                b = 4 + (state["bank"] % 2)
                state["bank"] += 1
                P.op("pe", lambda e, fc=fc, b=b, t0=t0, t1=t1: e.matmul(
                    C.ps[b][:, 0:t1 - t0], lhsT=wg2b[:, fc * 128:(fc + 1) * 128], rhs=gl[:, t0:t1],
                    start=True, stop=True), reads=["wg2b", ("gl", t0)], excl=[("ps", b)])
                P.op("act", lambda e, fc=fc, b=b, t0=t0, t1=t1, si=si: e.activation(
                    out=stf[si][:, t0:t1], in_=C.ps[b][:, 0:t1 - t0], func=AF.Exp,
                    bias=nbg[:, fc:fc + 1], scale=-1.0),
                    reads=["nbg"], excl=[("ps", b)], writes=[("stf", si)])
            P.op("act", lambda e, si=si: e.activation(out=stf[si][:], in_=stf[si][:], func=AF.Ln, bias=1.0, scale=1.0),
                 reads=[("stf", si)], writes=[("stf", si)])
            P.op("dve", lambda e, si=si: e.tensor_scalar(out=stf[si][:], in0=stf[si][:], scalar1=-1.0 / GLA_TAU,
                                                        scalar2=None, op0=ALU.mult),
                 reads=[("stf", si)], writes=[("stf", si)])
            P.dma("sp", lambda e, fc=fc, si=si: e.dma_start(out=outs["la"][fc * 128:(fc + 1) * 128, :], in_=stf[si][:]),
                  reads=[("stf", si)])

    def run_seg(wsrc, name, col0, ncols, mode, odt, oname=None):
        oname = oname or name
        for ch in range(ncols // 128):
            si = state["si"] % NS
            state["si"] += 1
            dst = stf[si] if odt == F32 else stb[si]
            tok = ("stf", si) if odt == F32 else ("stb", si)

            def ep(b, t0, t1, dst=dst, tok=tok):
                if mode == "scale":
                    P.op("act", lambda e: e.activation(out=dst[:, t0:t1], in_=C.ps[b][:, 0:t1 - t0], func=AF.Copy,
                                                       scale=float(qscale)),
                         excl=[("ps", b)], writes=[tok])
                else:
                    eng = "dve" if (t0 // 8) % 2 == 0 else "act"
                    if eng == "dve":
                        P.op("dve", lambda e: e.tensor_copy(out=dst[:, t0:t1], in_=C.ps[b][:, 0:t1 - t0]),
                             excl=[("ps", b)], writes=[tok])
                    else:
                        P.op("act", lambda e: e.activation(out=dst[:, t0:t1], in_=C.ps[b][:, 0:t1 - t0], func=AF.Copy),
                             excl=[("ps", b)], writes=[tok])
            proj_chunk(wsrc, col0 + ch * 128, 128, ep)
            P.dma("sp" if ch % 2 == 0 else "act",
                  lambda e, ch=ch, dst=dst: e.dma_start(out=outs[oname][ch * 128:(ch + 1) * 128, :], in_=dst[:]),
                  reads=[tok])

    if kind == "gla":
        for (name, col0, ncols, mode, odt) in segs[1:]:
            run_seg(w, name, col0, ncols, mode, odt)
    else:
        for (name, col0, ncols, mode, odt) in segs:
            run_seg(wq, name, col0, ncols, mode, odt)
        if kind == "fox_kv":
            run_seg(wkv, "kk", 0, D, "copy", BF16)
            run_seg(wkv, "vv", D, D, "copy", BF16)
            bfs = C.sb("bfs", [FOX_HEADS, 1], F32)
            nbf = C.sb("nbf", [FOX_HEADS, 1], F32)
            lfs = C.sb("lfs", [FOX_HEADS, T], F32)
            P.dma("sp", lambda e: e.dma_start(out=bfs[:], in_=bf), writes=["bfs"])
            P.op("dve", lambda e: e.tensor_scalar(out=nbf[:], in0=bfs[:], scalar1=-1.0, scalar2=None, op0=ALU.mult),
                 reads=["bfs"], writes=["nbf"])

            def ep_f(b, t0, t1):
                P.op("act", lambda e: e.activation(out=lfs[:, t0:t1], in_=C.ps[b][0:FOX_HEADS, 0:t1 - t0], func=AF.Exp,
                                                   bias=nbf[:, 0:1], scale=-1.0),
                     reads=["nbf"], excl=[("ps", b)], writes=["lfs"])
            proj_chunk(wkv, 2 * D, FOX_HEADS, ep_f)
            P.op("act", lambda e: e.activation(out=lfs[:], in_=lfs[:], func=AF.Ln, bias=1.0, scale=1.0),
                 reads=["lfs"], writes=["lfs"])
            P.op("dve", lambda e: e.tensor_scalar(out=lfs[:], in0=lfs[:], scalar1=-1.0, scalar2=None, op0=ALU.mult),
                 reads=["lfs"], writes=["lfs"])
            P.dma("sp", lambda e: e.dma_start(out=outs["lf"], in_=lfs[:]), reads=["lfs"])

    P.finish()
    P.emit()
    return nc


def build_post(kind):
    C = Ctx("post_" + kind)
    nc, P = C.nc, C.P
    ht = C.din("ht", [D, TP], F32)
    ot = C.din("ot", [D, TP], F32)
    gt = C.din("gt", [D, TP], F32)
    vm = C.din("vmask", [128, TP], F32)
    wo = C.din("wo", [D, D], F32)
    lnp = C.din("lnp", [128, 4, 16], F32)
    wup = C.din("wup", [D, 2 * D_FF], F32)
    cwd = C.din("cw", [128, 88, 4], F32)
    wdn = C.din("wdn", [D_FF, D], F32)
    hout = C.dout("hout", [D, TC], F32)
    if kind == "gla":
        ngd = C.din("ng", [128, 4], F32)

    TW = 343
    TPX = 3 * TW
    hs = C.sb("hs", [128, 16, TP], F32)
    hb = C.sb("hb", [128, 16, TPX + 3], BF16)
    tA = [C.sb("tA%d" % i, [128, TPX + 3], F32) for i in range(2)]
    tB = [C.sb("tB%d" % i, [128, TPX + 3], F32) for i in range(2)]
    tC = [C.sb("tC%d" % i, [128, TPX + 3], F32) for i in range(2)]
    NW = 4
    wt = [C.sb("wt%d" % i, [128, 16, 128], BF16) for i in range(NW)]
    GP = 11
    NG = 4
    act = C.sb("act", [128, GP, TC], BF16)
    NWD = 3
    wd = [C.sb("wd%d" % i, [128, GP, 128], BF16) for i in range(NWD)]
    st = [C.sb("st%d" % i, [128, TP], F32) for i in range(3)]
    ones = C.sb("ones", [128, 128], F32)
    lns = C.sb("lns", [128, 4, 16], F32)
    cws = C.sb("cws", [128, 88, 4], F32)
    vms = C.sb("vms", [128, TP], F32)
    tiles_p = ntiles(TP)
    tiles_c = ntiles(TC)
    cnt = {"w": 0, "wd": 0, "a": 0, "b": 0, "c": 0, "dq": 0}

    def dq():
        cnt["dq"] += 1
        return "sp" if cnt["dq"] % 2 else "act"

    P.op("pool", lambda e: e.memset(ones[:], 1.0), writes=["ones"])
    P.op("pool", lambda e: e.memset(hb[:], 0.0), writes=[("hb", c) for c in range(16)])
    P.dma("sp", lambda e: e.dma_start(out=lns[:], in_=lnp), writes=["lns"])
    P.dma("sp", lambda e: e.dma_start(out=cws[:], in_=cwd), writes=["cws"])
    P.dma("sp", lambda e: e.dma_start(out=vms[:], in_=vm), writes=["vms"])
    if kind == "gla":
        ngs = C.sb("ngs", [128, 4], F32)
        P.dma("sp", lambda e: e.dma_start(out=ngs[:], in_=ngd), writes=["ngs"])
    for c in range(16):
        P.dma(dq(), lambda e, c=c: e.dma_start(out=hs[:, c, :], in_=ht[c * 128:(c + 1) * 128, :]),
              writes=[("hs", c)])

    def load(buf, tokname, src, c):
        i = cnt[tokname] % 2
        cnt[tokname] += 1
        P.dma(dq(), lambda e: e.dma_start(out=buf[i][:, 0:TP], in_=src[c * 128:(c + 1) * 128, :]),
              writes=[(tokname, i)])
        return i

    def rstd_from(src_banks, tiles, scale, dst, dst_tok):
        for ti, (t0, t1) in enumerate(tiles):
            b = src_banks[ti]
            P.op("dve", lambda e, b=b, t0=t0, t1=t1: e.tensor_scalar(
                out=dst[:, t0:t1], in0=C.ps[b][:, 0:t1 - t0], scalar1=float(scale), scalar2=float(LN_EPS),
                op0=ALU.mult, op1=ALU.add), excl=[("ps", b)], writes=[dst_tok])
        P.op("act", lambda e: e.activation(out=dst[:], in_=dst[:], func=AF.Sqrt), reads=[dst_tok],
             writes=[dst_tok])
        P.op("dve", lambda e: e.reciprocal(out=dst[:], in_=dst[:]), reads=[dst_tok],
             writes=[dst_tok])

    if kind == "gla":
        for hd in range(4):
            for cc in range(4):
                c = hd * 4 + cc
                ia = load(tA, "a", ot, c)
                ic = cnt["c"] % 2
                cnt["c"] += 1
                P.op("act", lambda e, ia=ia, ic=ic: e.activation(out=tC[ic][:, 0:TP], in_=tA[ia][:, 0:TP], func=AF.Square),
                     reads=[("a", ia)], writes=[("c", ic)])
                for ti, (t0, t1) in enumerate(tiles_p):
                    P.op("pe", lambda e, ic=ic, ti=ti, t0=t0, t1=t1, cc=cc: e.matmul(
                        C.ps[ti][:, 0:t1 - t0], lhsT=ones[:], rhs=tC[ic][:, t0:t1], start=(cc == 0), stop=(cc == 3)),
                        reads=["ones", ("c", ic)], excl=[("ps", ti)])
            rstd_from([0, 1, 2], tiles_p, 1.0 / GLA_HV, st[0], ("st", 0))
            for cc in range(4):
                c = hd * 4 + cc
                ia = load(tA, "a", ot, c)
                ib = load(tB, "b", gt, c)
                P.op("act", lambda e, ib=ib: e.activation(out=tB[ib][:, 0:TP], in_=tB[ib][:, 0:TP], func=AF.Silu),
                     reads=[("b", ib)], writes=[("b", ib)])
                P.op("dve", lambda e, ia=ia: e.tensor_tensor(out=tA[ia][:, 0:TP], in0=tA[ia][:, 0:TP], in1=st[0][:], op=ALU.mult),
                     reads=[("a", ia), ("st", 0)], writes=[("a", ia)])
                P.op("act", lambda e, ib=ib, cc=cc: e.activation(
                    out=tB[ib][:, 0:TP], in_=tB[ib][:, 0:TP], func=AF.Copy, scale=ngs[:, cc:cc + 1]),
                    reads=[("b", ib), "ngs"], writes=[("b", ib)])
                P.op("pool", lambda e, ia=ia, ib=ib, c=c: e.tensor_tensor(
                    out=hb[:, c, 0:TP], in0=tA[ia][:, 0:TP], in1=tB[ib][:, 0:TP], op=ALU.mult),
                    reads=[("a", ia), ("b", ib)], writes=[("hb", c)])
    else:
        for c in range(16):
            ia = load(tA, "a", ot, c)
            ib = load(tB, "b", gt, c)
            P.op("act", lambda e, ib=ib: e.activation(out=tB[ib][:, 0:TP], in_=tB[ib][:, 0:TP], func=AF.Sigmoid),
                 reads=[("b", ib)], writes=[("b", ib)])
            P.op("dve" if c % 2 == 0 else "pool",
                 lambda e, ia=ia, ib=ib, c=c: e.tensor_tensor(out=hb[:, c, 0:TP], in0=tA[ia][:, 0:TP], in1=tB[ib][:, 0:TP], op=ALU.mult),
                 reads=[("a", ia), ("b", ib)], writes=[("hb", c)])

    def wload(src_ap):
        wi = cnt["w"] % NW
        cnt["w"] += 1
        P.dma("pool", lambda e: e.dma_start(out=wt[wi][:], in_=src_ap.rearrange("(kc p) j -> p kc j", p=128)),
              writes=[("wt", wi)])
        return wi

    obank = [6, 7]
    ob = {"i": 0}
    wow = {}

    def ensure_wo(fc):
        if fc in wow or fc >= 16:
            return
        wow[fc] = wload(wo[:, fc * 128:(fc + 1) * 128])

    for fc in range(16):
        for k in range(3):
            ensure_wo(fc + k)
        wi = wow[fc]
        for (t0, t1) in tiles_p:
            b = obank[ob["i"] % 2]
            ob["i"] += 1
            for kc in range(16):
                P.op("pe", lambda e, kc=kc, b=b, t0=t0, t1=t1, wi=wi: e.matmul(
                    C.ps[b][:, 0:t1 - t0], lhsT=wt[wi][:, kc, :], rhs=hb[:, kc, t0:t1], start=(kc == 0), stop=(kc == 15)),
                    reads=[("wt", wi), ("hb", kc)], excl=[("ps", b)])
            P.op("dve", lambda e, fc=fc, b=b, t0=t0, t1=t1: e.scalar_tensor_tensor(
                out=hs[:, fc, t0:t1], in0=hs[:, fc, t0:t1], scalar=float(ALPHA), in1=C.ps[b][:, 0:t1 - t0],
                op0=ALU.mult, op1=ALU.add), reads=[("hs", fc)], writes=[("hs", fc)], excl=[("ps", b)])

    def layer_norm(lo, gi, bi, make_hb):
        n = TP - lo
        tl = [(lo + a, lo + b_) for (a, b_) in ntiles(n)]
        for c in range(16):
            ic = cnt["c"] % 2
            cnt["c"] += 1
            P.op("act", lambda e, c=c, ic=ic: e.activation(out=tC[ic][:, lo:TP], in_=hs[:, c, lo:TP], func=AF.Square),
                 reads=[("hs", c)], writes=[("c", ic)])
            for ti, (t0, t1) in enumerate(tl):
                P.op("pe", lambda e, c=c, ti=ti, t0=t0, t1=t1: e.matmul(
                    C.ps[ti][:, 0:t1 - t0], lhsT=ones[:], rhs=hs[:, c, t0:t1], start=(c == 0), stop=(c == 15)),
                    reads=["ones", ("hs", c)], excl=[("ps", ti)])
                P.op("pe", lambda e, c=c, ic=ic, ti=ti, t0=t0, t1=t1: e.matmul(
                    C.ps[3 + ti][:, 0:t1 - t0], lhsT=ones[:], rhs=tC[ic][:, t0:t1], start=(c == 0), stop=(c == 15)),
                    reads=["ones", ("c", ic)], excl=[("ps", 3 + ti)])
        mean, var, tmp = st[0], st[1], st[2]
        for ti, (t0, t1) in enumerate(tl):
            P.op("dve", lambda e, ti=ti, t0=t0, t1=t1: e.tensor_scalar(
                out=mean[:, t0:t1], in0=C.ps[ti][:, 0:t1 - t0], scalar1=1.0 / D, scalar2=None, op0=ALU.mult),
                excl=[("ps", ti)], writes=[("st", 0)])
            P.op("dve", lambda e, ti=ti, t0=t0, t1=t1: e.tensor_scalar(
                out=var[:, t0:t1], in0=C.ps[3 + ti][:, 0:t1 - t0], scalar1=1.0 / D, scalar2=float(LN_EPS),
                op0=ALU.mult, op1=ALU.add), excl=[("ps", 3 + ti)], writes=[("st", 1)])
        P.op("dve", lambda e: e.tensor_tensor(out=tmp[:, lo:TP], in0=mean[:, lo:TP], in1=mean[:, lo:TP], op=ALU.mult),
             reads=[("st", 0)], writes=[("st", 2)])
        P.op("dve", lambda e: e.tensor_tensor(out=var[:, lo:TP], in0=var[:, lo:TP], in1=tmp[:, lo:TP], op=ALU.subtract),
             reads=[("st", 1), ("st", 2)], writes=[("st", 1)])
        P.op("act", lambda e: e.activation(out=var[:, lo:TP], in_=var[:, lo:TP], func=AF.Sqrt), reads=[("st", 1)], writes=[("st", 1)])
        P.op("dve", lambda e: e.reciprocal(out=var[:, lo:TP], in_=var[:, lo:TP]), reads=[("st", 1)], writes=[("st", 1)])
        for c in range(16):
            ia = cnt["a"] % 2
            cnt["a"] += 1
            P.op("dve", lambda e, c=c, ia=ia: e.tensor_tensor(out=tA[ia][:, lo:TP], in0=hs[:, c, lo:TP], in1=mean[:, lo:TP],
                                                             op=ALU.subtract),
                 reads=[("hs", c), ("st", 0)], writes=[("a", ia)])
            P.op("pool", lambda e, ia=ia: e.tensor_tensor(out=tA[ia][:, lo:TP], in0=tA[ia][:, lo:TP], in1=var[:, lo:TP],
                                                          op=ALU.mult),
                 reads=[("a", ia), ("st", 1)], writes=[("a", ia)])
            P.op("dve", lambda e, c=c, ia=ia: e.tensor_scalar(
                out=hs[:, c, lo:TP], in0=tA[ia][:, lo:TP], scalar1=lns[:, gi, c:c + 1], scalar2=lns[:, bi, c:c + 1],
                op0=ALU.mult, op1=ALU.add), reads=[("a", ia), "lns"], writes=[("hs", c)])
            if make_hb:
                P.op("pool", lambda e, c=c: e.tensor_tensor(out=hb[:, c, 0:TP], in0=hs[:, c, :], in1=vms[:], op=ALU.mult),
                     reads=[("hs", c), "vms"], writes=[("hb", c)])

    layer_norm(0, 0, 1, True)

    xu, xg, cu, cg = tA, tB, tC[0], tC[1]
    upw = {}
    dnw = {}

    def ensure_up(j):
        if j in upw or j >= NG * GP:
            return
        upw[j] = (wload(wup[:, j * 128:(j + 1) * 128]), wload(wup[:, D_FF + j * 128:D_FF + (j + 1) * 128]))

    def ensure_down(g, fc):
        if (g, fc) in dnw or fc >= 16:
            return
        wi = cnt["wd"] % NWD
        cnt["wd"] += 1
        P.dma("pool", lambda e: e.dma_start(
            out=wd[wi][:], in_=wdn[g * GP * 128:(g + 1) * GP * 128, fc * 128:(fc + 1) * 128].rearrange(
                "(jj p) m -> p jj m", p=128)), writes=[("wd", wi)])
        dnw[(g, fc)] = wi
    for g in range(NG):
        for jj in range(GP):
            j = g * GP + jj
            ensure_up(j)
            ensure_up(j + 1)
            if jj == GP - 1:
                ensure_down(g, 0)
                ensure_down(g, 1)
            wa, wb = upw[j]
            for (wi, b0, X, xt_, col) in ((wa, 0, (tA[0], tA[1], tC[0]), ("a", 0), j), (wb, 3, (tB[0], tB[1], tC[1]), ("b", 0), 44 + j)):
                for ti in range(3):
                    for kc in range(16):
                        P.op("pe", lambda e, kc=kc, wi=wi, b0=b0, ti=ti: e.matmul(
                            C.ps[b0 + ti][:, 0:TW], lhsT=wt[wi][:, kc, :], rhs=hb[:, kc, ti * TW:(ti + 1) * TW],
                            start=(kc == 0), stop=(kc == 15)),
                            reads=[("wt", wi), ("hb", kc)], excl=[("ps", b0 + ti)])
                src = C.psbig[:, b0 * 512:(b0 + 3) * 512].rearrange("p (b c) -> p b c", c=512)[:, :, 0:TW]
                banks = [("ps", b0), ("ps", b0 + 1), ("ps", b0 + 2)]
                for tap, buf, tok in ((2, X[0], (xt_[0], 0)), (1, X[1], (xt_[0], 1)), (0, X[2], ("c", 0 if b0 == 0 else 1))):
                    P.op("act", lambda e, tap=tap, buf=buf, src=src, col=col: e.activation(
                        out=buf[:, 0:TPX].rearrange("p (b c) -> p b c", c=TW), in_=src, func=AF.Identity,
                        scale=cws[:, col, tap:tap + 1], bias=(cws[:, col, 3:4] if tap == 2 else 0.0)),
                        reads=["cws"], excl=banks, writes=[tok])
                t2, t1_, t0_ = (xt_[0], 0), (xt_[0], 1), ("c", 0 if b0 == 0 else 1)
                P.op("dve", lambda e, X=X: e.tensor_tensor(out=X[0][:, 2:TP], in0=X[0][:, 2:TP], in1=X[1][:, 1:TP - 1], op=ALU.add),
                     reads=[t2, t1_], writes=[t2])
                P.op("dve", lambda e, X=X: e.tensor_tensor(out=X[0][:, 2:TP], in0=X[0][:, 2:TP], in1=X[2][:, 0:TC], op=ALU.add),
                     reads=[t2, t0_], writes=[t2])
            P.op("act", lambda e: e.activation(out=tB[0][:, 2:TP], in_=tB[0][:, 2:TP], func=AF.Silu), reads=[("b", 0)], writes=[("b", 0)])
            P.op("dve", lambda e, jj=jj: e.tensor_tensor(out=act[:, jj, :], in0=tA[0][:, 2:TP], in1=tB[0][:, 2:TP], op=ALU.mult),
                 reads=[("a", 0), ("b", 0)], writes=[("act", jj)])
        for fc in range(16):
            ensure_down(g, fc)
            ensure_down(g, fc + 1)
            ensure_down(g, fc + 2)
            wi = dnw[(g, fc)]
            for (t0, t1) in tiles_c:
                b = obank[ob["i"] % 2]
                ob["i"] += 1
                for jj in range(GP):
                    P.op("pe", lambda e, jj=jj, wi=wi, b=b, t0=t0, t1=t1: e.matmul(
                        C.ps[b][:, 0:t1 - t0], lhsT=wd[wi][:, jj, :], rhs=act[:, jj, t0:t1],
                        start=(jj == 0), stop=(jj == GP - 1)),
                        reads=[("wd", wi), ("act", jj)], excl=[("ps", b)])
                P.op("dve", lambda e, fc=fc, b=b, t0=t0, t1=t1, g=g: e.scalar_tensor_tensor(
                    out=hs[:, fc, 2 + t0:2 + t1], in0=hs[:, fc, 2 + t0:2 + t1], scalar=float(ALPHA if g == 0 else 1.0),
                    in1=C.ps[b][:, 0:t1 - t0], op0=ALU.mult, op1=ALU.add),
                    reads=[("hs", fc)], writes=[("hs", fc)], excl=[("ps", b)])

    layer_norm(2, 2, 3, False)
    for c in range(16):
        P.dma(dq(), lambda e, c=c: e.dma_start(out=hout[c * 128:(c + 1) * 128, :], in_=hs[:, c, 2:TP]),
              reads=[("hs", c)])
    P.finish()
    P.emit()
    return nc


def build_scan(debug=False):
    C = Ctx("scan")
    nc, P = C.nc, C.P
    L = LPAD
    qf = C.din("qf", [256, L], BF16)
    kf = C.din("kf", [256, L], BF16)
    ktm = C.din("ktm", [L, 256], BF16)
    vtm = C.din("vtm", [L, 256], BF16)
    latm = C.din("latm", [L, 256], F32)
    cmd = C.din("cm", [128, 2, 128], F32)
    od = C.dout("o", [L, 256], F32)
    GT = 13
    NGRP = NT // GT
    GL = GT * 128
    qin = [C.sb("qin%d" % i, [128, 2, GL], BF16) for i in range(2)]
    kin = [C.sb("kin%d" % i, [128, 2, GL], BF16) for i in range(2)]
    ktin = [C.sb("ktin%d" % i, [128, GT, 256], BF16) for i in range(2)]
    vtin = [C.sb("vtin%d" % i, [128, GT, 256], BF16) for i in range(2)]
    lain = [C.sb("lain%d" % i, [128, GT, 256], F32) for i in range(2)]
    cm = C.sb("cm_sb", [128, 2, 128], F32)
    eb = [C.sb("eb%d" % i, [128, 2, 128], F32) for i in range(2)]
    enb = [C.sb("enb%d" % i, [128, 2, 128], F32) for i in range(2)]
    qg = [C.sb("qg%d" % i, [128, 2, 128], BF16) for i in range(2)]
    kg = [C.sb("kg%d" % i, [128, 2, 128], BF16) for i in range(2)]
    ekd = [C.sb("ekd%d" % i, [128, 256], F32) for i in range(2)]
    kd = [C.sb("kd%d" % i, [128, 256], BF16) for i in range(2)]
    attm = [C.sb("attm%d" % i, [128, 128], BF16) for i in range(2)]
    osb = [C.sb("osb%d" % i, [64, 2, 256], F32) for i in range(2)]
    S = C.sb("S_sb", [128, 2, 256], F32)
    Sb = [C.sb("Sb_sb%d" % i, [128, 2, 256], BF16) for i in range(2)]
    ps = C.ps

    P.dma("sp", lambda e: e.dma_start(out=cm[:], in_=cmd), writes=["cm"])
    P.op("dve", lambda e: e.memset(S[:], 0.0), writes=[("S", 0), ("S", 1)])
    for i in range(2):
        P.op("pool", lambda e, i=i: e.memset(Sb[i][:], 0.0), writes=[("Sb", i, 0), ("Sb", i, 1)])

    def load_group(g):
        i = g % 2
        c0 = g * GL
        P.dma("sp", lambda e: e.dma_start(out=qin[i][:], in_=qf[:, c0:c0 + GL].rearrange("(kc p) t -> p kc t", p=128)),
              writes=[("qin", i)])
        P.dma("act", lambda e: e.dma_start(out=kin[i][:], in_=kf[:, c0:c0 + GL].rearrange("(kc p) t -> p kc t", p=128)),
              writes=[("kin", i)])
        P.dma("sp", lambda e: e.dma_start(out=ktin[i][:], in_=ktm[c0:c0 + GL, :].rearrange("(t p) k -> p t k", p=128)),
              writes=[("ktin", i)])
        P.dma("act", lambda e: e.dma_start(out=vtin[i][:], in_=vtm[c0:c0 + GL, :].rearrange("(t p) k -> p t k", p=128)),
              writes=[("vtin", i)])
        P.dma("sp", lambda e: e.dma_start(out=lain[i][:], in_=latm[c0:c0 + GL, :].rearrange("(t p) k -> p t k", p=128)),
              writes=[("lain", i)])

    def precompute(tt):
        g, tl = divmod(tt, GT)
        gi = g % 2
        pr = tt % 2
        bb = 0 if pr == 0 else 6
        b2 = 1 if pr == 0 else 7
        cs = slice(tl * 128, (tl + 1) * 128)
        for kc in range(2):
            P.op("pe", lambda e, kc=kc: e.matmul(ps[bb][:, kc * 128:(kc + 1) * 128], lhsT=lain[gi][:, tl, kc * 128:(kc + 1) * 128],
                                                rhs=cm[:, 0, :], start=True, stop=True),
                 reads=[("lain", gi), "cm"], excl=[("ps", bb)])
        P.op("act", lambda e: e.activation(out=eb[pr][:].rearrange("p a b -> p (a b)"), in_=ps[bb][:, 0:256], func=AF.Exp),
             excl=[("ps", bb)], writes=[("eb", pr)])
        P.op("act", lambda e: e.activation(out=enb[pr][:].rearrange("p a b -> p (a b)"), in_=ps[bb][:, 0:256], func=AF.Exp, scale=-1.0),
             excl=[("ps", bb)], writes=[("enb", pr)])
        P.op("dve", lambda e: e.tensor_tensor(out=qg[pr][:], in0=qin[gi][:, :, cs], in1=eb[pr][:], op=ALU.mult),
             reads=[("qin", gi), ("eb", pr)], writes=[("qg", pr)])
        P.op("pool", lambda e: e.tensor_tensor(out=kg[pr][:], in0=kin[gi][:, :, cs], in1=enb[pr][:], op=ALU.mult),
             reads=[("kin", gi), ("enb", pr)], writes=[("kg", pr)])
        P.op("pe", lambda e: e.matmul(ps[b2][:, 0:256], lhsT=cm[:, 1, :], rhs=lain[gi][:, tl, :], start=True, stop=True),
             reads=[("lain", gi), "cm"], excl=[("ps", b2)])
        P.op("act", lambda e: e.activation(out=ekd[pr][:], in_=ps[b2][:, 0:256], func=AF.Exp),
             excl=[("ps", b2)], writes=[("ekd", pr)])
        P.op("pool", lambda e: e.tensor_tensor(out=kd[pr][:], in0=ktin[gi][:, tl, :], in1=ekd[pr][:], op=ALU.mult),
             reads=[("ktin", gi), ("ekd", pr)], writes=[("kd", pr)])
        for kc in range(2):
            P.op("pe", lambda e, kc=kc: e.matmul(ps[b2][:, 256:384], lhsT=kg[pr][:, kc, :], rhs=qg[pr][:, kc, :],
                                                start=(kc == 0), stop=(kc == 1)),
                 reads=[("kg", pr), ("qg", pr)], excl=[("ps", b2)])
        P.op("dve", lambda e: e.tensor_tensor(out=attm[pr][:], in0=ps[b2][:, 256:384], in1=cm[:, 0, :], op=ALU.mult),
             reads=["cm"], excl=[("ps", b2)], writes=[("attm", pr)])

    def chain(tt):
        g, tl = divmod(tt, GT)
        gi = g % 2
        pr = tt % 2
        for s in range(2):
            rs = slice(s * 64, (s + 1) * 64)
            ob = 3 + s
            ci = 2 * tt + s
            cur, nxt = ci % 2, (ci + 1) % 2
            ub = 5 if ci % 2 == 0 else 2
            for kc in range(2):
                P.op("pe", lambda e, kc=kc, rs=rs, ub=ub: e.matmul(
                    ps[ub][:, kc * 256:(kc + 1) * 256], lhsT=kd[pr][rs, kc * 128:(kc + 1) * 128],
                    rhs=vtin[gi][rs, tl, :], start=True, stop=True),
                    reads=[("kd", pr), ("vtin", gi)], excl=[("ps", ub)])
            P.op("pe", lambda e, rs=rs, ob=ob: e.matmul(ps[ob][0:64, 0:256], lhsT=attm[pr][rs, rs], rhs=vtin[gi][rs, tl, :],
                                                       start=True, stop=False),
                 reads=[("attm", pr), ("vtin", gi)], excl=[("ps", ob)])
            for kc in range(2):
                P.op("pe", lambda e, kc=kc, rs=rs, ob=ob, cur=cur: e.matmul(
                    ps[ob][0:64, 0:256], lhsT=qg[pr][:, kc, rs], rhs=Sb[cur][:, kc, :], start=False, stop=(kc == 1)),
                    reads=[("qg", pr), ("Sb", cur, kc)], excl=[("ps", ob)])
            P.op("act", lambda e, s=s, ob=ob: e.activation(out=osb[pr][:, s, :], in_=ps[ob][0:64, 0:256], func=AF.Copy),
                 excl=[("ps", ob)], writes=[("osb", pr)])
            for kc in range(2):
                P.op("dve", lambda e, kc=kc, s=s, ub=ub, nxt=nxt: e.scalar_tensor_tensor(
                    out=Sb[nxt][:, kc, :], in0=S[:, kc, :], scalar=eb[pr][:, kc, s * 64 + 63:s * 64 + 64],
                    in1=ps[ub][:, kc * 256:(kc + 1) * 256], op0=ALU.mult, op1=ALU.add),
                    reads=[("S", kc), ("eb", pr)], writes=[("Sb", nxt, kc)], excl=[("ps", ub)])
            for kc in range(2):
                P.op("dve", lambda e, kc=kc, s=s, ub=ub: e.scalar_tensor_tensor(
                    out=S[:, kc, :], in0=S[:, kc, :], scalar=eb[pr][:, kc, s * 64 + 63:s * 64 + 64],
                    in1=ps[ub][:, kc * 256:(kc + 1) * 256], op0=ALU.mult, op1=ALU.add),
                    reads=[("S", kc), ("eb", pr)], writes=[("S", kc)], excl=[("ps", ub)])
        P.dma("sp" if tt % 2 == 0 else "act",
              lambda e: e.dma_start(out=od[tt * 128:(tt + 1) * 128, :].rearrange("(s i) v -> i s v", s=2), in_=osb[pr][:]),
              reads=[("osb", pr)])

    load_group(0)
    precompute(0)
    if debug:
        dbg = {n: C.dout("dbg_" + n, shp, dt) for n, shp, dt in (("eb", [128, 256], F32), ("qg", [128, 256], BF16),
               ("kg", [128, 256], BF16), ("kd", [128, 256], BF16), ("attm", [128, 128], BF16), ("ekd", [128, 256], F32))}
        P.dma("sp", lambda e: e.dma_start(out=dbg["eb"], in_=eb[0][:].rearrange("p a b -> p (a b)")), reads=[("eb", 0)])
        P.dma("sp", lambda e: e.dma_start(out=dbg["qg"], in_=qg[0][:].rearrange("p a b -> p (a b)")), reads=[("qg", 0)])
        P.dma("sp", lambda e: e.dma_start(out=dbg["kg"], in_=kg[0][:].rearrange("p a b -> p (a b)")), reads=[("kg", 0)])
        P.dma("sp", lambda e: e.dma_start(out=dbg["kd"], in_=kd[0][:]), reads=[("kd", 0)])
        P.dma("sp", lambda e: e.dma_start(out=dbg["attm"], in_=attm[0][:]), reads=[("attm", 0)])
        P.dma("sp", lambda e: e.dma_start(out=dbg["ekd"], in_=ekd[0][:]), reads=[("ekd", 0)])
    for tt in range(NT if not debug else 2):
        g, tl = divmod(tt, GT)
        if tl == 0 and g + 1 < NGRP:
            load_group(g + 1)
        if tt + 1 < NT:
            precompute(tt + 1)
        chain(tt)
    P.finish()
    P.emit()
    return nc


HPC = FOX_HEADS // NCORE


def build_attn():
    C = Ctx("attn")
    nc, P = C.nc, C.P
    L = LPAD
    qf = C.din("qf", [HPC, 128, L], BF16)
    kf = C.din("kf", [HPC, 128, L], BF16)
    vtm = C.din("vtm", [HPC, L, 128], BF16)
    lft = C.din("lft", [HPC, 128, NT], F32)
    cmd = C.din("cm", [128, 4, 128], F32)
    lmd = C.din("lm", [NT, NT], F32)
    pmd = C.din("pm", [3, 3], F32)
    od = C.dout("o", [HPC, L, 128], F32)
    scr = [nc.dram_tensor("cscr%d" % h, [NT, 128], F32, kind="Internal").ap() for h in range(HPC)]
    ps = C.ps

    cm = C.sb("cm_sb", [128, 4, 128], F32)
    lm = C.sb("lm_sb", [NT, NT], F32)
    pm = C.sb("pm_sb", [3, 3], F32)
    ibf = C.sb("ibf", [128, 128], BF16)
    mneg = C.sb("mneg", [128, 128], BF16)
    ones3 = C.sb("ones3", [3, 128], BF16)
    qh = C.sb("qh", [128, L], BF16)
    kh = C.sb("kh", [128, L], BF16)
    vaug = C.sb("vaug", [128, NT, 129], BF16)
    lfs = C.sb("lfs", [128, NT], F32)
    asb = C.sb("asb", [NT, 128], F32)
    csb = C.sb("csb", [128, NT], F32)
    negc = C.sb("negc", [128, NT], F32)
    ctsb = C.sb("ctsb", [NT, 128], F32)
    cqf = C.sb("cqf", [3, L], F32)
    cr1 = C.sb("cr1", [3, L], F32)
    chi = C.sb("chi", [3, L], BF16)
    cq3 = C.sb("cq3", [3, L], BF16)
    pt = [C.sb("pt%d" % i, [128, 512], BF16) for i in range(3)]
    osb = [C.sb("osb%d" % i, [128, 4, 128], F32) for i in range(2)]
    rec = [C.sb("rec%d" % i, [128, 4], F32) for i in range(2)]

    P.dma("sp", lambda e: e.dma_start(out=cm[:], in_=cmd), writes=["cm"])
    P.dma("sp", lambda e: e.dma_start(out=lm[:], in_=lmd), writes=["lm"])
    P.dma("sp", lambda e: e.dma_start(out=pm[:], in_=pmd), writes=["pm"])
    P.op("dve", lambda e: e.tensor_copy(out=ibf[:], in_=cm[:, 2, :]), reads=["cm"], writes=["ibf"])
    P.op("dve", lambda e: e.tensor_copy(out=mneg[:], in_=cm[:, 3, :]), reads=["cm"], writes=["mneg"])
    P.op("pool", lambda e: e.memset(ones3[:], 1.0), writes=["ones3"])

    cnt = {"sb": 0, "pb": 0, "ob": 0}
    for h in range(HPC):
        P.dma("sp", lambda e, h=h: e.dma_start(out=qh[:], in_=qf[h]), writes=["qh"])
        P.dma("act", lambda e, h=h: e.dma_start(out=kh[:], in_=kf[h]), writes=["kh"])
        P.op("pool", lambda e: e.memset(vaug[:], 1.0), writes=["vaug"])
        P.dma("sp", lambda e, h=h: e.dma_start(out=vaug[:, :, 0:128], in_=vtm[h].rearrange("(t p) d -> p t d", p=128)),
              writes=["vaug"])
        P.dma("act", lambda e, h=h: e.dma_start(out=lfs[:], in_=lft[h]), writes=["lfs"])
        P.op("pe", lambda e: e.matmul(ps[6][0:NT, 0:128], lhsT=lfs[:], rhs=cm[:, 1, :], start=True, stop=True),
             reads=["lfs", "cm"], excl=[("ps", 6)])
        P.op("act", lambda e: e.activation(out=asb[:], in_=ps[6][0:NT, 0:128], func=AF.Copy), excl=[("ps", 6)], writes=["asb"])
        P.op("pe", lambda e: e.matmul(ps[7][:, 0:NT], lhsT=cm[:, 0, :], rhs=lfs[:], start=True, stop=False),
             reads=["lfs", "cm"], excl=[("ps", 7)])
        P.op("pe", lambda e: e.matmul(ps[7][:, 0:NT], lhsT=asb[:], rhs=lm[:], start=False, stop=True),
             reads=["asb", "lm"], excl=[("ps", 7)])
        P.op("dve", lambda e: e.tensor_copy(out=csb[:], in_=ps[7][:, 0:NT]), excl=[("ps", 7)], writes=["csb"])
        P.op("dve", lambda e: e.tensor_scalar(out=negc[:], in0=csb[:], scalar1=-1.0, scalar2=None, op0=ALU.mult),
             reads=["csb"], writes=["negc"])
        P.op("pe", lambda e: e.transpose(ps[6][0:NT, 0:128], csb[:], cm[:, 2, :]), reads=["csb", "cm"], excl=[("ps", 6)])
        P.op("act", lambda e: e.activation(out=ctsb[:], in_=ps[6][0:NT, 0:128], func=AF.Copy), excl=[("ps", 6)], writes=["ctsb"])
        P.dma("sp", lambda e, h=h: e.dma_start(out=scr[h], in_=ctsb[:]), reads=["ctsb"], writes=[("scr", h)])
        for r in range(3):
            P.dma("sp" if r != 1 else "act",
                  lambda e, h=h, r=r: e.dma_start(out=cqf[r:r + 1, :], in_=scr[h].rearrange("(o t) j -> o (t j)", o=1)),
                  reads=[("scr", h)], writes=["cqf"])
        P.op("dve", lambda e: e.tensor_copy(out=chi[:], in_=cqf[:]), reads=["cqf"], writes=["chi"])
        P.op("dve", lambda e: e.tensor_scalar(out=cq3[:], in0=chi[:], scalar1=pm[:, 0:1], scalar2=None, op0=ALU.mult),
             reads=["chi", "pm"], writes=["cq3"])
        P.op("dve", lambda e: e.tensor_tensor(out=cr1[:], in0=cqf[:], in1=chi[:], op=ALU.subtract),
             reads=["cqf", "chi"], writes=["cr1"])
        P.op("dve", lambda e: e.tensor_copy(out=chi[:], in_=cr1[:]), reads=["cr1"], writes=["chi"])
        P.op("dve", lambda e: e.scalar_tensor_tensor(out=cq3[:], in0=chi[:], scalar=pm[:, 1:2], in1=cq3[:],
                                                     op0=ALU.mult, op1=ALU.add), reads=["chi", "pm", "cq3"], writes=["cq3"])
        P.op("dve", lambda e: e.tensor_tensor(out=cr1[:], in0=cr1[:], in1=chi[:], op=ALU.subtract),
             reads=["cr1", "chi"], writes=["cr1"])
        P.op("dve", lambda e: e.tensor_copy(out=chi[:], in_=cr1[:]), reads=["cr1"], writes=["chi"])
        P.op("dve", lambda e: e.scalar_tensor_tensor(out=cq3[:], in0=chi[:], scalar=pm[:, 2:3], in1=cq3[:],
                                                     op0=ALU.mult, op1=ALU.add), reads=["chi", "pm", "cq3"], writes=["cq3"])
        steps = []
        for q0 in range(0, NT, 4):
            nq = min(4, NT - q0)
            for kt in range(q0 + nq):
                steps.append((q0, nq, kt))
        LA = 2
        stb = [0, 1, 6]
        info = {}

        def emit_st(i):
            q0, nq, kt = steps[i]
            r0 = max(0, kt - q0)
            n = (nq - r0) * 128
            qa, qb = (q0 + r0) * 128, (q0 + nq) * 128
            sbk = stb[cnt["sb"] % 3]
            cnt["sb"] += 1
            diag = kt >= q0
            info[i] = (sbk, r0, n)
            P.op("pe", lambda e: e.matmul(ps[sbk][:, 0:n], lhsT=kh[:, kt * 128:(kt + 1) * 128], rhs=qh[:, qa:qb],
                                          start=True, stop=False), reads=["kh", "qh"], excl=[("ps", sbk)])
            P.op("pe", lambda e: e.matmul(ps[sbk][:, 0:n], lhsT=ones3[:], rhs=cq3[:, qa:qb], start=False, stop=(not diag)),
                 reads=["ones3", "cq3"], excl=[("ps", sbk)])
            if diag:
                P.op("pe", lambda e: e.matmul(ps[sbk][:, 0:128], lhsT=ibf[:], rhs=mneg[:], start=False, stop=True),
                     reads=["ibf", "mneg"], excl=[("ps", sbk)])

        def emit_exp_pv(i):
            q0, nq, kt = steps[i]
            sbk, r0, n = info[i]
            pb = cnt["pb"] % 3
            cnt["pb"] += 1
            P.op("act", lambda e: e.activation(out=pt[pb][:, 0:n], in_=ps[sbk][:, 0:n], func=AF.Exp,
                                               bias=negc[:, kt:kt + 1], scale=1.0),
                 reads=["negc"], excl=[("ps", sbk)], writes=[("pt", pb)])
            for r in range(r0, nq):
                qt = q0 + r
                P.op("pe", lambda e, r=r, qt=qt: e.matmul(
                    ps[2 + r][:, 0:129], lhsT=pt[pb][:, (r - r0) * 128:(r - r0 + 1) * 128], rhs=vaug[:, kt, :],
                    start=(kt == 0), stop=(kt == qt)),
                    reads=[("pt", pb), "vaug"], excl=[("ps", 2 + r)])

        def finalize(q0, nq):
            oi = cnt["ob"] % 2
            cnt["ob"] += 1
            for r in range(nq):
                P.op("dve", lambda e, r=r: e.reciprocal(out=rec[oi][:, r:r + 1], in_=ps[2 + r][:, 128:129]),
                     excl=[("ps", 2 + r)], writes=[("rec", oi)])
                P.op("dve", lambda e, r=r: e.tensor_scalar(out=osb[oi][:, r, :], in0=ps[2 + r][:, 0:128],
                                                          scalar1=rec[oi][:, r:r + 1], scalar2=None, op0=ALU.mult),
                     reads=[("rec", oi)], excl=[("ps", 2 + r)], writes=[("osb", oi)])
            P.dma("sp" if oi == 0 else "act",
                  lambda e, h=h: e.dma_start(
                      out=od[h, q0 * 128:(q0 + nq) * 128, :].rearrange("(r i) d -> i r d", i=128), in_=osb[oi][:, 0:nq, :]),
                  reads=[("osb", oi)])

        nst = len(steps)
        for i in range(min(LA, nst)):
            emit_st(i)
        for i in range(nst):
            if i + LA < nst:
                emit_st(i + LA)
            emit_exp_pv(i)
            q0, nq, kt = steps[i]
            if kt == q0 + nq - 1:
                finalize(q0, nq)
    P.finish()
    P.emit()
    return nc


def attn_consts():
    ii = np.arange(128)
    cm = np.zeros((128, 4, 128), np.float32)
    cm[:, 0, :] = (ii[:, None] <= ii[None, :])
    cm[:, 1, :] = 1.0
    cm[:, 2, :] = np.eye(128)
    cm[:, 3, :] = np.where(ii[:, None] > ii[None, :], NEG, 0.0)
    tt = np.arange(NT)
    lm = (tt[:, None] < tt[None, :]).astype(np.float32)
    return {"cm": cm, "lm": lm, "pm": np.eye(3, dtype=np.float32)}


_PROGS = {}


def _prog(name):
    if name not in _PROGS:
        if name.startswith("pre_"):
            _PROGS[name] = build_pre(name[4:])
        elif name.startswith("post_"):
            _PROGS[name] = build_post(name[5:])
        elif name == "scan":
            _PROGS[name] = build_scan()
        elif name == "attn":
            _PROGS[name] = build_attn()
    return _PROGS[name]


def _run(name, in_maps):
    res = run_bass_kernel_spmd(_prog(name), in_maps, core_ids=list(range(NCORE)))
    return res.results


def _fm(vec, nch):
    return np.ascontiguousarray(np.asarray(vec, np.float32).reshape(nch, 128).T)


def _cat_tokens(results, key, dtype):
    full = np.concatenate([np.asarray(r[key]) for r in results], axis=1)
    out = np.zeros((full.shape[0], LPAD), dtype)
    out[:, :NTOK] = full
    return out


def _scan_consts():
    ii = np.arange(128)
    same = (ii[:, None] // 64) == (ii[None, :] // 64)
    return np.stack([((ii[:, None] <= ii[None, :]) & same), ((ii[:, None] > ii[None, :]) & same)],
                    axis=1).astype(np.float32)


def _post_inputs(h, o_full, g_fm, c):
    lo = c * TC - HALO
    def tok_major(a):
        if lo < 0:
            blk = np.concatenate([np.zeros((HALO, D), np.float32), a[0:TC]], axis=0)
        else:
            blk = a[lo:lo + TP]
        return np.ascontiguousarray(blk.T)
    if lo < 0:
        g = np.concatenate([np.zeros((D, HALO), np.float32), g_fm[:, 0:TC]], axis=1)
    else:
        g = g_fm[:, lo:lo + TP]
    vm = np.ones((128, TP), np.float32)
    if c == 0:
        vm[:, 0:HALO] = 0.0
    return tok_major(h), tok_major(o_full), np.ascontiguousarray(g, dtype=np.float32), vm


def kernel(x, meta, ln_g, ln_b, gla_w_in, gla_w_g2, gla_b_g2, gla_norm_g, gla_w_out,
           kv_w, kv_bf, fox_w_in, fox_w_out, ffn_w_up, ffn_conv_w, ffn_conv_b, ffn_w_down):
    f32 = np.float32
    x = np.asarray(x, f32)
    h = np.concatenate([np.asarray(meta, f32), x[0]], axis=0)
    ln_g, ln_b = np.asarray(ln_g, f32), np.asarray(ln_b, f32)
    kF = vF = lfF = None
    for l in range(DEPTH):
        hts = [np.ascontiguousarray(h[c * TC:(c + 1) * TC].T) for c in range(NCORE)]
        if l < 2:
            w_in = np.asarray(gla_w_in[l], f32)
            wg2 = np.asarray(gla_w_g2[l], f32)
            bg2 = _fm(gla_b_g2[l], 8)
            res = _run("pre_gla", [{"ht": hts[c], "w": w_in, "wg2": wg2, "bg2": bg2} for c in range(NCORE)])
            qF = _cat_tokens(res, "q", NPBF)
            kFg = _cat_tokens(res, "k", NPBF)
            vFg = _cat_tokens(res, "v", NPBF)
            laF = _cat_tokens(res, "la", f32)
            rF = _cat_tokens(res, "r", f32)
            cmc = _scan_consts()
            ims = []
            for u in range(NCORE):
                hd, vh = divmod(u, 2)
                ks = kFg[hd * 256:(hd + 1) * 256]
                ims.append({"qf": np.ascontiguousarray(qF[hd * 256:(hd + 1) * 256]), "kf": np.ascontiguousarray(ks),
                            "ktm": np.ascontiguousarray(ks.T),
                            "vtm": np.ascontiguousarray(vFg[hd * 512 + vh * 256:hd * 512 + (vh + 1) * 256].T),
                            "latm": np.ascontiguousarray(laF[hd * 256:(hd + 1) * 256].T), "cm": cmc})
            res = _run("scan", ims)
            o_full = np.zeros((NTOK, D), f32)
            for u in range(NCORE):
                hd, vh = divmod(u, 2)
                o_full[:, hd * 512 + vh * 256:hd * 512 + (vh + 1) * 256] = np.asarray(res[u]["o"])[:NTOK]
            g_fm = rF
            pname = "post_gla"
            w_out = np.asarray(gla_w_out[l], f32)
        else:
            j = l - 2
            w_in = np.asarray(fox_w_in[j], f32)
            if l == 2:
                wkv = np.asarray(kv_w, f32)
                bfv = np.asarray(kv_bf, f32).reshape(FOX_HEADS, 1)
                res = _run("pre_fox_kv", [{"ht": hts[c], "w": w_in, "wkv": wkv, "bf": bfv} for c in range(NCORE)])
                kF = _cat_tokens(res, "kk", NPBF)
                vF = _cat_tokens(res, "vv", NPBF)
                lfF = _cat_tokens(res, "lf", f32)
            else:
                res = _run("pre_fox", [{"ht": hts[c], "w": w_in} for c in range(NCORE)])
            qF = _cat_tokens(res, "q", NPBF)
            ogF = _cat_tokens(res, "og", f32)
            ac = attn_consts()
            ims = []
            for c in range(NCORE):
                rows = slice(c * HPC * 128, (c + 1) * HPC * 128)
                im = dict(ac)
                im["qf"] = np.ascontiguousarray(qF[rows].reshape(HPC, 128, LPAD))
                im["kf"] = np.ascontiguousarray(kF[rows].reshape(HPC, 128, LPAD))
                im["vtm"] = np.ascontiguousarray(vF[rows].reshape(HPC, 128, LPAD).transpose(0, 2, 1))
                im["lft"] = np.ascontiguousarray(lfF[c * HPC:(c + 1) * HPC].reshape(HPC, NT, 128).transpose(0, 2, 1))
                ims.append(im)
            res = _run("attn", ims)
            o_full = np.zeros((NTOK, D), f32)
            for c in range(NCORE):
                o = np.asarray(res[c]["o"])
                for hh in range(HPC):
                    o_full[:, (c * HPC + hh) * 128:(c * HPC + hh + 1) * 128] = o[hh, :NTOK]
            g_fm = ogF
            pname = "post_fox"
            w_out = np.asarray(fox_w_out[j], f32)
        lnp = np.ascontiguousarray(np.stack([_fm(ln_g[l, 0], 16), _fm(ln_b[l, 0], 16), _fm(ln_g[l, 1], 16),
                                             _fm(ln_b[l, 1], 16)], axis=1))
        cw = np.asarray(ffn_conv_w[l], f32)
        cb = np.asarray(ffn_conv_b[l], f32)
        cwl = np.ascontiguousarray(np.concatenate([cw.reshape(3, 88, 128).transpose(2, 1, 0),
                                                   cb.reshape(88, 128).T[:, :, None]], axis=2))
        wup = np.asarray(ffn_w_up[l], f32)
        wdn = np.asarray(ffn_w_down[l], f32)
        ims = []
        for c in range(NCORE):
            ht_, ot_, gt_, vm_ = _post_inputs(h, o_full, g_fm, c)
            im = {"ht": ht_, "ot": ot_, "gt": gt_, "vmask": vm_, "wo": w_out, "lnp": lnp, "wup": wup, "cw": cwl, "wdn": wdn}
            if pname == "post_gla":
                im["ng"] = _fm(gla_norm_g[l], 4)
            ims.append(im)
        res = _run(pname, ims)
        h = np.concatenate([np.asarray(res[c]["hout"]).T for c in range(NCORE)], axis=0)
    return np.ascontiguousarray(h[N_META:].reshape(1, SEQ, D).astype(f32))
```

```python
import contextlib
import numpy as np
import ml_dtypes
import concourse.bass as bass
import concourse.mybir as mybir
from concourse.bass_utils import run_bass_kernel_spmd

F32 = mybir.dt.float32
BF16 = mybir.dt.bfloat16
AF = mybir.ActivationFunctionType
ALU = mybir.AluOpType
NPBF = ml_dtypes.bfloat16

D = 2048
SEQ = 8192
DEPTH = 4
N_META = 16
NTOK = SEQ + N_META
NCORE = 8
TC = NTOK // NCORE
HALO = 2
TP = TC + HALO
LPAD = 8320
NT = LPAD // 128
ALPHA = (2.0 * DEPTH) ** 0.25
LN_EPS = 1e-5
GLA_HEADS, GLA_HK, GLA_HV, GLA_RANK, GLA_TAU = 4, 256, 512, 16, 16.0
GLA_DK, GLA_DV = 1024, 2048
FOX_HEADS, FOX_HD = 16, 128
D_FF = 5632
NEG = -30000.0

ENGS = ("pe", "act", "dve", "pool", "sp")
NDS = 8


class Prog:
    def __init__(self, nc):
        self.nc = nc
        self.ops = {e: [] for e in ENGS}
        self.cnt = {e: 0 for e in ENGS}
        self.dcnt = {e: 0 for e in ENGS}
        self.lastw = {}
        self.readers = {}
        self.known = {e: {} for e in ENGS}
        self.dknown = {e: {q: set() for q in ENGS} for e in ENGS}
        self.dmax = {e: {q: 0 for q in ENGS} for e in ENGS}

    def _deps(self, eng, reads, writes):
        deps = {}
        ddeps = set()

        def need(ev):
            if ev is None:
                return
            kind, e, idx = ev
            if kind == "d":
                ddeps.add((e, idx))
                return
            if e == eng == "pe":
                return
            if deps.get(e, 0) < idx:
                deps[e] = idx

        for t in reads:
            need(self.lastw.get(t))
        for t in writes:
            need(self.lastw.get(t))
            for r in self.readers.get(t, ()):
                need(r)
        out = []
        kn = self.known[eng]
        for e, idx in deps.items():
            if kn.get(e, 0) >= idx:
                continue
            kn[e] = idx
            out.append(("c", e, idx))
        for (q, idx) in sorted(ddeps):
            if idx in self.dknown[eng][q] or idx <= self.dmax[eng][q] - NDS:
                continue
            self.dknown[eng][q].add(idx)
            if idx > self.dmax[eng][q]:
                self.dmax[eng][q] = idx
            out.append(("d", q, idx))
        return out

    def _commit(self, ev, reads, writes):
        for t in reads:
            self.readers.setdefault(t, []).append(ev)
        for t in writes:
            self.lastw[t] = ev
            self.readers[t] = []

    def op(self, eng, fn, reads=(), writes=(), excl=()):
        if excl:
            reads = list(reads) + list(excl)
            writes = list(writes) + list(excl)
        waits = self._deps(eng, reads, writes)
        self.cnt[eng] += 1
        self.ops[eng].append(("c", fn, waits, 0))
        self._commit(("c", eng, self.cnt[eng]), reads, writes)

    def dma(self, eng, fn, reads=(), writes=()):
        waits = self._deps(eng, reads, writes)
        self.dcnt[eng] += 1
        i = self.dcnt[eng]
        if i > NDS:
            prev = i - NDS
            if not (prev in self.dknown[eng][eng] or prev <= self.dmax[eng][eng] - NDS):
                self.dknown[eng][eng].add(prev)
                self.dmax[eng][eng] = max(self.dmax[eng][eng], prev)
                waits.append(("d", eng, prev))
        self.ops[eng].append(("d", fn, waits, i))
        self._commit(("d", eng, i), reads, writes)

    def finish(self, eng="sp"):
        waits = []
        for e in ENGS:
            for i in range(max(1, self.dcnt[e] - NDS + 1), self.dcnt[e] + 1):
                waits.append(("d", e, i))
            if self.cnt[e]:
                waits.append(("c", e, self.cnt[e]))
        self.ops[eng].append(("w", None, waits, 0))

    def emit(self):
        nc = self.nc
        with contextlib.ExitStack() as st:
            csem = {e: st.enter_context(nc.semaphore("c_" + e)) for e in ENGS}
            dsem = {e: [st.enter_context(nc.semaphore("d_%s%d" % (e, k))) for k in range(NDS)]
                    for e in ("sp", "act", "pool")}
            block = st.enter_context(nc.Block())

            def run(engname):
                def body(eh):
                    for kind, fn, waits, di in self.ops[engname]:
                        for (k, e, idx) in waits:
                            if k == "c":
                                eh.wait_ge(csem[e], idx)
                            else:
                                eh.wait_ge(dsem[e][(idx - 1) % NDS], 16 * ((idx - 1) // NDS + 1))
                        if kind == "c":
                            fn(eh).then_inc(csem[engname], 1)
                        elif kind == "d":
                            fn(eh).then_inc(dsem[engname][(di - 1) % NDS], 16)
                return body

            block.sync(run("sp"))
            block.scalar(run("act"))
            block.vector(run("dve"))
            block.gpsimd(run("pool"))
            block.tensor(run("pe"))


def ntiles(total, maxn=512):
    n = -(-total // maxn)
    base = -(-total // n)
    out = []
    s = 0
    while s < total:
        e = min(total, s + base)
        out.append((s, e))
        s = e
    return out


class Ctx:
    def __init__(self, name):
        self.nc = bass.Bass("TRN2", target_bir_lowering=False)
        self.P = Prog(self.nc)
        self.psbig = self.nc.alloc_psum_tensor("psbig", [128, 4096], F32)
        self.ps = [self.psbig[:, b * 512:(b + 1) * 512] for b in range(8)]
        self.uid = 0

    def sb(self, name, shape, dt):
        return self.nc.alloc_sbuf_tensor(name, list(shape), dt)

    def din(self, name, shape, dt):
        return self.nc.dram_tensor(name, list(shape), dt, kind="ExternalInput").ap()

    def dout(self, name, shape, dt):
        return self.nc.dram_tensor(name, list(shape), dt, kind="ExternalOutput").ap()


def build_pre(kind):
    C = Ctx("pre_" + kind)
    nc, P = C.nc, C.P
    T = TC
    ht = C.din("ht", [D, T], F32)
    if kind == "gla":
        dout_w = 2 * GLA_DK + GLA_DV + GLA_RANK + GLA_DV
        w = C.din("w", [D, dout_w], F32)
        wg2 = C.din("wg2", [GLA_RANK, GLA_DK], F32)
        bg2 = C.din("bg2", [128, GLA_DK // 128], F32)
        segs = [("g", 2 * GLA_DK + GLA_DV, 16, "glow", None),
                ("q", 0, GLA_DK, "scale", BF16), ("k", GLA_DK, GLA_DK, "copy", BF16),
                ("v", 2 * GLA_DK, GLA_DV, "copy", BF16),
                ("r", 2 * GLA_DK + GLA_DV + GLA_RANK, GLA_DV, "copy", F32)]
        qscale = GLA_HK ** -0.5
        outs = {"q": C.dout("q", [GLA_DK, T], BF16), "k": C.dout("k", [GLA_DK, T], BF16),
                "v": C.dout("v", [GLA_DV, T], BF16), "r": C.dout("r", [GLA_DV, T], F32),
                "la": C.dout("la", [GLA_DK, T], F32)}
    else:
        wq = C.din("w", [D, 2 * D], F32)
        segs = [("q", 0, D, "scale", BF16), ("og", D, D, "copy", F32)]
        qscale = FOX_HD ** -0.5
        outs = {"q": C.dout("q", [D, T], BF16), "og": C.dout("og", [D, T], F32)}
        if kind == "fox_kv":
            wkv = C.din("wkv", [D, 2 * D + FOX_HEADS], F32)
            bf = C.din("bf", [FOX_HEADS, 1], F32)
            outs.update({"kk": C.dout("kk", [D, T], BF16), "vv": C.dout("vv", [D, T], BF16),
                         "lf": C.dout("lf", [FOX_HEADS, T], F32)})

    hs = C.sb("hs", [128, 16, T], F32)
    hb = C.sb("hb", [128, 16, T], BF16)
    NW = 4
    wt = [C.sb("wt%d" % i, [128, 16, 128], BF16) for i in range(NW)]
    NS = 3
    stf = [C.sb("stf%d" % i, [128, T], F32) for i in range(NS)]
    stb = [C.sb("stb%d" % i, [128, T], BF16) for i in range(NS)]
    tiles = ntiles(T)

    for c in range(16):
        q = "sp" if c % 2 == 0 else "act"
        P.dma(q, lambda e, c=c: e.dma_start(out=hs[:, c, :], in_=ht[c * 128:(c + 1) * 128, :]),
              writes=[("hs", c)])
        if c % 2 == 0:
            P.op("dve", lambda e, c=c: e.tensor_copy(out=hb[:, c, :], in_=hs[:, c, :]),
                 reads=[("hs", c)], writes=[("hb", c)])
        else:
            P.op("act", lambda e, c=c: e.activation(out=hb[:, c, :], in_=hs[:, c, :], func=AF.Copy),
                 reads=[("hs", c)], writes=[("hb", c)])
    hb_tok = [("hb", c) for c in range(16)]

    state = {"wi": 0, "si": 0, "bank": 0}

    def proj_chunk(wsrc, col0, ncols, epilogue):
        wi = state["wi"] % NW
        state["wi"] += 1
        P.dma("pool", lambda e: e.dma_start(
            out=wt[wi][:, :, 0:ncols],
            in_=wsrc[:, col0:col0 + ncols].rearrange("(kc p) j -> p kc j", p=128)),
            writes=[("wt", wi)])
        for (t0, t1) in tiles:
            b = state["bank"] % 4
            state["bank"] += 1
            for kc in range(16):
                P.op("pe", lambda e, kc=kc, b=b, t0=t0, t1=t1: e.matmul(
                    C.ps[b][0:ncols, 0:t1 - t0], lhsT=wt[wi][:, kc, 0:ncols], rhs=hb[:, kc, t0:t1],
                    start=(kc == 0), stop=(kc == 15)),
                    reads=[("wt", wi), ("hb", kc)], excl=[("ps", b)])
            epilogue(b, t0, t1)

    if kind == "gla":
        gl = C.sb("gl", [GLA_RANK, T], BF16)
        wg2f = C.sb("wg2f", [GLA_RANK, GLA_DK], F32)
        wg2b = C.sb("wg2b", [GLA_RANK, GLA_DK], BF16)
        bg = C.sb("bg", [128, 8], F32)
        nbg = C.sb("nbg", [128, 8], F32)
        P.dma("sp", lambda e: e.dma_start(out=wg2f[:], in_=wg2), writes=["wg2f"])
        P.dma("sp", lambda e: e.dma_start(out=bg[:], in_=bg2), writes=["bg"])
        P.op("dve", lambda e: e.tensor_copy(out=wg2b[:], in_=wg2f[:]), reads=["wg2f"], writes=["wg2b"])
        P.op("dve", lambda e: e.tensor_scalar(out=nbg[:], in0=bg[:], scalar1=-1.0, scalar2=None, op0=ALU.mult),
             reads=["bg"], writes=["nbg"])

        def ep_glow(b, t0, t1):
            P.op("act", lambda e: e.activation(out=gl[:, t0:t1], in_=C.ps[b][0:GLA_RANK, 0:t1 - t0], func=AF.Copy),
                 excl=[("ps", b)], writes=[("gl", t0)])
        proj_chunk(w, 2 * GLA_DK + GLA_DV, GLA_RANK, ep_glow)
        for fc in range(8):
            si = state["si"] % NS
            state["si"] += 1
            for (t0, t1) in tiles:
                b = 4 + (state["bank"] % 2)
                state["bank"] += 1
                P.op("pe", lambda e, fc=fc, b=b, t0=t0, t1=t1: e.matmul(
                    C.ps[b][:, 0:t1 - t0], lhsT=wg2b[:, fc * 128:(fc + 1) * 128], rhs=gl[:, t0:t1],
                    start=True, stop=True), reads=["wg2b", ("gl", t0)], excl=[("ps", b)])
                P.op("act", lambda e, fc=fc, b=b, t0=t0, t1=t1, si=si: e.activation(
                    out=stf[si][:, t0:t1], in_=C.ps[b][:, 0:t1 - t0], func=AF.Exp,
                    bias=nbg[:, fc:fc + 1], scale=-1.0),
                    reads=["nbg"], excl=[("ps", b)], writes=[("stf", si)])
            P.op("act", lambda e, si=si: e.activation(out=stf[si][:], in_=stf[si][:], func=AF.Ln, bias=1.0, scale=1.0),
                 reads=[("stf", si)], writes=[("stf", si)])
            P.op("dve", lambda e, si=si: e.tensor_scalar(out=stf[si][:], in0=stf[si][:], scalar1=-1.0 / GLA_TAU,
                                                        scalar2=None, op0=ALU.mult),
                 reads=[("stf", si)], writes=[("stf", si)])
            P.dma("sp", lambda e, fc=fc, si=si: e.dma_start(out=outs["la"][fc * 128:(fc + 1) * 128, :], in_=stf[si][:]),
                  reads=[("stf", si)])

    def run_seg(wsrc, name, col0, ncols, mode, odt, oname=None):
        oname = oname or name
        for ch in range(ncols // 128):
            si = state["si"] % NS
            state["si"] += 1
            dst = stf[si] if odt == F32 else stb[si]
            tok = ("stf", si) if odt == F32 else ("stb", si)

            def ep(b, t0, t1, dst=dst, tok=tok):
                if mode == "scale":
                    P.op("act", lambda e: e.activation(out=dst[:, t0:t1], in_=C.ps[b][:, 0:t1 - t0], func=AF.Copy,
                                                       scale=float(qscale)),
                         excl=[("ps", b)], writes=[tok])
                else:
                    eng = "dve" if (t0 // 8) % 2 == 0 else "act"
                    if eng == "dve":
                        P.op("dve", lambda e: e.tensor_copy(out=dst[:, t0:t1], in_=C.ps[b][:, 0:t1 - t0]),
                             excl=[("ps", b)], writes=[tok])
                    else:
                        P.op("act", lambda e: e.activation(out=dst[:, t0:t1], in_=C.ps[b][:, 0:t1 - t0], func=AF.Copy),
                             excl=[("ps", b)], writes=[tok])
            proj_chunk(wsrc, col0 + ch * 128, 128, ep)
            P.dma("sp" if ch % 2 == 0 else "act",
                  lambda e, ch=ch, dst=dst: e.dma_start(out=outs[oname][ch * 128:(ch + 1) * 128, :], in_=dst[:]),
                  reads=[tok])

    if kind == "gla":
        for (name, col0, ncols, mode, odt) in segs[1:]:
            run_seg(w, name, col0, ncols, mode, odt)
    else:
        for (name, col0, ncols, mode, odt) in segs:
            run_seg(wq, name, col0, ncols, mode, odt)
        if kind == "fox_kv":
            run_seg(wkv, "kk", 0, D, "copy", BF16)
            run_seg(wkv, "vv", D, D, "copy", BF16)
            bfs = C.sb("bfs", [FOX_HEADS, 1], F32)
            nbf = C.sb("nbf", [FOX_HEADS, 1], F32)
            lfs = C.sb("lfs", [FOX_HEADS, T], F32)
            P.dma("sp", lambda e: e.dma_start(out=bfs[:], in_=bf), writes=["bfs"])
            P.op("dve", lambda e: e.tensor_scalar(out=nbf[:], in0=bfs[:], scalar1=-1.0, scalar2=None, op0=ALU.mult),
                 reads=["bfs"], writes=["nbf"])

            def ep_f(b, t0, t1):
                P.op("act", lambda e: e.activation(out=lfs[:, t0:t1], in_=C.ps[b][0:FOX_HEADS, 0:t1 - t0], func=AF.Exp,
                                                   bias=nbf[:, 0:1], scale=-1.0),
                     reads=["nbf"], excl=[("ps", b)], writes=["lfs"])
            proj_chunk(wkv, 2 * D, FOX_HEADS, ep_f)
            P.op("act", lambda e: e.activation(out=lfs[:], in_=lfs[:], func=AF.Ln, bias=1.0, scale=1.0),
                 reads=["lfs"], writes=["lfs"])
            P.op("dve", lambda e: e.tensor_scalar(out=lfs[:], in0=lfs[:], scalar1=-1.0, scalar2=None, op0=ALU.mult),
                 reads=["lfs"], writes=["lfs"])
            P.dma("sp", lambda e: e.dma_start(out=outs["lf"], in_=lfs[:]), reads=["lfs"])

    P.finish()
    P.emit()
    return nc


def build_post(kind):
    C = Ctx("post_" + kind)
    nc, P = C.nc, C.P
    ht = C.din("ht", [D, TP], F32)
    ot = C.din("ot", [D, TP], F32)
    gt = C.din("gt", [D, TP], F32)
    vm = C.din("vmask", [128, TP], F32)
    wo = C.din("wo", [D, D], F32)
    lnp = C.din("lnp", [128, 4, 16], F32)
    wup = C.din("wup", [D, 2 * D_FF], F32)
    cwd = C.din("cw", [128, 88, 4], F32)
    wdn = C.din("wdn", [D_FF, D], F32)
    hout = C.dout("hout", [D, TC], F32)
    if kind == "gla":
        ngd = C.din("ng", [128, 4], F32)

    TW = 343
    TPX = 3 * TW
    hs = C.sb("hs", [128, 16, TP], F32)
    hb = C.sb("hb", [128, 16, TPX + 3], BF16)
    tA = [C.sb("tA%d" % i, [128, TPX + 3], F32) for i in range(2)]
    tB = [C.sb("tB%d" % i, [128, TPX + 3], F32) for i in range(2)]
    tC = [C.sb("tC%d" % i, [128, TPX + 3], F32) for i in range(2)]
    NW = 4
    wt = [C.sb("wt%d" % i, [128, 16, 128], BF16) for i in range(NW)]
    GP = 11
    NG = 4
    act = C.sb("act", [128, GP, TC], BF16)
    NWD = 3
    wd = [C.sb("wd%d" % i, [128, GP, 128], BF16) for i in range(NWD)]
    st = [C.sb("st%d" % i, [128, TP], F32) for i in range(3)]
    ones = C.sb("ones", [128, 128], F32)
    lns = C.sb("lns", [128, 4, 16], F32)
    cws = C.sb("cws", [128, 88, 4], F32)
    vms = C.sb("vms", [128, TP], F32)
    tiles_p = ntiles(TP)
    tiles_c = ntiles(TC)
    cnt = {"w": 0, "wd": 0, "a": 0, "b": 0, "c": 0, "dq": 0}

    def dq():
        cnt["dq"] += 1
        return "sp" if cnt["dq"] % 2 else "act"

    P.op("pool", lambda e: e.memset(ones[:], 1.0), writes=["ones"])
    P.op("pool", lambda e: e.memset(hb[:], 0.0), writes=[("hb", c) for c in range(16)])
    P.dma("sp", lambda e: e.dma_start(out=lns[:], in_=lnp), writes=["lns"])
    P.dma("sp", lambda e: e.dma_start(out=cws[:], in_=cwd), writes=["cws"])
    P.dma("sp", lambda e: e.dma_start(out=vms[:], in_=vm), writes=["vms"])
    if kind == "gla":
        ngs = C.sb("ngs", [128, 4], F32)
        P.dma("sp", lambda e: e.dma_start(out=ngs[:], in_=ngd), writes=["ngs"])
    for c in range(16):
        P.dma(dq(), lambda e, c=c: e.dma_start(out=hs[:, c, :], in_=ht[c * 128:(c + 1) * 128, :]),
              writes=[("hs", c)])

    def load(buf, tokname, src, c):
        i = cnt[tokname] % 2
        cnt[tokname] += 1
        P.dma(dq(), lambda e: e.dma_start(out=buf[i][:, 0:TP], in_=src[c * 128:(c + 1) * 128, :]),
              writes=[(tokname, i)])
        return i

    def rstd_from(src_banks, tiles, scale, dst, dst_tok):
        for ti, (t0, t1) in enumerate(tiles):
            b = src_banks[ti]
            P.op("dve", lambda e, b=b, t0=t0, t1=t1: e.tensor_scalar(
                out=dst[:, t0:t1], in0=C.ps[b][:, 0:t1 - t0], scalar1=float(scale), scalar2=float(LN_EPS),
                op0=ALU.mult, op1=ALU.add), excl=[("ps", b)], writes=[dst_tok])
        P.op("act", lambda e: e.activation(out=dst[:], in_=dst[:], func=AF.Sqrt), reads=[dst_tok],
             writes=[dst_tok])
        P.op("dve", lambda e: e.reciprocal(out=dst[:], in_=dst[:]), reads=[dst_tok],
             writes=[dst_tok])

    if kind == "gla":
        for hd in range(4):
            for cc in range(4):
                c = hd * 4 + cc
                ia = load(tA, "a", ot, c)
                ic = cnt["c"] % 2
                cnt["c"] += 1
                P.op("act", lambda e, ia=ia, ic=ic: e.activation(out=tC[ic][:, 0:TP], in_=tA[ia][:, 0:TP], func=AF.Square),
                     reads=[("a", ia)], writes=[("c", ic)])
                for ti, (t0, t1) in enumerate(tiles_p):
                    P.op("pe", lambda e, ic=ic, ti=ti, t0=t0, t1=t1, cc=cc: e.matmul(
                        C.ps[ti][:, 0:t1 - t0], lhsT=ones[:], rhs=tC[ic][:, t0:t1], start=(cc == 0), stop=(cc == 3)),
                        reads=["ones", ("c", ic)], excl=[("ps", ti)])
            rstd_from([0, 1, 2], tiles_p, 1.0 / GLA_HV, st[0], ("st", 0))
            for cc in range(4):
                c = hd * 4 + cc
                ia = load(tA, "a", ot, c)
                ib = load(tB, "b", gt, c)
                P.op("act", lambda e, ib=ib: e.activation(out=tB[ib][:, 0:TP], in_=tB[ib][:, 0:TP], func=AF.Silu),
                     reads=[("b", ib)], writes=[("b", ib)])
                P.op("dve", lambda e, ia=ia: e.tensor_tensor(out=tA[ia][:, 0:TP], in0=tA[ia][:, 0:TP], in1=st[0][:], op=ALU.mult),
                     reads=[("a", ia), ("st", 0)], writes=[("a", ia)])
                P.op("act", lambda e, ib=ib, cc=cc: e.activation(
                    out=tB[ib][:, 0:TP], in_=tB[ib][:, 0:TP], func=AF.Copy, scale=ngs[:, cc:cc + 1]),
                    reads=[("b", ib), "ngs"], writes=[("b", ib)])
                P.op("pool" if c % 2 == 0 else "dve", lambda e, ia=ia, ib=ib, c=c: e.tensor_tensor(
                    out=hb[:, c, 0:TP], in0=tA[ia][:, 0:TP], in1=tB[ib][:, 0:TP], op=ALU.mult),
                    reads=[("a", ia), ("b", ib)], writes=[("hb", c)])
    else:
        for c in range(16):
            ia = load(tA, "a", ot, c)
            ib = load(tB, "b", gt, c)
            P.op("act", lambda e, ib=ib: e.activation(out=tB[ib][:, 0:TP], in_=tB[ib][:, 0:TP], func=AF.Sigmoid),
                 reads=[("b", ib)], writes=[("b", ib)])
            P.op("dve" if c % 2 == 0 else "pool",
                 lambda e, ia=ia, ib=ib, c=c: e.tensor_tensor(out=hb[:, c, 0:TP], in0=tA[ia][:, 0:TP], in1=tB[ib][:, 0:TP], op=ALU.mult),
                 reads=[("a", ia), ("b", ib)], writes=[("hb", c)])

    def wload(src_ap):
        wi = cnt["w"] % NW
        cnt["w"] += 1
        P.dma("pool", lambda e: e.dma_start(out=wt[wi][:], in_=src_ap.rearrange("(kc p) j -> p kc j", p=128)),
              writes=[("wt", wi)])
        return wi

    obank = [6, 7]
    ob = {"i": 0}
    wow = {}

    def ensure_wo(fc):
        if fc in wow or fc >= 16:
            return
        wow[fc] = wload(wo[:, fc * 128:(fc + 1) * 128])

    for fc in range(16):
        for k in range(3):
            ensure_wo(fc + k)
        wi = wow[fc]
        for (t0, t1) in tiles_p:
            b = obank[ob["i"] % 2]
            ob["i"] += 1
            for kc in range(16):
                P.op("pe", lambda e, kc=kc, b=b, t0=t0, t1=t1, wi=wi: e.matmul(
                    C.ps[b][:, 0:t1 - t0], lhsT=wt[wi][:, kc, :], rhs=hb[:, kc, t0:t1], start=(kc == 0), stop=(kc == 15)),
                    reads=[("wt", wi), ("hb", kc)], excl=[("ps", b)])
            P.op("dve", lambda e, fc=fc, b=b, t0=t0, t1=t1: e.scalar_tensor_tensor(
                out=hs[:, fc, t0:t1], in0=hs[:, fc, t0:t1], scalar=float(ALPHA), in1=C.ps[b][:, 0:t1 - t0],
                op0=ALU.mult, op1=ALU.add), reads=[("hs", fc)], writes=[("hs", fc)], excl=[("ps", b)])

    def layer_norm(lo, gi, bi, make_hb):
        n = TP - lo
        tl = [(lo + a, lo + b_) for (a, b_) in ntiles(n)]
        for c in range(16):
            ic = cnt["c"] % 2
            cnt["c"] += 1
            P.op("act", lambda e, c=c, ic=ic: e.activation(out=tC[ic][:, lo:TP], in_=hs[:, c, lo:TP], func=AF.Square),
                 reads=[("hs", c)], writes=[("c", ic)])
            for ti, (t0, t1) in enumerate(tl):
                P.op("pe", lambda e, c=c, ti=ti, t0=t0, t1=t1: e.matmul(
                    C.ps[ti][:, 0:t1 - t0], lhsT=ones[:], rhs=hs[:, c, t0:t1], start=(c == 0), stop=(c == 15)),
                    reads=["ones", ("hs", c)], excl=[("ps", ti)])
                P.op("pe", lambda e, c=c, ic=ic, ti=ti, t0=t0, t1=t1: e.matmul(
                    C.ps[3 + ti][:, 0:t1 - t0], lhsT=ones[:], rhs=tC[ic][:, t0:t1], start=(c == 0), stop=(c == 15)),
                    reads=["ones", ("c", ic)], excl=[("ps", 3 + ti)])
        mean, var, tmp = st[0], st[1], st[2]
        for ti, (t0, t1) in enumerate(tl):
            P.op("dve", lambda e, ti=ti, t0=t0, t1=t1: e.tensor_scalar(
                out=mean[:, t0:t1], in0=C.ps[ti][:, 0:t1 - t0], scalar1=1.0 / D, scalar2=None, op0=ALU.mult),
                excl=[("ps", ti)], writes=[("st", 0)])
            P.op("dve", lambda e, ti=ti, t0=t0, t1=t1: e.tensor_scalar(
                out=var[:, t0:t1], in0=C.ps[3 + ti][:, 0:t1 - t0], scalar1=1.0 / D, scalar2=float(LN_EPS),
                op0=ALU.mult, op1=ALU.add), excl=[("ps", 3 + ti)], writes=[("st", 1)])
        P.op("dve", lambda e: e.tensor_tensor(out=tmp[:, lo:TP], in0=mean[:, lo:TP], in1=mean[:, lo:TP], op=ALU.mult),
             reads=[("st", 0)], writes=[("st", 2)])
        P.op("dve", lambda e: e.tensor_tensor(out=var[:, lo:TP], in0=var[:, lo:TP], in1=tmp[:, lo:TP], op=ALU.subtract),
             reads=[("st", 1), ("st", 2)], writes=[("st", 1)])
        P.op("act", lambda e: e.activation(out=var[:, lo:TP], in_=var[:, lo:TP], func=AF.Sqrt), reads=[("st", 1)], writes=[("st", 1)])
        P.op("dve", lambda e: e.reciprocal(out=var[:, lo:TP], in_=var[:, lo:TP]), reads=[("st", 1)], writes=[("st", 1)])
        for c in range(16):
            ia = cnt["a"] % 2
            cnt["a"] += 1
            P.op("dve", lambda e, c=c, ia=ia: e.tensor_tensor(out=tA[ia][:, lo:TP], in0=hs[:, c, lo:TP], in1=mean[:, lo:TP],
                                                             op=ALU.subtract),
                 reads=[("hs", c), ("st", 0)], writes=[("a", ia)])
            P.op("pool" if c % 2 == 0 else "dve",
                 lambda e, ia=ia: e.tensor_tensor(out=tA[ia][:, lo:TP], in0=tA[ia][:, lo:TP], in1=var[:, lo:TP],
                                                  op=ALU.mult),
                 reads=[("a", ia), ("st", 1)], writes=[("a", ia)])
            P.op("dve", lambda e, c=c, ia=ia: e.tensor_scalar(
                out=hs[:, c, lo:TP], in0=tA[ia][:, lo:TP], scalar1=lns[:, gi, c:c + 1], scalar2=lns[:, bi, c:c + 1],
                op0=ALU.mult, op1=ALU.add), reads=[("a", ia), "lns"], writes=[("hs", c)])
            if make_hb:
                P.op("act", lambda e, c=c: e.activation(out=hb[:, c, 0:TP], in_=hs[:, c, :], func=AF.Copy),
                     reads=[("hs", c)], writes=[("hb", c)])
                P.op("dve", lambda e, c=c: e.tensor_tensor(out=hb[:, c, 0:HALO], in0=hb[:, c, 0:HALO], in1=vms[:, 0:HALO],
                                                          op=ALU.mult),
                     reads=["vms"], writes=[("hb", c)])

    layer_norm(0, 0, 1, True)

    xu, xg, cu, cg = tA, tB, tC[0], tC[1]
    upw = {}
    dnw = {}

    def ensure_up(j):
        if j in upw or j >= NG * GP:
            return
        upw[j] = (wload(wup[:, j * 128:(j + 1) * 128]), wload(wup[:, D_FF + j * 128:D_FF + (j + 1) * 128]))

    def ensure_down(g, fc):
        if (g, fc) in dnw or fc >= 16:
            return
        wi = cnt["wd"] % NWD
        cnt["wd"] += 1
        P.dma("pool", lambda e: e.dma_start(
            out=wd[wi][:], in_=wdn[g * GP * 128:(g + 1) * GP * 128, fc * 128:(fc + 1) * 128].rearrange(
                "(jj p) m -> p jj m", p=128)), writes=[("wd", wi)])
        dnw[(g, fc)] = wi
    for g in range(NG):
        for jj in range(GP):
            j = g * GP + jj
            ensure_up(j)
            ensure_up(j + 1)
            if jj == GP - 1:
                ensure_down(g, 0)
                ensure_down(g, 1)
            wa, wb = upw[j]
            for (wi, b0, X, xt_, col) in ((wa, 0, (tA[0], tA[1], tC[0]), ("a", 0), j), (wb, 3, (tB[0], tB[1], tC[1]), ("b", 0), 44 + j)):
                for ti in range(3):
                    for kc in range(16):
                        P.op("pe", lambda e, kc=kc, wi=wi, b0=b0, ti=ti: e.matmul(
                            C.ps[b0 + ti][:, 0:TW], lhsT=wt[wi][:, kc, :], rhs=hb[:, kc, ti * TW:(ti + 1) * TW],
                            start=(kc == 0), stop=(kc == 15)),
                            reads=[("wt", wi), ("hb", kc)], excl=[("ps", b0 + ti)])
                src = C.psbig[:, b0 * 512:(b0 + 3) * 512].rearrange("p (b c) -> p b c", c=512)[:, :, 0:TW]
                banks = [("ps", b0), ("ps", b0 + 1), ("ps", b0 + 2)]
                for tap, buf, tok in ((2, X[0], (xt_[0], 0)), (1, X[1], (xt_[0], 1)), (0, X[2], ("c", 0 if b0 == 0 else 1))):
                    P.op("act", lambda e, tap=tap, buf=buf, src=src, col=col: e.activation(
                        out=buf[:, 0:TPX].rearrange("p (b c) -> p b c", c=TW), in_=src, func=AF.Identity,
                        scale=cws[:, col, tap:tap + 1], bias=(cws[:, col, 3:4] if tap == 2 else 0.0)),
                        reads=["cws"], excl=banks, writes=[tok])
                t2, t1_, t0_ = (xt_[0], 0), (xt_[0], 1), ("c", 0 if b0 == 0 else 1)
                P.op("dve", lambda e, X=X: e.tensor_tensor(out=X[0][:, 2:TP], in0=X[0][:, 2:TP], in1=X[1][:, 1:TP - 1], op=ALU.add),
                     reads=[t2, t1_], writes=[t2])
                P.op("dve", lambda e, X=X: e.tensor_tensor(out=X[0][:, 2:TP], in0=X[0][:, 2:TP], in1=X[2][:, 0:TC], op=ALU.add),
                     reads=[t2, t0_], writes=[t2])
            P.op("act", lambda e: e.activation(out=tB[0][:, 2:TP], in_=tB[0][:, 2:TP], func=AF.Silu), reads=[("b", 0)], writes=[("b", 0)])
            P.op("dve", lambda e, jj=jj: e.tensor_tensor(out=act[:, jj, :], in0=tA[0][:, 2:TP], in1=tB[0][:, 2:TP], op=ALU.mult),
                 reads=[("a", 0), ("b", 0)], writes=[("act", jj)])
        for fc in range(16):
            ensure_down(g, fc)
            ensure_down(g, fc + 1)
            ensure_down(g, fc + 2)
            wi = dnw[(g, fc)]
            for (t0, t1) in tiles_c:
                b = obank[ob["i"] % 2]
                ob["i"] += 1
                for jj in range(GP):
                    P.op("pe", lambda e, jj=jj, wi=wi, b=b, t0=t0, t1=t1: e.matmul(
                        C.ps[b][:, 0:t1 - t0], lhsT=wd[wi][:, jj, :], rhs=act[:, jj, t0:t1],
                        start=(jj == 0), stop=(jj == GP - 1)),
                        reads=[("wd", wi), ("act", jj)], excl=[("ps", b)])
                P.op("dve", lambda e, fc=fc, b=b, t0=t0, t1=t1, g=g: e.scalar_tensor_tensor(
                    out=hs[:, fc, 2 + t0:2 + t1], in0=hs[:, fc, 2 + t0:2 + t1], scalar=float(ALPHA if g == 0 else 1.0),
                    in1=C.ps[b][:, 0:t1 - t0], op0=ALU.mult, op1=ALU.add),
                    reads=[("hs", fc)], writes=[("hs", fc)], excl=[("ps", b)])

    layer_norm(2, 2, 3, False)
    for c in range(16):
        P.dma(dq(), lambda e, c=c: e.dma_start(out=hout[c * 128:(c + 1) * 128, :], in_=hs[:, c, 2:TP]),
              reads=[("hs", c)])
    P.finish()
    P.emit()
    return nc


def build_scan(debug=False):
    C = Ctx("scan")
    nc, P = C.nc, C.P
    L = LPAD
    qf = C.din("qf", [256, L], BF16)
    kf = C.din("kf", [256, L], BF16)
    ktm = C.din("ktm", [L, 256], BF16)
    vtm = C.din("vtm", [L, 256], BF16)
    latm = C.din("latm", [L, 256], F32)
    cmd = C.din("cm", [128, 2, 128], F32)
    od = C.dout("o", [L, 256], F32)
    GT = 13
    NGRP = NT // GT
    GL = GT * 128
    qin = [C.sb("qin%d" % i, [128, 2, GL], BF16) for i in range(2)]
    kin = [C.sb("kin%d" % i, [128, 2, GL], BF16) for i in range(2)]
    ktin = [C.sb("ktin%d" % i, [128, GT, 256], BF16) for i in range(2)]
    vtin = [C.sb("vtin%d" % i, [128, GT, 256], BF16) for i in range(2)]
    lain = [C.sb("lain%d" % i, [128, GT, 256], F32) for i in range(2)]
    cm = C.sb("cm_sb", [128, 2, 128], F32)
    eb = [C.sb("eb%d" % i, [128, 2, 128], F32) for i in range(2)]
    enb = [C.sb("enb%d" % i, [128, 2, 128], F32) for i in range(2)]
    qg = [C.sb("qg%d" % i, [128, 2, 128], BF16) for i in range(2)]
    kg = [C.sb("kg%d" % i, [128, 2, 128], BF16) for i in range(2)]
    ekd = [C.sb("ekd%d" % i, [128, 256], F32) for i in range(2)]
    kd = [C.sb("kd%d" % i, [128, 256], BF16) for i in range(2)]
    attm = [C.sb("attm%d" % i, [128, 128], BF16) for i in range(2)]
    osb = [C.sb("osb%d" % i, [64, 2, 256], F32) for i in range(2)]
    S = C.sb("S_sb", [128, 2, 256], F32)
    Sb = [C.sb("Sb_sb%d" % i, [128, 2, 256], BF16) for i in range(2)]
    ps = C.ps

    P.dma("sp", lambda e: e.dma_start(out=cm[:], in_=cmd), writes=["cm"])
    P.op("dve", lambda e: e.memset(S[:], 0.0), writes=[("S", 0), ("S", 1)])
    for i in range(2):
        P.op("pool", lambda e, i=i: e.memset(Sb[i][:], 0.0), writes=[("Sb", i, 0), ("Sb", i, 1)])

    def load_group(g):
        i = g % 2
        c0 = g * GL
        P.dma("sp", lambda e: e.dma_start(out=qin[i][:], in_=qf[:, c0:c0 + GL].rearrange("(kc p) t -> p kc t", p=128)),
              writes=[("qin", i)])
        P.dma("act", lambda e: e.dma_start(out=kin[i][:], in_=kf[:, c0:c0 + GL].rearrange("(kc p) t -> p kc t", p=128)),
              writes=[("kin", i)])
        P.dma("sp", lambda e: e.dma_start(out=ktin[i][:], in_=ktm[c0:c0 + GL, :].rearrange("(t p) k -> p t k", p=128)),
              writes=[("ktin", i)])
        P.dma("act", lambda e: e.dma_start(out=vtin[i][:], in_=vtm[c0:c0 + GL, :].rearrange("(t p) k -> p t k", p=128)),
              writes=[("vtin", i)])
        P.dma("sp", lambda e: e.dma_start(out=lain[i][:], in_=latm[c0:c0 + GL, :].rearrange("(t p) k -> p t k", p=128)),
              writes=[("lain", i)])

    def precompute(tt):
        g, tl = divmod(tt, GT)
        gi = g % 2
        pr = tt % 2
        bb = 0 if pr == 0 else 6
        b2 = 1 if pr == 0 else 7
        cs = slice(tl * 128, (tl + 1) * 128)
        for kc in range(2):
            P.op("pe", lambda e, kc=kc: e.matmul(ps[bb][:, kc * 128:(kc + 1) * 128], lhsT=lain[gi][:, tl, kc * 128:(kc + 1) * 128],
                                                rhs=cm[:, 0, :], start=True, stop=True),
                 reads=[("lain", gi), "cm"], excl=[("ps", bb)])
        P.op("act", lambda e: e.activation(out=eb[pr][:].rearrange("p a b -> p (a b)"), in_=ps[bb][:, 0:256], func=AF.Exp),
             excl=[("ps", bb)], writes=[("eb", pr)])
        P.op("act", lambda e: e.activation(out=enb[pr][:].rearrange("p a b -> p (a b)"), in_=ps[bb][:, 0:256], func=AF.Exp, scale=-1.0),
             excl=[("ps", bb)], writes=[("enb", pr)])
        P.op("dve", lambda e: e.tensor_tensor(out=qg[pr][:], in0=qin[gi][:, :, cs], in1=eb[pr][:], op=ALU.mult),
             reads=[("qin", gi), ("eb", pr)], writes=[("qg", pr)])
        P.op("pool", lambda e: e.tensor_tensor(out=kg[pr][:], in0=kin[gi][:, :, cs], in1=enb[pr][:], op=ALU.mult),
             reads=[("kin", gi), ("enb", pr)], writes=[("kg", pr)])
        P.op("pe", lambda e: e.matmul(ps[b2][:, 0:256], lhsT=cm[:, 1, :], rhs=lain[gi][:, tl, :], start=True, stop=True),
             reads=[("lain", gi), "cm"], excl=[("ps", b2)])
        P.op("act", lambda e: e.activation(out=ekd[pr][:], in_=ps[b2][:, 0:256], func=AF.Exp),
             excl=[("ps", b2)], writes=[("ekd", pr)])
        P.op("pool", lambda e: e.tensor_tensor(out=kd[pr][:], in0=ktin[gi][:, tl, :], in1=ekd[pr][:], op=ALU.mult),
             reads=[("ktin", gi), ("ekd", pr)], writes=[("kd", pr)])
        for kc in range(2):
            P.op("pe", lambda e, kc=kc: e.matmul(ps[b2][:, 256:384], lhsT=kg[pr][:, kc, :], rhs=qg[pr][:, kc, :],
                                                start=(kc == 0), stop=(kc == 1)),
                 reads=[("kg", pr), ("qg", pr)], excl=[("ps", b2)])
        P.op("dve", lambda e: e.tensor_tensor(out=attm[pr][:], in0=ps[b2][:, 256:384], in1=cm[:, 0, :], op=ALU.mult),
             reads=["cm"], excl=[("ps", b2)], writes=[("attm", pr)])

    def chain(tt):
        g, tl = divmod(tt, GT)
        gi = g % 2
        pr = tt % 2
        for s in range(2):
            rs = slice(s * 64, (s + 1) * 64)
            ob = 3 + s
            ci = 2 * tt + s
            cur, nxt = ci % 2, (ci + 1) % 2
            ub = 5 if ci % 2 == 0 else 2
            for kc in range(2):
                P.op("pe", lambda e, kc=kc, rs=rs, ub=ub: e.matmul(
                    ps[ub][:, kc * 256:(kc + 1) * 256], lhsT=kd[pr][rs, kc * 128:(kc + 1) * 128],
                    rhs=vtin[gi][rs, tl, :], start=True, stop=True),
                    reads=[("kd", pr), ("vtin", gi)], excl=[("ps", ub)])
            P.op("pe", lambda e, rs=rs, ob=ob: e.matmul(ps[ob][0:64, 0:256], lhsT=attm[pr][rs, rs], rhs=vtin[gi][rs, tl, :],
                                                       start=True, stop=False),
                 reads=[("attm", pr), ("vtin", gi)], excl=[("ps", ob)])
            for kc in range(2):
                P.op("pe", lambda e, kc=kc, rs=rs, ob=ob, cur=cur: e.matmul(
                    ps[ob][0:64, 0:256], lhsT=qg[pr][:, kc, rs], rhs=Sb[cur][:, kc, :], start=False, stop=(kc == 1)),
                    reads=[("qg", pr), ("Sb", cur, kc)], excl=[("ps", ob)])
            P.op("act", lambda e, s=s, ob=ob: e.activation(out=osb[pr][:, s, :], in_=ps[ob][0:64, 0:256], func=AF.Copy),
                 excl=[("ps", ob)], writes=[("osb", pr)])
            for kc in range(2):
                P.op("dve", lambda e, kc=kc, s=s, ub=ub, nxt=nxt: e.scalar_tensor_tensor(
                    out=Sb[nxt][:, kc, :], in0=S[:, kc, :], scalar=eb[pr][:, kc, s * 64 + 63:s * 64 + 64],
                    in1=ps[ub][:, kc * 256:(kc + 1) * 256], op0=ALU.mult, op1=ALU.add),
                    reads=[("S", kc), ("eb", pr)], writes=[("Sb", nxt, kc)], excl=[("ps", ub)])
            for kc in range(2):
                P.op("dve", lambda e, kc=kc, s=s, ub=ub: e.scalar_tensor_tensor(
                    out=S[:, kc, :], in0=S[:, kc, :], scalar=eb[pr][:, kc, s * 64 + 63:s * 64 + 64],
                    in1=ps[ub][:, kc * 256:(kc + 1) * 256], op0=ALU.mult, op1=ALU.add),
                    reads=[("S", kc), ("eb", pr)], writes=[("S", kc)], excl=[("ps", ub)])
        P.dma("sp" if tt % 2 == 0 else "act",
              lambda e: e.dma_start(out=od[tt * 128:(tt + 1) * 128, :].rearrange("(s i) v -> i s v", s=2), in_=osb[pr][:]),
              reads=[("osb", pr)])

    load_group(0)
    precompute(0)
    if debug:
        dbg = {n: C.dout("dbg_" + n, shp, dt) for n, shp, dt in (("eb", [128, 256], F32), ("qg", [128, 256], BF16),
               ("kg", [128, 256], BF16), ("kd", [128, 256], BF16), ("attm", [128, 128], BF16), ("ekd", [128, 256], F32))}
        P.dma("sp", lambda e: e.dma_start(out=dbg["eb"], in_=eb[0][:].rearrange("p a b -> p (a b)")), reads=[("eb", 0)])
        P.dma("sp", lambda e: e.dma_start(out=dbg["qg"], in_=qg[0][:].rearrange("p a b -> p (a b)")), reads=[("qg", 0)])
        P.dma("sp", lambda e: e.dma_start(out=dbg["kg"], in_=kg[0][:].rearrange("p a b -> p (a b)")), reads=[("kg", 0)])
        P.dma("sp", lambda e: e.dma_start(out=dbg["kd"], in_=kd[0][:]), reads=[("kd", 0)])
        P.dma("sp", lambda e: e.dma_start(out=dbg["attm"], in_=attm[0][:]), reads=[("attm", 0)])
        P.dma("sp", lambda e: e.dma_start(out=dbg["ekd"], in_=ekd[0][:]), reads=[("ekd", 0)])
    for tt in range(NT if not debug else 2):
        g, tl = divmod(tt, GT)
        if tl == 0 and g + 1 < NGRP:
            load_group(g + 1)
        if tt + 1 < NT:
            precompute(tt + 1)
        chain(tt)
    P.finish()
    P.emit()
    return nc


HPC = FOX_HEADS // NCORE


def build_attn():
    C = Ctx("attn")
    nc, P = C.nc, C.P
    L = LPAD
    qf = C.din("qf", [HPC, 128, L], BF16)
    kf = C.din("kf", [HPC, 128, L], BF16)
    vtm = C.din("vtm", [HPC, L, 128], BF16)
    lft = C.din("lft", [HPC, 128, NT], F32)
    cmd = C.din("cm", [128, 4, 128], F32)
    lmd = C.din("lm", [NT, NT], F32)
    pmd = C.din("pm", [3, 3], F32)
    od = C.dout("o", [HPC, L, 128], F32)
    scr = [nc.dram_tensor("cscr%d" % h, [NT, 128], F32, kind="Internal").ap() for h in range(HPC)]
    ps = C.ps

    cm = C.sb("cm_sb", [128, 4, 128], F32)
    lm = C.sb("lm_sb", [NT, NT], F32)
    pm = C.sb("pm_sb", [3, 3], F32)
    ibf = C.sb("ibf", [128, 128], BF16)
    mneg = C.sb("mneg", [128, 128], BF16)
    ones3 = C.sb("ones3", [3, 128], BF16)
    qh = C.sb("qh", [128, L], BF16)
    kh = C.sb("kh", [128, L], BF16)
    vaug = C.sb("vaug", [128, NT, 129], BF16)
    lfs = C.sb("lfs", [128, NT], F32)
    asb = C.sb("asb", [NT, 128], F32)
    csb = C.sb("csb", [128, NT], F32)
    negc = C.sb("negc", [128, NT], F32)
    ctsb = C.sb("ctsb", [NT, 128], F32)
    cqf = C.sb("cqf", [3, L], F32)
    cr1 = C.sb("cr1", [3, L], F32)
    chi = C.sb("chi", [3, L], BF16)
    cq3 = C.sb("cq3", [3, L], BF16)
    pt = [C.sb("pt%d" % i, [128, 512], BF16) for i in range(3)]
    osb = [C.sb("osb%d" % i, [128, 4, 128], F32) for i in range(2)]
    rec = [C.sb("rec%d" % i, [128, 4], F32) for i in range(2)]

    P.dma("sp", lambda e: e.dma_start(out=cm[:], in_=cmd), writes=["cm"])
    P.dma("sp", lambda e: e.dma_start(out=lm[:], in_=lmd), writes=["lm"])
    P.dma("sp", lambda e: e.dma_start(out=pm[:], in_=pmd), writes=["pm"])
    P.op("dve", lambda e: e.tensor_copy(out=ibf[:], in_=cm[:, 2, :]), reads=["cm"], writes=["ibf"])
    P.op("dve", lambda e: e.tensor_copy(out=mneg[:], in_=cm[:, 3, :]), reads=["cm"], writes=["mneg"])
    P.op("pool", lambda e: e.memset(ones3[:], 1.0), writes=["ones3"])

    cnt = {"sb": 0, "pb": 0, "ob": 0}
    for h in range(HPC):
        P.dma("sp", lambda e, h=h: e.dma_start(out=qh[:], in_=qf[h]), writes=["qh"])
        P.dma("act", lambda e, h=h: e.dma_start(out=kh[:], in_=kf[h]), writes=["kh"])
        P.op("pool", lambda e: e.memset(vaug[:], 1.0), writes=["vaug"])
        P.dma("sp", lambda e, h=h: e.dma_start(out=vaug[:, :, 0:128], in_=vtm[h].rearrange("(t p) d -> p t d", p=128)),
              writes=["vaug"])
        P.dma("act", lambda e, h=h: e.dma_start(out=lfs[:], in_=lft[h]), writes=["lfs"])
        P.op("pe", lambda e: e.matmul(ps[6][0:NT, 0:128], lhsT=lfs[:], rhs=cm[:, 1, :], start=True, stop=True),
             reads=["lfs", "cm"], excl=[("ps", 6)])
        P.op("act", lambda e: e.activation(out=asb[:], in_=ps[6][0:NT, 0:128], func=AF.Copy), excl=[("ps", 6)], writes=["asb"])
        P.op("pe", lambda e: e.matmul(ps[7][:, 0:NT], lhsT=cm[:, 0, :], rhs=lfs[:], start=True, stop=False),
             reads=["lfs", "cm"], excl=[("ps", 7)])
        P.op("pe", lambda e: e.matmul(ps[7][:, 0:NT], lhsT=asb[:], rhs=lm[:], start=False, stop=True),
             reads=["asb", "lm"], excl=[("ps", 7)])
        P.op("dve", lambda e: e.tensor_copy(out=csb[:], in_=ps[7][:, 0:NT]), excl=[("ps", 7)], writes=["csb"])
        P.op("dve", lambda e: e.tensor_scalar(out=negc[:], in0=csb[:], scalar1=-1.0, scalar2=None, op0=ALU.mult),
             reads=["csb"], writes=["negc"])
        P.op("pe", lambda e: e.transpose(ps[6][0:NT, 0:128], csb[:], cm[:, 2, :]), reads=["csb", "cm"], excl=[("ps", 6)])
        P.op("act", lambda e: e.activation(out=ctsb[:], in_=ps[6][0:NT, 0:128], func=AF.Copy), excl=[("ps", 6)], writes=["ctsb"])
        P.dma("sp", lambda e, h=h: e.dma_start(out=scr[h], in_=ctsb[:]), reads=["ctsb"], writes=[("scr", h)])
        for r in range(3):
            P.dma("sp" if r != 1 else "act",
                  lambda e, h=h, r=r: e.dma_start(out=cqf[r:r + 1, :], in_=scr[h].rearrange("(o t) j -> o (t j)", o=1)),
                  reads=[("scr", h)], writes=["cqf"])
        P.op("dve", lambda e: e.tensor_copy(out=chi[:], in_=cqf[:]), reads=["cqf"], writes=["chi"])
        P.op("dve", lambda e: e.tensor_scalar(out=cq3[:], in0=chi[:], scalar1=pm[:, 0:1], scalar2=None, op0=ALU.mult),
             reads=["chi", "pm"], writes=["cq3"])
        P.op("dve", lambda e: e.tensor_tensor(out=cr1[:], in0=cqf[:], in1=chi[:], op=ALU.subtract),
             reads=["cqf", "chi"], writes=["cr1"])
        P.op("dve", lambda e: e.tensor_copy(out=chi[:], in_=cr1[:]), reads=["cr1"], writes=["chi"])
        P.op("dve", lambda e: e.scalar_tensor_tensor(out=cq3[:], in0=chi[:], scalar=pm[:, 1:2], in1=cq3[:],
                                                     op0=ALU.mult, op1=ALU.add), reads=["chi", "pm", "cq3"], writes=["cq3"])
        P.op("dve", lambda e: e.tensor_tensor(out=cr1[:], in0=cr1[:], in1=chi[:], op=ALU.subtract),
             reads=["cr1", "chi"], writes=["cr1"])
        P.op("dve", lambda e: e.tensor_copy(out=chi[:], in_=cr1[:]), reads=["cr1"], writes=["chi"])
        P.op("dve", lambda e: e.scalar_tensor_tensor(out=cq3[:], in0=chi[:], scalar=pm[:, 2:3], in1=cq3[:],
                                                     op0=ALU.mult, op1=ALU.add), reads=["chi", "pm", "cq3"], writes=["cq3"])
        steps = []
        for q0 in range(0, NT, 4):
            nq = min(4, NT - q0)
            for kt in range(q0 + nq):
                steps.append((q0, nq, kt))
        LA = 2
        stb = [0, 1, 6]
        info = {}

        def emit_st(i):
            q0, nq, kt = steps[i]
            r0 = max(0, kt - q0)
            n = (nq - r0) * 128
            qa, qb = (q0 + r0) * 128, (q0 + nq) * 128
            sbk = stb[cnt["sb"] % 3]
            cnt["sb"] += 1
            diag = kt >= q0
            info[i] = (sbk, r0, n)
            P.op("pe", lambda e: e.matmul(ps[sbk][:, 0:n], lhsT=kh[:, kt * 128:(kt + 1) * 128], rhs=qh[:, qa:qb],
                                          start=True, stop=False), reads=["kh", "qh"], excl=[("ps", sbk)])
            P.op("pe", lambda e: e.matmul(ps[sbk][:, 0:n], lhsT=ones3[:], rhs=cq3[:, qa:qb], start=False, stop=(not diag)),
                 reads=["ones3", "cq3"], excl=[("ps", sbk)])
            if diag:
                P.op("pe", lambda e: e.matmul(ps[sbk][:, 0:128], lhsT=ibf[:], rhs=mneg[:], start=False, stop=True),
                     reads=["ibf", "mneg"], excl=[("ps", sbk)])

        def emit_exp_pv(i):
            q0, nq, kt = steps[i]
            sbk, r0, n = info[i]
            pb = cnt["pb"] % 3
            cnt["pb"] += 1
            P.op("act", lambda e: e.activation(out=pt[pb][:, 0:n], in_=ps[sbk][:, 0:n], func=AF.Exp,
                                               bias=negc[:, kt:kt + 1], scale=1.0),
                 reads=["negc"], excl=[("ps", sbk)], writes=[("pt", pb)])
            for r in range(r0, nq):
                qt = q0 + r
                P.op("pe", lambda e, r=r, qt=qt: e.matmul(
                    ps[2 + r][:, 0:129], lhsT=pt[pb][:, (r - r0) * 128:(r - r0 + 1) * 128], rhs=vaug[:, kt, :],
                    start=(kt == 0), stop=(kt == qt)),
                    reads=[("pt", pb), "vaug"], excl=[("ps", 2 + r)])

        def finalize(q0, nq):
            oi = cnt["ob"] % 2
            cnt["ob"] += 1
            for r in range(nq):
                P.op("dve", lambda e, r=r: e.reciprocal(out=rec[oi][:, r:r + 1], in_=ps[2 + r][:, 128:129]),
                     excl=[("ps", 2 + r)], writes=[("rec", oi)])
                P.op("dve", lambda e, r=r: e.tensor_scalar(out=osb[oi][:, r, :], in0=ps[2 + r][:, 0:128],
                                                          scalar1=rec[oi][:, r:r + 1], scalar2=None, op0=ALU.mult),
                     reads=[("rec", oi)], excl=[("ps", 2 + r)], writes=[("osb", oi)])
            P.dma("sp" if oi == 0 else "act",
                  lambda e, h=h: e.dma_start(
                      out=od[h, q0 * 128:(q0 + nq) * 128, :].rearrange("(r i) d -> i r d", i=128), in_=osb[oi][:, 0:nq, :]),
                  reads=[("osb", oi)])

        nst = len(steps)
        for i in range(min(LA, nst)):
            emit_st(i)
        for i in range(nst):
            if i + LA < nst:
                emit_st(i + LA)
            emit_exp_pv(i)
            q0, nq, kt = steps[i]
            if kt == q0 + nq - 1:
                finalize(q0, nq)
    P.finish()
    P.emit()
    return nc


def attn_consts():
    ii = np.arange(128)
    cm = np.zeros((128, 4, 128), np.float32)
    cm[:, 0, :] = (ii[:, None] <= ii[None, :])
    cm[:, 1, :] = 1.0
    cm[:, 2, :] = np.eye(128)
    cm[:, 3, :] = np.where(ii[:, None] > ii[None, :], NEG, 0.0)
    tt = np.arange(NT)
    lm = (tt[:, None] < tt[None, :]).astype(np.float32)
    return {"cm": cm, "lm": lm, "pm": np.eye(3, dtype=np.float32)}


_PROGS = {}


def _prog(name):
    if name not in _PROGS:
        if name.startswith("pre_"):
            _PROGS[name] = build_pre(name[4:])
        elif name.startswith("post_"):
            _PROGS[name] = build_post(name[5:])
        elif name == "scan":
            _PROGS[name] = build_scan()
        elif name == "attn":
            _PROGS[name] = build_attn()
    return _PROGS[name]


def _run(name, in_maps):
    res = run_bass_kernel_spmd(_prog(name), in_maps, core_ids=list(range(NCORE)))
    return res.results


def _fm(vec, nch):
    return np.ascontiguousarray(np.asarray(vec, np.float32).reshape(nch, 128).T)


def _cat_tokens(results, key, dtype):
    full = np.concatenate([np.asarray(r[key]) for r in results], axis=1)
    out = np.zeros((full.shape[0], LPAD), dtype)
    out[:, :NTOK] = full
    return out


def _scan_consts():
    ii = np.arange(128)
    same = (ii[:, None] // 64) == (ii[None, :] // 64)
    return np.stack([((ii[:, None] <= ii[None, :]) & same), ((ii[:, None] > ii[None, :]) & same)],
                    axis=1).astype(np.float32)


def _post_inputs(h, o_full, g_fm, c):
    lo = c * TC - HALO
    def tok_major(a):
        if lo < 0:
            blk = np.concatenate([np.zeros((HALO, D), np.float32), a[0:TC]], axis=0)
        else:
            blk = a[lo:lo + TP]
        return np.ascontiguousarray(blk.T)
    if lo < 0:
        g = np.concatenate([np.zeros((D, HALO), np.float32), g_fm[:, 0:TC]], axis=1)
    else:
        g = g_fm[:, lo:lo + TP]
    vm = np.ones((128, TP), np.float32)
    if c == 0:
        vm[:, 0:HALO] = 0.0
    return tok_major(h), tok_major(o_full), np.ascontiguousarray(g, dtype=np.float32), vm


def kernel(x, meta, ln_g, ln_b, gla_w_in, gla_w_g2, gla_b_g2, gla_norm_g, gla_w_out,
           kv_w, kv_bf, fox_w_in, fox_w_out, ffn_w_up, ffn_conv_w, ffn_conv_b, ffn_w_down):
    f32 = np.float32
    x = np.asarray(x, f32)
    h = np.concatenate([np.asarray(meta, f32), x[0]], axis=0)
    ln_g, ln_b = np.asarray(ln_g, f32), np.asarray(ln_b, f32)
    kF = vF = lfF = None
    for l in range(DEPTH):
        hts = [np.ascontiguousarray(h[c * TC:(c + 1) * TC].T) for c in range(NCORE)]
        if l < 2:
            w_in = np.asarray(gla_w_in[l], f32)
            wg2 = np.asarray(gla_w_g2[l], f32)
            bg2 = _fm(gla_b_g2[l], 8)
            res = _run("pre_gla", [{"ht": hts[c], "w": w_in, "wg2": wg2, "bg2": bg2} for c in range(NCORE)])
            qF = _cat_tokens(res, "q", NPBF)
            kFg = _cat_tokens(res, "k", NPBF)
            vFg = _cat_tokens(res, "v", NPBF)
            laF = _cat_tokens(res, "la", f32)
            rF = _cat_tokens(res, "r", f32)
            cmc = _scan_consts()
            ims = []
            for u in range(NCORE):
                hd, vh = divmod(u, 2)
                ks = kFg[hd * 256:(hd + 1) * 256]
                ims.append({"qf": np.ascontiguousarray(qF[hd * 256:(hd + 1) * 256]), "kf": np.ascontiguousarray(ks),
                            "ktm": np.ascontiguousarray(ks.T),
                            "vtm": np.ascontiguousarray(vFg[hd * 512 + vh * 256:hd * 512 + (vh + 1) * 256].T),
                            "latm": np.ascontiguousarray(laF[hd * 256:(hd + 1) * 256].T), "cm": cmc})
            res = _run("scan", ims)
            o_full = np.zeros((NTOK, D), f32)
            for u in range(NCORE):
                hd, vh = divmod(u, 2)
                o_full[:, hd * 512 + vh * 256:hd * 512 + (vh + 1) * 256] = np.asarray(res[u]["o"])[:NTOK]
            g_fm = rF
            pname = "post_gla"
            w_out = np.asarray(gla_w_out[l], f32)
        else:
            j = l - 2
            w_in = np.asarray(fox_w_in[j], f32)
            if l == 2:
                wkv = np.asarray(kv_w, f32)
                bfv = np.asarray(kv_bf, f32).reshape(FOX_HEADS, 1)
                res = _run("pre_fox_kv", [{"ht": hts[c], "w": w_in, "wkv": wkv, "bf": bfv} for c in range(NCORE)])
                kF = _cat_tokens(res, "kk", NPBF)
                vF = _cat_tokens(res, "vv", NPBF)
                lfF = _cat_tokens(res, "lf", f32)
            else:
                res = _run("pre_fox", [{"ht": hts[c], "w": w_in} for c in range(NCORE)])
            qF = _cat_tokens(res, "q", NPBF)
            ogF = _cat_tokens(res, "og", f32)
            ac = attn_consts()
            ims = []
            for c in range(NCORE):
                rows = slice(c * HPC * 128, (c + 1) * HPC * 128)
                im = dict(ac)
                im["qf"] = np.ascontiguousarray(qF[rows].reshape(HPC, 128, LPAD))
                im["kf"] = np.ascontiguousarray(kF[rows].reshape(HPC, 128, LPAD))
                im["vtm"] = np.ascontiguousarray(vF[rows].reshape(HPC, 128, LPAD).transpose(0, 2, 1))
                im["lft"] = np.ascontiguousarray(lfF[c * HPC:(c + 1) * HPC].reshape(HPC, NT, 128).transpose(0, 2, 1))
                ims.append(im)
            res = _run("attn", ims)
            o_full = np.zeros((NTOK, D), f32)
            for c in range(NCORE):
                o = np.asarray(res[c]["o"])
                for hh in range(HPC):
                    o_full[:, (c * HPC + hh) * 128:(c * HPC + hh + 1) * 128] = o[hh, :NTOK]
            g_fm = ogF
            pname = "post_fox"
            w_out = np.asarray(fox_w_out[j], f32)
        lnp = np.ascontiguousarray(np.stack([_fm(ln_g[l, 0], 16), _fm(ln_b[l, 0], 16), _fm(ln_g[l, 1], 16),
                                             _fm(ln_b[l, 1], 16)], axis=1))
        cw = np.asarray(ffn_conv_w[l], f32)
        cb = np.asarray(ffn_conv_b[l], f32)
        cwl = np.ascontiguousarray(np.concatenate([cw.reshape(3, 88, 128).transpose(2, 1, 0),
                                                   cb.reshape(88, 128).T[:, :, None]], axis=2))
        wup = np.asarray(ffn_w_up[l], f32)
        wdn = np.asarray(ffn_w_down[l], f32)
        ims = []
        for c in range(NCORE):
            ht_, ot_, gt_, vm_ = _post_inputs(h, o_full, g_fm, c)
            im = {"ht": ht_, "ot": ot_, "gt": gt_, "vmask": vm_, "wo": w_out, "lnp": lnp, "wup": wup, "cw": cwl, "wdn": wdn}
            if pname == "post_gla":
                im["ng"] = _fm(gla_norm_g[l], 4)
            ims.append(im)
        res = _run(pname, ims)
        h = np.concatenate([np.asarray(res[c]["hout"]).T for c in range(NCORE)], axis=0)
    return np.ascontiguousarray(h[N_META:].reshape(1, SEQ, D).astype(f32))
```

```python
import contextlib
import numpy as np
import ml_dtypes
import concourse.bass as bass
import concourse.mybir as mybir
from concourse.bass_utils import run_bass_kernel_spmd

F32 = mybir.dt.float32
BF16 = mybir.dt.bfloat16
AF = mybir.ActivationFunctionType
ALU = mybir.AluOpType
NPBF = ml_dtypes.bfloat16

D = 2048
SEQ = 8192
DEPTH = 4
N_META = 16
NTOK = SEQ + N_META
NCORE = 8
TC = NTOK // NCORE
HALO = 2
TP = TC + HALO
LPAD = 8320
NT = LPAD // 128
ALPHA = (2.0 * DEPTH) ** 0.25
LN_EPS = 1e-5
GLA_HEADS, GLA_HK, GLA_HV, GLA_RANK, GLA_TAU = 4, 256, 512, 16, 16.0
GLA_DK, GLA_DV = 1024, 2048
FOX_HEADS, FOX_HD = 16, 128
D_FF = 5632
NEG = -30000.0

ENGS = ("pe", "act", "dve", "pool", "sp")
NDS = 8


class Prog:
    def __init__(self, nc):
        self.nc = nc
        self.ops = {e: [] for e in ENGS}
        self.cnt = {e: 0 for e in ENGS}
        self.dcnt = {e: 0 for e in ENGS}
        self.lastw = {}
        self.readers = {}
        self.known = {e: {} for e in ENGS}
        self.dknown = {e: {q: set() for q in ENGS} for e in ENGS}
        self.dmax = {e: {q: 0 for q in ENGS} for e in ENGS}

    def _deps(self, eng, reads, writes):
        deps = {}
        ddeps = set()

        def need(ev):
            if ev is None:
                return
            kind, e, idx = ev
            if kind == "d":
                ddeps.add((e, idx))
                return
            if e == eng == "pe":
                return
            if deps.get(e, 0) < idx:
                deps[e] = idx

        for t in reads:
            need(self.lastw.get(t))
        for t in writes:
            need(self.lastw.get(t))
            for r in self.readers.get(t, ()):
                need(r)
        out = []
        kn = self.known[eng]
        for e, idx in deps.items():
            if kn.get(e, 0) >= idx:
                continue
            kn[e] = idx
            out.append(("c", e, idx))
        for (q, idx) in sorted(ddeps):
            if idx in self.dknown[eng][q] or idx <= self.dmax[eng][q] - NDS:
                continue
            self.dknown[eng][q].add(idx)
            if idx > self.dmax[eng][q]:
                self.dmax[eng][q] = idx
            out.append(("d", q, idx))
        return out

    def _commit(self, ev, reads, writes):
        for t in reads:
            self.readers.setdefault(t, []).append(ev)
        for t in writes:
            self.lastw[t] = ev
            self.readers[t] = []

    def op(self, eng, fn, reads=(), writes=(), excl=()):
        if excl:
            reads = list(reads) + list(excl)
            writes = list(writes) + list(excl)
        waits = self._deps(eng, reads, writes)
        self.cnt[eng] += 1
        self.ops[eng].append(("c", fn, waits, 0))
        self._commit(("c", eng, self.cnt[eng]), reads, writes)

    def dma(self, eng, fn, reads=(), writes=()):
        waits = self._deps(eng, reads, writes)
        self.dcnt[eng] += 1
        i = self.dcnt[eng]
        if i > NDS:
            prev = i - NDS
            if not (prev in self.dknown[eng][eng] or prev <= self.dmax[eng][eng] - NDS):
                self.dknown[eng][eng].add(prev)
                self.dmax[eng][eng] = max(self.dmax[eng][eng], prev)
                waits.append(("d", eng, prev))
        self.ops[eng].append(("d", fn, waits, i))
        self._commit(("d", eng, i), reads, writes)

    def finish(self, eng="sp"):
        waits = []
        for e in ENGS:
            for i in range(max(1, self.dcnt[e] - NDS + 1), self.dcnt[e] + 1):
                waits.append(("d", e, i))
            if self.cnt[e]:
                waits.append(("c", e, self.cnt[e]))
        self.ops[eng].append(("w", None, waits, 0))

    def emit(self):
        nc = self.nc
        with contextlib.ExitStack() as st:
            csem = {e: st.enter_context(nc.semaphore("c_" + e)) for e in ENGS}
            dsem = {e: [st.enter_context(nc.semaphore("d_%s%d" % (e, k))) for k in range(NDS)]
                    for e in ("sp", "act", "pool")}
            block = st.enter_context(nc.Block())

            def run(engname):
                def body(eh):
                    for kind, fn, waits, di in self.ops[engname]:
                        for (k, e, idx) in waits:
                            if k == "c":
                                eh.wait_ge(csem[e], idx)
                            else:
                                eh.wait_ge(dsem[e][(idx - 1) % NDS], 16 * ((idx - 1) // NDS + 1))
                        if kind == "c":
                            fn(eh).then_inc(csem[engname], 1)
                        elif kind == "d":
                            fn(eh).then_inc(dsem[engname][(di - 1) % NDS], 16)
                return body

            block.sync(run("sp"))
            block.scalar(run("act"))
            block.vector(run("dve"))
            block.gpsimd(run("pool"))
            block.tensor(run("pe"))


def ntiles(total, maxn=512):
    n = -(-total // maxn)
    base = -(-total // n)
    out = []
    s = 0
    while s < total:
        e = min(total, s + base)
        out.append((s, e))
        s = e
    return out


class Ctx:
    def __init__(self, name):
        self.nc = bass.Bass("TRN2", target_bir_lowering=False)
        self.P = Prog(self.nc)
        self.psbig = self.nc.alloc_psum_tensor("psbig", [128, 4096], F32)
        self.ps = [self.psbig[:, b * 512:(b + 1) * 512] for b in range(8)]
        self.uid = 0

    def sb(self, name, shape, dt):
        return self.nc.alloc_sbuf_tensor(name, list(shape), dt)

    def din(self, name, shape, dt):
        return self.nc.dram_tensor(name, list(shape), dt, kind="ExternalInput").ap()

    def dout(self, name, shape, dt):
        return self.nc.dram_tensor(name, list(shape), dt, kind="ExternalOutput").ap()


def build_pre(kind):
    C = Ctx("pre_" + kind)
    nc, P = C.nc, C.P
    T = TC
    ht = C.din("ht", [D, T], F32)
    if kind == "gla":
        dout_w = 2 * GLA_DK + GLA_DV + GLA_RANK + GLA_DV
        w = C.din("w", [D, dout_w], F32)
        wg2 = C.din("wg2", [GLA_RANK, GLA_DK], F32)
        bg2 = C.din("bg2", [128, GLA_DK // 128], F32)
        segs = [("g", 2 * GLA_DK + GLA_DV, 16, "glow", None),
                ("q", 0, GLA_DK, "scale", BF16), ("k", GLA_DK, GLA_DK, "copy", BF16),
                ("v", 2 * GLA_DK, GLA_DV, "copy", BF16),
                ("r", 2 * GLA_DK + GLA_DV + GLA_RANK, GLA_DV, "copy", F32)]
        qscale = GLA_HK ** -0.5
        outs = {"q": C.dout("q", [GLA_DK, T], BF16), "k": C.dout("k", [GLA_DK, T], BF16),
                "v": C.dout("v", [GLA_DV, T], BF16), "r": C.dout("r", [GLA_DV, T], F32),
                "la": C.dout("la", [GLA_DK, T], F32)}
    else:
        wq = C.din("w", [D, 2 * D], F32)
        segs = [("q", 0, D, "scale", BF16), ("og", D, D, "copy", F32)]
        qscale = FOX_HD ** -0.5
        outs = {"q": C.dout("q", [D, T], BF16), "og": C.dout("og", [D, T], F32)}
        if kind == "fox_kv":
            wkv = C.din("wkv", [D, 2 * D + FOX_HEADS], F32)
            bf = C.din("bf", [FOX_HEADS, 1], F32)
            outs.update({"kk": C.dout("kk", [D, T], BF16), "vv": C.dout("vv", [D, T], BF16),
                         "lf": C.dout("lf", [FOX_HEADS, T], F32)})

    hs = C.sb("hs", [128, 16, T], F32)
    hb = C.sb("hb", [128, 16, T], BF16)
    NW = 4
    wt = [C.sb("wt%d" % i, [128, 16, 128], BF16) for i in range(NW)]
    NS = 3
    stf = [C.sb("stf%d" % i, [128, T], F32) for i in range(NS)]
    stb = [C.sb("stb%d" % i, [128, T], BF16) for i in range(NS)]
    tiles = ntiles(T)

    for c in range(16):
        q = "sp" if c % 2 == 0 else "act"
        P.dma(q, lambda e, c=c: e.dma_start(out=hs[:, c, :], in_=ht[c * 128:(c + 1) * 128, :]),
              writes=[("hs", c)])
        if c % 2 == 0:
            P.op("dve", lambda e, c=c: e.tensor_copy(out=hb[:, c, :], in_=hs[:, c, :]),
                 reads=[("hs", c)], writes=[("hb", c)])
        else:
            P.op("act", lambda e, c=c: e.activation(out=hb[:, c, :], in_=hs[:, c, :], func=AF.Copy),
                 reads=[("hs", c)], writes=[("hb", c)])
    hb_tok = [("hb", c) for c in range(16)]

    state = {"wi": 0, "si": 0, "bank": 0}

    def proj_chunk(wsrc, col0, ncols, epilogue):
        wi = state["wi"] % NW
        state["wi"] += 1
        P.dma("pool", lambda e: e.dma_start(
            out=wt[wi][:, :, 0:ncols],
            in_=wsrc[:, col0:col0 + ncols].rearrange("(kc p) j -> p kc j", p=128)),
            writes=[("wt", wi)])
        for (t0, t1) in tiles:
            b = state["bank"] % 4
            state["bank"] += 1
            for kc in range(16):
                P.op("pe", lambda e, kc=kc, b=b, t0=t0, t1=t1: e.matmul(
                    C.ps[b][0:ncols, 0:t1 - t0], lhsT=wt[wi][:, kc, 0:ncols], rhs=hb[:, kc, t0:t1],
                    start=(kc == 0), stop=(kc == 15)),
                    reads=[("wt", wi), ("hb", kc)], excl=[("ps", b)])
            epilogue(b, t0, t1)

    if kind == "gla":
        gl = C.sb("gl", [GLA_RANK, T], BF16)
        wg2f = C.sb("wg2f", [GLA_RANK, GLA_DK], F32)
        wg2b = C.sb("wg2b", [GLA_RANK, GLA_DK], BF16)
        bg = C.sb("bg", [128, 8], F32)
        nbg = C.sb("nbg", [128, 8], F32)
        P.dma("sp", lambda e: e.dma_start(out=wg2f[:], in_=wg2), writes=["wg2f"])
        P.dma("sp", lambda e: e.dma_start(out=bg[:], in_=bg2), writes=["bg"])
        P.op("dve", lambda e: e.tensor_copy(out=wg2b[:], in_=wg2f[:]), reads=["wg2f"], writes=["wg2b"])
        P.op("dve", lambda e: e.tensor_scalar(out=nbg[:], in0=bg[:], scalar1=-1.0, scalar2=None, op0=ALU.mult),
             reads=["bg"], writes=["nbg"])

        def ep_glow(b, t0, t1):
            P.op("act", lambda e: e.activation(out=gl[:, t0:t1], in_=C.ps[b][0:GLA_RANK, 0:t1 - t0], func=AF.Copy),
                 excl=[("ps", b)], writes=[("gl", t0)])
        proj_chunk(w, 2 * GLA_DK + GLA_DV, GLA_RANK, ep_glow)
        for fc in range(8):
            si = state["si"] % NS
            state["si"] += 1
            for (t0, t1) in tiles:
                b = 4 + (state["bank"] % 2)
                state["bank"] += 1
                P.op("pe", lambda e, fc=fc, b=b, t0=t0, t1=t1: e.matmul(
                    C.ps[b][:, 0:t1 - t0], lhsT=wg2b[:, fc * 128:(fc + 1) * 128], rhs=gl[:, t0:t1],
                    start=True, stop=True), reads=["wg2b", ("gl", t0)], excl=[("ps", b)])
                P.op("act", lambda e, fc=fc, b=b, t0=t0, t1=t1, si=si: e.activation(
                    out=stf[si][:, t0:t1], in_=C.ps[b][:, 0:t1 - t0], func=AF.Exp,
                    bias=nbg[:, fc:fc + 1], scale=-1.0),
                    reads=["nbg"], excl=[("ps", b)], writes=[("stf", si)])
            P.op("act", lambda e, si=si: e.activation(out=stf[si][:], in_=stf[si][:], func=AF.Ln, bias=1.0, scale=1.0),
                 reads=[("stf", si)], writes=[("stf", si)])
            P.op("dve", lambda e, si=si: e.tensor_scalar(out=stf[si][:], in0=stf[si][:], scalar1=-1.0 / GLA_TAU,
                                                        scalar2=None, op0=ALU.mult),
                 reads=[("stf", si)], writes=[("stf", si)])
            P.dma("sp", lambda e, fc=fc, si=si: e.dma_start(out=outs["la"][fc * 128:(fc + 1) * 128, :], in_=stf[si][:]),
                  reads=[("stf", si)])

    def run_seg(wsrc, name, col0, ncols, mode, odt, oname=None):
        oname = oname or name
        for ch in range(ncols // 128):
            si = state["si"] % NS
            state["si"] += 1
            dst = stf[si] if odt == F32 else stb[si]
            tok = ("stf", si) if odt == F32 else ("stb", si)

            def ep(b, t0, t1, dst=dst, tok=tok):
                if mode == "scale":
                    P.op("act", lambda e: e.activation(out=dst[:, t0:t1], in_=C.ps[b][:, 0:t1 - t0], func=AF.Copy,
                                                       scale=float(qscale)),
                         excl=[("ps", b)], writes=[tok])
                else:
                    eng = "dve" if (t0 // 8) % 2 == 0 else "act"
                    if eng == "dve":
                        P.op("dve", lambda e: e.tensor_copy(out=dst[:, t0:t1], in_=C.ps[b][:, 0:t1 - t0]),
                             excl=[("ps", b)], writes=[tok])
                    else:
                        P.op("act", lambda e: e.activation(out=dst[:, t0:t1], in_=C.ps[b][:, 0:t1 - t0], func=AF.Copy),
                             excl=[("ps", b)], writes=[tok])
            proj_chunk(wsrc, col0 + ch * 128, 128, ep)
            P.dma("sp" if ch % 2 == 0 else "act",
                  lambda e, ch=ch, dst=dst: e.dma_start(out=outs[oname][ch * 128:(ch + 1) * 128, :], in_=dst[:]),
                  reads=[tok])

    if kind == "gla":
        for (name, col0, ncols, mode, odt) in segs[1:]:
            run_seg(w, name, col0, ncols, mode, odt)
    else:
        for (name, col0, ncols, mode, odt) in segs:
            run_seg(wq, name, col0, ncols, mode, odt)
        if kind == "fox_kv":
            run_seg(wkv, "kk", 0, D, "copy", BF16)
            run_seg(wkv, "vv", D, D, "copy", BF16)
            bfs = C.sb("bfs", [FOX_HEADS, 1], F32)
            nbf = C.sb("nbf", [FOX_HEADS, 1], F32)
            lfs = C.sb("lfs", [FOX_HEADS, T], F32)
            P.dma("sp", lambda e: e.dma_start(out=bfs[:], in_=bf), writes=["bfs"])
            P.op("dve", lambda e: e.tensor_scalar(out=nbf[:], in0=bfs[:], scalar1=-1.0, scalar2=None, op0=ALU.mult),
                 reads=["bfs"], writes=["nbf"])

            def ep_f(b, t0, t1):
                P.op("act", lambda e: e.activation(out=lfs[:, t0:t1], in_=C.ps[b][0:FOX_HEADS, 0:t1 - t0], func=AF.Exp,
                                                   bias=nbf[:, 0:1], scale=-1.0),
                     reads=["nbf"], excl=[("ps", b)], writes=["lfs"])
            proj_chunk(wkv, 2 * D, FOX_HEADS, ep_f)
            P.op("act", lambda e: e.activation(out=lfs[:], in_=lfs[:], func=AF.Ln, bias=1.0, scale=1.0),
                 reads=["lfs"], writes=["lfs"])
            P.op("dve", lambda e: e.tensor_scalar(out=lfs[:], in0=lfs[:], scalar1=-1.0, scalar2=None, op0=ALU.mult),
                 reads=["lfs"], writes=["lfs"])
            P.dma("sp", lambda e: e.dma_start(out=outs["lf"], in_=lfs[:]), reads=["lfs"])

    P.finish()
    P.emit()
    return nc


def build_post(kind):
    C = Ctx("post_" + kind)
    nc, P = C.nc, C.P
    ht = C.din("ht", [D, TP], F32)
    ot = C.din("ot", [D, TP], F32)
    gt = C.din("gt", [D, TP], F32)
    vm = C.din("vmask", [128, TP], F32)
    wo = C.din("wo", [D, D], F32)
    lnp = C.din("lnp", [128, 4, 16], F32)
    wup = C.din("wup", [D, 2 * D_FF], F32)
    cwd = C.din("cw", [128, 88, 4], F32)
    wdn = C.din("wdn", [D_FF, D], F32)
    hout = C.dout("hout", [D, TC], F32)
    if kind == "gla":
        ngd = C.din("ng", [128, 4], F32)

    TW = 343
    TPX = 3 * TW
    hs = C.sb("hs", [128, 16, TP], F32)
    hb = C.sb("hb", [128, 16, TPX + 3], BF16)
    tA = [C.sb("tA%d" % i, [128, TPX + 3], F32) for i in range(2)]
    tB = [C.sb("tB%d" % i, [128, TPX + 3], F32) for i in range(2)]
    tC = [C.sb("tC%d" % i, [128, TPX + 3], F32) for i in range(2)]
    NW = 4
    wt = [C.sb("wt%d" % i, [128, 16, 128], BF16) for i in range(NW)]
    GP = 11
    NG = 4
    act = C.sb("act", [128, GP, TC], BF16)
    NWD = 3
    wd = [C.sb("wd%d" % i, [128, GP, 128], BF16) for i in range(NWD)]
    st = [C.sb("st%d" % i, [128, TP], F32) for i in range(3)]
    ones = C.sb("ones", [128, 128], F32)
    lns = C.sb("lns", [128, 4, 16], F32)
    cws = C.sb("cws", [128, 88, 4], F32)
    vms = C.sb("vms", [128, TP], F32)
    tiles_p = ntiles(TP)
    tiles_c = ntiles(TC)
    cnt = {"w": 0, "wd": 0, "a": 0, "b": 0, "c": 0, "dq": 0}

    def dq():
        cnt["dq"] += 1
        return "sp" if cnt["dq"] % 2 else "act"

    P.op("pool", lambda e: e.memset(ones[:], 1.0), writes=["ones"])
    P.op("pool", lambda e: e.memset(hb[:], 0.0), writes=[("hb", c) for c in range(16)])
    P.dma("sp", lambda e: e.dma_start(out=lns[:], in_=lnp), writes=["lns"])
    P.dma("sp", lambda e: e.dma_start(out=cws[:], in_=cwd), writes=["cws"])
    P.dma("sp", lambda e: e.dma_start(out=vms[:], in_=vm), writes=["vms"])
    if kind == "gla":
        ngs = C.sb("ngs", [128, 4], F32)
        P.dma("sp", lambda e: e.dma_start(out=ngs[:], in_=ngd), writes=["ngs"])
    for c in range(16):
        P.dma(dq(), lambda e, c=c: e.dma_start(out=hs[:, c, :], in_=ht[c * 128:(c + 1) * 128, :]),
              writes=[("hs", c)])

    def load(buf, tokname, src, c):
        i = cnt[tokname] % 2
        cnt[tokname] += 1
        P.dma(dq(), lambda e: e.dma_start(out=buf[i][:, 0:TP], in_=src[c * 128:(c + 1) * 128, :]),
              writes=[(tokname, i)])
        return i

    def rstd_from(src_banks, tiles, scale, dst, dst_tok):
        for ti, (t0, t1) in enumerate(tiles):
            b = src_banks[ti]
            P.op("dve", lambda e, b=b, t0=t0, t1=t1: e.tensor_scalar(
                out=dst[:, t0:t1], in0=C.ps[b][:, 0:t1 - t0], scalar1=float(scale), scalar2=float(LN_EPS),
                op0=ALU.mult, op1=ALU.add), excl=[("ps", b)], writes=[dst_tok])
        P.op("act", lambda e: e.activation(out=dst[:], in_=dst[:], func=AF.Sqrt), reads=[dst_tok],
             writes=[dst_tok])
        P.op("dve", lambda e: e.reciprocal(out=dst[:], in_=dst[:]), reads=[dst_tok],
             writes=[dst_tok])

    if kind == "gla":
        for hd in range(4):
            for cc in range(4):
                c = hd * 4 + cc
                ia = load(tA, "a", ot, c)
                ic = cnt["c"] % 2
                cnt["c"] += 1
                P.op("act", lambda e, ia=ia, ic=ic: e.activation(out=tC[ic][:, 0:TP], in_=tA[ia][:, 0:TP], func=AF.Square),
                     reads=[("a", ia)], writes=[("c", ic)])
                for ti, (t0, t1) in enumerate(tiles_p):
                    P.op("pe", lambda e, ic=ic, ti=ti, t0=t0, t1=t1, cc=cc: e.matmul(
                        C.ps[ti][:, 0:t1 - t0], lhsT=ones[:], rhs=tC[ic][:, t0:t1], start=(cc == 0), stop=(cc == 3)),
                        reads=["ones", ("c", ic)], excl=[("ps", ti)])
            rstd_from([0, 1, 2], tiles_p, 1.0 / GLA_HV, st[0], ("st", 0))
            for cc in range(4):
                c = hd * 4 + cc
                ia = load(tA, "a", ot, c)
                ib = load(tB, "b", gt, c)
                P.op("act", lambda e, ib=ib: e.activation(out=tB[ib][:, 0:TP], in_=tB[ib][:, 0:TP], func=AF.Silu),
                     reads=[("b", ib)], writes=[("b", ib)])
                P.op("dve", lambda e, ia=ia: e.tensor_tensor(out=tA[ia][:, 0:TP], in0=tA[ia][:, 0:TP], in1=st[0][:], op=ALU.mult),
                     reads=[("a", ia), ("st", 0)], writes=[("a", ia)])
                P.op("act", lambda e, ib=ib, cc=cc: e.activation(
                    out=tB[ib][:, 0:TP], in_=tB[ib][:, 0:TP], func=AF.Copy, scale=ngs[:, cc:cc + 1]),
                    reads=[("b", ib), "ngs"], writes=[("b", ib)])
                P.op("pool" if c % 2 == 0 else "dve", lambda e, ia=ia, ib=ib, c=c: e.tensor_tensor(
                    out=hb[:, c, 0:TP], in0=tA[ia][:, 0:TP], in1=tB[ib][:, 0:TP], op=ALU.mult),
                    reads=[("a", ia), ("b", ib)], writes=[("hb", c)])
    else:
        for c in range(16):
            ia = load(tA, "a", ot, c)
            ib = load(tB, "b", gt, c)
            P.op("act", lambda e, ib=ib: e.activation(out=tB[ib][:, 0:TP], in_=tB[ib][:, 0:TP], func=AF.Sigmoid),
                 reads=[("b", ib)], writes=[("b", ib)])
            P.op("dve" if c % 2 == 0 else "pool",
                 lambda e, ia=ia, ib=ib, c=c: e.tensor_tensor(out=hb[:, c, 0:TP], in0=tA[ia][:, 0:TP], in1=tB[ib][:, 0:TP], op=ALU.mult),
                 reads=[("a", ia), ("b", ib)], writes=[("hb", c)])

    def wload(src_ap):
        wi = cnt["w"] % NW
        cnt["w"] += 1
        P.dma("pool", lambda e: e.dma_start(out=wt[wi][:], in_=src_ap.rearrange("(kc p) j -> p kc j", p=128)),
              writes=[("wt", wi)])
        return wi

    obank = [6, 7]
    ob = {"i": 0}
    wow = {}

    def ensure_wo(fc):
        if fc in wow or fc >= 16:
            return
        wow[fc] = wload(wo[:, fc * 128:(fc + 1) * 128])

    for fc in range(16):
        for k in range(3):
            ensure_wo(fc + k)
        wi = wow[fc]
        for (t0, t1) in tiles_p:
            b = obank[ob["i"] % 2]
            ob["i"] += 1
            for kc in range(16):
                P.op("pe", lambda e, kc=kc, b=b, t0=t0, t1=t1, wi=wi: e.matmul(
                    C.ps[b][:, 0:t1 - t0], lhsT=wt[wi][:, kc, :], rhs=hb[:, kc, t0:t1], start=(kc == 0), stop=(kc == 15)),
                    reads=[("wt", wi), ("hb", kc)], excl=[("ps", b)])
            P.op("dve", lambda e, fc=fc, b=b, t0=t0, t1=t1: e.scalar_tensor_tensor(
                out=hs[:, fc, t0:t1], in0=hs[:, fc, t0:t1], scalar=float(ALPHA), in1=C.ps[b][:, 0:t1 - t0],
                op0=ALU.mult, op1=ALU.add), reads=[("hs", fc)], writes=[("hs", fc)], excl=[("ps", b)])

    def layer_norm(lo, gi, bi, make_hb):
        n = TP - lo
        tl = [(lo + a, lo + b_) for (a, b_) in ntiles(n)]
        for c in range(16):
            ic = cnt["c"] % 2
            cnt["c"] += 1
            P.op("act", lambda e, c=c, ic=ic: e.activation(out=tC[ic][:, lo:TP], in_=hs[:, c, lo:TP], func=AF.Square),
                 reads=[("hs", c)], writes=[("c", ic)])
            for ti, (t0, t1) in enumerate(tl):
                P.op("pe", lambda e, c=c, ti=ti, t0=t0, t1=t1: e.matmul(
                    C.ps[ti][:, 0:t1 - t0], lhsT=ones[:], rhs=hs[:, c, t0:t1], start=(c == 0), stop=(c == 15)),
                    reads=["ones", ("hs", c)], excl=[("ps", ti)])
                P.op("pe", lambda e, c=c, ic=ic, ti=ti, t0=t0, t1=t1: e.matmul(
                    C.ps[3 + ti][:, 0:t1 - t0], lhsT=ones[:], rhs=tC[ic][:, t0:t1], start=(c == 0), stop=(c == 15)),
                    reads=["ones", ("c", ic)], excl=[("ps", 3 + ti)])
        mean, var, tmp = st[0], st[1], st[2]
        for ti, (t0, t1) in enumerate(tl):
            P.op("dve", lambda e, ti=ti, t0=t0, t1=t1: e.tensor_scalar(
                out=mean[:, t0:t1], in0=C.ps[ti][:, 0:t1 - t0], scalar1=1.0 / D, scalar2=None, op0=ALU.mult),
                excl=[("ps", ti)], writes=[("st", 0)])
            P.op("dve", lambda e, ti=ti, t0=t0, t1=t1: e.tensor_scalar(
                out=var[:, t0:t1], in0=C.ps[3 + ti][:, 0:t1 - t0], scalar1=1.0 / D, scalar2=float(LN_EPS),
                op0=ALU.mult, op1=ALU.add), excl=[("ps", 3 + ti)], writes=[("st", 1)])
        P.op("dve", lambda e: e.tensor_tensor(out=tmp[:, lo:TP], in0=mean[:, lo:TP], in1=mean[:, lo:TP], op=ALU.mult),
             reads=[("st", 0)], writes=[("st", 2)])
        P.op("dve", lambda e: e.tensor_tensor(out=var[:, lo:TP], in0=var[:, lo:TP], in1=tmp[:, lo:TP], op=ALU.subtract),
             reads=[("st", 1), ("st", 2)], writes=[("st", 1)])
        P.op("act", lambda e: e.activation(out=var[:, lo:TP], in_=var[:, lo:TP], func=AF.Sqrt), reads=[("st", 1)], writes=[("st", 1)])
        P.op("dve", lambda e: e.reciprocal(out=var[:, lo:TP], in_=var[:, lo:TP]), reads=[("st", 1)], writes=[("st", 1)])
        for c in range(16):
            ia = cnt["a"] % 2
            cnt["a"] += 1
            P.op("dve", lambda e, c=c, ia=ia: e.tensor_tensor(out=tA[ia][:, lo:TP], in0=hs[:, c, lo:TP], in1=mean[:, lo:TP],
                                                             op=ALU.subtract),
                 reads=[("hs", c), ("st", 0)], writes=[("a", ia)])
            P.op("pool" if c % 2 == 0 else "dve",
                 lambda e, ia=ia: e.tensor_tensor(out=tA[ia][:, lo:TP], in0=tA[ia][:, lo:TP], in1=var[:, lo:TP],
                                                  op=ALU.mult),
                 reads=[("a", ia), ("st", 1)], writes=[("a", ia)])
            P.op("dve", lambda e, c=c, ia=ia: e.tensor_scalar(
                out=hs[:, c, lo:TP], in0=tA[ia][:, lo:TP], scalar1=lns[:, gi, c:c + 1], scalar2=lns[:, bi, c:c + 1],
                op0=ALU.mult, op1=ALU.add), reads=[("a", ia), "lns"], writes=[("hs", c)])
            if make_hb:
                P.op("act", lambda e, c=c: e.activation(out=hb[:, c, 0:TP], in_=hs[:, c, :], func=AF.Copy),
                     reads=[("hs", c)], writes=[("hb", c)])
                P.op("dve", lambda e, c=c: e.tensor_tensor(out=hb[:, c, 0:HALO], in0=hb[:, c, 0:HALO], in1=vms[:, 0:HALO],
                                                          op=ALU.mult),
                     reads=["vms"], writes=[("hb", c)])

    layer_norm(0, 0, 1, True)

    xu, xg, cu, cg = tA, tB, tC[0], tC[1]
    upw = {}
    dnw = {}

    def ensure_up(j):
        if j in upw or j >= NG * GP:
            return
        upw[j] = (wload(wup[:, j * 128:(j + 1) * 128]), wload(wup[:, D_FF + j * 128:D_FF + (j + 1) * 128]))

    def ensure_down(g, fc):
        if (g, fc) in dnw or fc >= 16:
            return
        wi = cnt["wd"] % NWD
        cnt["wd"] += 1
        P.dma("pool", lambda e: e.dma_start(
            out=wd[wi][:], in_=wdn[g * GP * 128:(g + 1) * GP * 128, fc * 128:(fc + 1) * 128].rearrange(
                "(jj p) m -> p jj m", p=128)), writes=[("wd", wi)])
        dnw[(g, fc)] = wi
    for g in range(NG):
        for jj in range(GP):
            j = g * GP + jj
            ensure_up(j)
            ensure_up(j + 1)
            if jj == GP - 1:
                ensure_down(g, 0)
                ensure_down(g, 1)
            wa, wb = upw[j]
            for (wi, b0, X, xt_, col) in ((wa, 0, (tA[0], tA[1], tC[0]), ("a", 0), j), (wb, 3, (tB[0], tB[1], tC[1]), ("b", 0), 44 + j)):
                for ti in range(3):
                    for kc in range(16):
                        P.op("pe", lambda e, kc=kc, wi=wi, b0=b0, ti=ti: e.matmul(
                            C.ps[b0 + ti][:, 0:TW], lhsT=wt[wi][:, kc, :], rhs=hb[:, kc, ti * TW:(ti + 1) * TW],
                            start=(kc == 0), stop=(kc == 15)),
                            reads=[("wt", wi), ("hb", kc)], excl=[("ps", b0 + ti)])
                src = C.psbig[:, b0 * 512:(b0 + 3) * 512].rearrange("p (b c) -> p b c", c=512)[:, :, 0:TW]
                banks = [("ps", b0), ("ps", b0 + 1), ("ps", b0 + 2)]
                for tap, buf, tok in ((2, X[0], (xt_[0], 0)), (1, X[1], (xt_[0], 1)), (0, X[2], ("c", 0 if b0 == 0 else 1))):
                    P.op("act", lambda e, tap=tap, buf=buf, src=src, col=col: e.activation(
                        out=buf[:, 0:TPX].rearrange("p (b c) -> p b c", c=TW), in_=src, func=AF.Identity,
                        scale=cws[:, col, tap:tap + 1], bias=(cws[:, col, 3:4] if tap == 2 else 0.0)),
                        reads=["cws"], excl=banks, writes=[tok])
                t2, t1_, t0_ = (xt_[0], 0), (xt_[0], 1), ("c", 0 if b0 == 0 else 1)
                P.op("dve", lambda e, X=X: e.tensor_tensor(out=X[0][:, 2:TP], in0=X[0][:, 2:TP], in1=X[1][:, 1:TP - 1], op=ALU.add),
                     reads=[t2, t1_], writes=[t2])
                P.op("dve", lambda e, X=X: e.tensor_tensor(out=X[0][:, 2:TP], in0=X[0][:, 2:TP], in1=X[2][:, 0:TC], op=ALU.add),
                     reads=[t2, t0_], writes=[t2])
            P.op("act", lambda e: e.activation(out=tB[0][:, 2:TP], in_=tB[0][:, 2:TP], func=AF.Silu), reads=[("b", 0)], writes=[("b", 0)])
            P.op("dve", lambda e, jj=jj: e.tensor_tensor(out=act[:, jj, :], in0=tA[0][:, 2:TP], in1=tB[0][:, 2:TP], op=ALU.mult),
                 reads=[("a", 0), ("b", 0)], writes=[("act", jj)])
        for fc in range(16):
            ensure_down(g, fc)
            ensure_down(g, fc + 1)
            ensure_down(g, fc + 2)
            wi = dnw[(g, fc)]
            for (t0, t1) in tiles_c:
                b = obank[ob["i"] % 2]
                ob["i"] += 1
                for jj in range(GP):
                    P.op("pe", lambda e, jj=jj, wi=wi, b=b, t0=t0, t1=t1: e.matmul(
                        C.ps[b][:, 0:t1 - t0], lhsT=wd[wi][:, jj, :], rhs=act[:, jj, t0:t1],
                        start=(jj == 0), stop=(jj == GP - 1)),
                        reads=[("wd", wi), ("act", jj)], excl=[("ps", b)])
                P.op("dve", lambda e, fc=fc, b=b, t0=t0, t1=t1, g=g: e.scalar_tensor_tensor(
                    out=hs[:, fc, 2 + t0:2 + t1], in0=hs[:, fc, 2 + t0:2 + t1], scalar=float(ALPHA if g == 0 else 1.0),
                    in1=C.ps[b][:, 0:t1 - t0], op0=ALU.mult, op1=ALU.add),
                    reads=[("hs", fc)], writes=[("hs", fc)], excl=[("ps", b)])

    layer_norm(2, 2, 3, False)
    for c in range(16):
        P.dma(dq(), lambda e, c=c: e.dma_start(out=hout[c * 128:(c + 1) * 128, :], in_=hs[:, c, 2:TP]),
              reads=[("hs", c)])
    P.finish()
    P.emit()
    return nc


def build_scan(debug=False):
    C = Ctx("scan")
    nc, P = C.nc, C.P
    L = LPAD
    qf = C.din("qf", [256, L], BF16)
    kf = C.din("kf", [256, L], BF16)
    ktm = C.din("ktm", [L, 256], BF16)
    vtm = C.din("vtm", [L, 256], BF16)
    latm = C.din("latm", [L, 256], F32)
    cmd = C.din("cm", [128, 2, 128], F32)
    od = C.dout("o", [L, 256], F32)
    GT = 13
    NGRP = NT // GT
    GL = GT * 128
    qin = [C.sb("qin%d" % i, [128, 2, GL], BF16) for i in range(2)]
    kin = [C.sb("kin%d" % i, [128, 2, GL], BF16) for i in range(2)]
    ktin = [C.sb("ktin%d" % i, [128, GT, 256], BF16) for i in range(2)]
    vtin = [C.sb("vtin%d" % i, [128, GT, 256], BF16) for i in range(2)]
    lain = [C.sb("lain%d" % i, [128, GT, 256], F32) for i in range(2)]
    cm = C.sb("cm_sb", [128, 2, 128], F32)
    eb = [C.sb("eb%d" % i, [128, 2, 128], F32) for i in range(2)]
    enb = [C.sb("enb%d" % i, [128, 2, 128], F32) for i in range(2)]
    qg = [C.sb("qg%d" % i, [128, 2, 128], BF16) for i in range(2)]
    kg = [C.sb("kg%d" % i, [128, 2, 128], BF16) for i in range(2)]
    ekd = [C.sb("ekd%d" % i, [128, 256], F32) for i in range(2)]
    kd = [C.sb("kd%d" % i, [128, 256], BF16) for i in range(2)]
    attm = [C.sb("attm%d" % i, [128, 128], BF16) for i in range(2)]
    osb = [C.sb("osb%d" % i, [64, 2, 256], F32) for i in range(2)]
    S = C.sb("S_sb", [128, 2, 256], F32)
    Sb = [C.sb("Sb_sb%d" % i, [128, 2, 256], BF16) for i in range(2)]
    ps = C.ps

    P.dma("sp", lambda e: e.dma_start(out=cm[:], in_=cmd), writes=["cm"])
    P.op("dve", lambda e: e.memset(S[:], 0.0), writes=[("S", 0), ("S", 1)])
    for i in range(2):
        P.op("pool", lambda e, i=i: e.memset(Sb[i][:], 0.0), writes=[("Sb", i, 0), ("Sb", i, 1)])

    def load_group(g):
        i = g % 2
        c0 = g * GL
        P.dma("sp", lambda e: e.dma_start(out=qin[i][:], in_=qf[:, c0:c0 + GL].rearrange("(kc p) t -> p kc t", p=128)),
              writes=[("qin", i)])
        P.dma("act", lambda e: e.dma_start(out=kin[i][:], in_=kf[:, c0:c0 + GL].rearrange("(kc p) t -> p kc t", p=128)),
              writes=[("kin", i)])
        P.dma("sp", lambda e: e.dma_start(out=ktin[i][:], in_=ktm[c0:c0 + GL, :].rearrange("(t p) k -> p t k", p=128)),
              writes=[("ktin", i)])
        P.dma("act", lambda e: e.dma_start(out=vtin[i][:], in_=vtm[c0:c0 + GL, :].rearrange("(t p) k -> p t k", p=128)),
              writes=[("vtin", i)])
        P.dma("sp", lambda e: e.dma_start(out=lain[i][:], in_=latm[c0:c0 + GL, :].rearrange("(t p) k -> p t k", p=128)),
              writes=[("lain", i)])

    def precompute(tt):
        g, tl = divmod(tt, GT)
        gi = g % 2
        pr = tt % 2
        bb = 0 if pr == 0 else 6
        b2 = 1 if pr == 0 else 7
        cs = slice(tl * 128, (tl + 1) * 128)
        for kc in range(2):
            P.op("pe", lambda e, kc=kc: e.matmul(ps[bb][:, kc * 128:(kc + 1) * 128], lhsT=lain[gi][:, tl, kc * 128:(kc + 1) * 128],
                                                rhs=cm[:, 0, :], start=True, stop=True),
                 reads=[("lain", gi), "cm"], excl=[("ps", bb)])
        P.op("act", lambda e: e.activation(out=eb[pr][:].rearrange("p a b -> p (a b)"), in_=ps[bb][:, 0:256], func=AF.Exp),
             excl=[("ps", bb)], writes=[("eb", pr)])
        P.op("act", lambda e: e.activation(out=enb[pr][:].rearrange("p a b -> p (a b)"), in_=ps[bb][:, 0:256], func=AF.Exp, scale=-1.0),
             excl=[("ps", bb)], writes=[("enb", pr)])
        P.op("dve", lambda e: e.tensor_tensor(out=qg[pr][:], in0=qin[gi][:, :, cs], in1=eb[pr][:], op=ALU.mult),
             reads=[("qin", gi), ("eb", pr)], writes=[("qg", pr)])
        P.op("pool", lambda e: e.tensor_tensor(out=kg[pr][:], in0=kin[gi][:, :, cs], in1=enb[pr][:], op=ALU.mult),
             reads=[("kin", gi), ("enb", pr)], writes=[("kg", pr)])
        P.op("pe", lambda e: e.matmul(ps[b2][:, 0:256], lhsT=cm[:, 1, :], rhs=lain[gi][:, tl, :], start=True, stop=True),
             reads=[("lain", gi), "cm"], excl=[("ps", b2)])
        P.op("act", lambda e: e.activation(out=ekd[pr][:], in_=ps[b2][:, 0:256], func=AF.Exp),
             excl=[("ps", b2)], writes=[("ekd", pr)])
        P.op("pool", lambda e: e.tensor_tensor(out=kd[pr][:], in0=ktin[gi][:, tl, :], in1=ekd[pr][:], op=ALU.mult),
             reads=[("ktin", gi), ("ekd", pr)], writes=[("kd", pr)])
        for kc in range(2):
            P.op("pe", lambda e, kc=kc: e.matmul(ps[b2][:, 256:384], lhsT=kg[pr][:, kc, :], rhs=qg[pr][:, kc, :],
                                                start=(kc == 0), stop=(kc == 1)),
                 reads=[("kg", pr), ("qg", pr)], excl=[("ps", b2)])
        P.op("dve", lambda e: e.tensor_tensor(out=attm[pr][:], in0=ps[b2][:, 256:384], in1=cm[:, 0, :], op=ALU.mult),
             reads=["cm"], excl=[("ps", b2)], writes=[("attm", pr)])

    def chain(tt):
        g, tl = divmod(tt, GT)
        gi = g % 2
        pr = tt % 2
        for s in range(2):
            rs = slice(s * 64, (s + 1) * 64)
            ob = 3 + s
            ci = 2 * tt + s
            cur, nxt = ci % 2, (ci + 1) % 2
            ub = 5 if ci % 2 == 0 else 2
            for kc in range(2):
                P.op("pe", lambda e, kc=kc, rs=rs, ub=ub: e.matmul(
                    ps[ub][:, kc * 256:(kc + 1) * 256], lhsT=kd[pr][rs, kc * 128:(kc + 1) * 128],
                    rhs=vtin[gi][rs, tl, :], start=True, stop=True),
                    reads=[("kd", pr), ("vtin", gi)], excl=[("ps", ub)])
            P.op("pe", lambda e, rs=rs, ob=ob: e.matmul(ps[ob][0:64, 0:256], lhsT=attm[pr][rs, rs], rhs=vtin[gi][rs, tl, :],
                                                       start=True, stop=False),
                 reads=[("attm", pr), ("vtin", gi)], excl=[("ps", ob)])
            for kc in range(2):
                P.op("pe", lambda e, kc=kc, rs=rs, ob=ob, cur=cur: e.matmul(
                    ps[ob][0:64, 0:256], lhsT=qg[pr][:, kc, rs], rhs=Sb[cur][:, kc, :], start=False, stop=(kc == 1)),
                    reads=[("qg", pr), ("Sb", cur, kc)], excl=[("ps", ob)])
            P.op("act", lambda e, s=s, ob=ob: e.activation(out=osb[pr][:, s, :], in_=ps[ob][0:64, 0:256], func=AF.Copy),
                 excl=[("ps", ob)], writes=[("osb", pr)])
            for kc in range(2):
                P.op("dve", lambda e, kc=kc, s=s, ub=ub, nxt=nxt: e.scalar_tensor_tensor(
                    out=Sb[nxt][:, kc, :], in0=S[:, kc, :], scalar=eb[pr][:, kc, s * 64 + 63:s * 64 + 64],
                    in1=ps[ub][:, kc * 256:(kc + 1) * 256], op0=ALU.mult, op1=ALU.add),
                    reads=[("S", kc), ("eb", pr)], writes=[("Sb", nxt, kc)], excl=[("ps", ub)])
            for kc in range(2):
                P.op("dve", lambda e, kc=kc, s=s, ub=ub: e.scalar_tensor_tensor(
                    out=S[:, kc, :], in0=S[:, kc, :], scalar=eb[pr][:, kc, s * 64 + 63:s * 64 + 64],
                    in1=ps[ub][:, kc * 256:(kc + 1) * 256], op0=ALU.mult, op1=ALU.add),
                    reads=[("S", kc), ("eb", pr)], writes=[("S", kc)], excl=[("ps", ub)])
        P.dma("sp" if tt % 2 == 0 else "act",
              lambda e: e.dma_start(out=od[tt * 128:(tt + 1) * 128, :].rearrange("(s i) v -> i s v", s=2), in_=osb[pr][:]),
              reads=[("osb", pr)])

    load_group(0)
    precompute(0)
    if debug:
        dbg = {n: C.dout("dbg_" + n, shp, dt) for n, shp, dt in (("eb", [128, 256], F32), ("qg", [128, 256], BF16),
               ("kg", [128, 256], BF16), ("kd", [128, 256], BF16), ("attm", [128, 128], BF16), ("ekd", [128, 256], F32))}
        P.dma("sp", lambda e: e.dma_start(out=dbg["eb"], in_=eb[0][:].rearrange("p a b -> p (a b)")), reads=[("eb", 0)])
        P.dma("sp", lambda e: e.dma_start(out=dbg["qg"], in_=qg[0][:].rearrange("p a b -> p (a b)")), reads=[("qg", 0)])
        P.dma("sp", lambda e: e.dma_start(out=dbg["kg"], in_=kg[0][:].rearrange("p a b -> p (a b)")), reads=[("kg", 0)])
        P.dma("sp", lambda e: e.dma_start(out=dbg["kd"], in_=kd[0][:]), reads=[("kd", 0)])
        P.dma("sp", lambda e: e.dma_start(out=dbg["attm"], in_=attm[0][:]), reads=[("attm", 0)])
        P.dma("sp", lambda e: e.dma_start(out=dbg["ekd"], in_=ekd[0][:]), reads=[("ekd", 0)])
    for tt in range(NT if not debug else 2):
        g, tl = divmod(tt, GT)
        if tl == 0 and g + 1 < NGRP:
            load_group(g + 1)
        if tt + 1 < NT:
            precompute(tt + 1)
        chain(tt)
    P.finish()
    P.emit()
    return nc


HPC = FOX_HEADS // NCORE


def build_attn():
    C = Ctx("attn")
    nc, P = C.nc, C.P
    L = LPAD
    qf = C.din("qf", [HPC, 128, L], BF16)
    kf = C.din("kf", [HPC, 128, L], BF16)
    vtm = C.din("vtm", [HPC, L, 128], BF16)
    lft = C.din("lft", [HPC, 128, NT], F32)
    cmd = C.din("cm", [128, 4, 128], F32)
    lmd = C.din("lm", [NT, NT], F32)
    pmd = C.din("pm", [3, 3], F32)
    od = C.dout("o", [HPC, L, 128], F32)
    scr = [nc.dram_tensor("cscr%d" % h, [3, NT, 128], BF16, kind="Internal").ap() for h in range(HPC)]
    ps6b = C.psbig[:, 6 * 512:7 * 512].bitcast(BF16)
    ps = C.ps

    cm = C.sb("cm_sb", [128, 4, 128], F32)
    lm = C.sb("lm_sb", [NT, NT], F32)
    pm = C.sb("pm_sb", [3, 3], F32)
    ibf = C.sb("ibf", [128, 128], BF16)
    mneg = C.sb("mneg", [128, 128], BF16)
    ones3 = C.sb("ones3", [3, 128], BF16)
    qh = C.sb("qh", [128, L], BF16)
    kh = C.sb("kh", [128, L], BF16)
    vaug = C.sb("vaug", [128, NT, 129], BF16)
    lfs = C.sb("lfs", [128, NT], F32)
    asb = C.sb("asb", [NT, 128], F32)
    csb = C.sb("csb", [128, NT], F32)
    negc = C.sb("negc", [128, NT], F32)
    c3 = C.sb("c3", [128, 3, NT], BF16)
    cr = C.sb("cr", [128, NT], F32)
    ct3 = C.sb("ct3", [NT, 384], BF16)
    cq3 = C.sb("cq3", [3, L], BF16)
    pt = [C.sb("pt%d" % i, [128, 512], BF16) for i in range(3)]
    osb = [C.sb("osb%d" % i, [128, 4, 128], F32) for i in range(2)]
    rec = [C.sb("rec%d" % i, [128, 4], F32) for i in range(2)]

    P.dma("sp", lambda e: e.dma_start(out=cm[:], in_=cmd), writes=["cm"])
    P.dma("sp", lambda e: e.dma_start(out=lm[:], in_=lmd), writes=["lm"])
    P.dma("sp", lambda e: e.dma_start(out=pm[:], in_=pmd), writes=["pm"])
    P.op("dve", lambda e: e.tensor_copy(out=ibf[:], in_=cm[:, 2, :]), reads=["cm"], writes=["ibf"])
    P.op("dve", lambda e: e.tensor_copy(out=mneg[:], in_=cm[:, 3, :]), reads=["cm"], writes=["mneg"])
    P.op("pool", lambda e: e.memset(ones3[:], 1.0), writes=["ones3"])

    cnt = {"sb": 0, "pb": 0, "ob": 0}
    for h in range(HPC):
        P.dma("sp", lambda e, h=h: e.dma_start(out=qh[:], in_=qf[h]), writes=["qh"])
        P.dma("act", lambda e, h=h: e.dma_start(out=kh[:], in_=kf[h]), writes=["kh"])
        P.op("pool", lambda e: e.memset(vaug[:], 1.0), writes=["vaug"])
        P.dma("sp", lambda e, h=h: e.dma_start(out=vaug[:, :, 0:128], in_=vtm[h].rearrange("(t p) d -> p t d", p=128)),
              writes=["vaug"])
        P.dma("act", lambda e, h=h: e.dma_start(out=lfs[:], in_=lft[h]), writes=["lfs"])
        P.op("pe", lambda e: e.matmul(ps[6][0:NT, 0:128], lhsT=lfs[:], rhs=cm[:, 1, :], start=True, stop=True),
             reads=["lfs", "cm"], excl=[("ps", 6)])
        P.op("act", lambda e: e.activation(out=asb[:], in_=ps[6][0:NT, 0:128], func=AF.Copy), excl=[("ps", 6)], writes=["asb"])
        P.op("pe", lambda e: e.matmul(ps[7][:, 0:NT], lhsT=cm[:, 0, :], rhs=lfs[:], start=True, stop=False),
             reads=["lfs", "cm"], excl=[("ps", 7)])
        P.op("pe", lambda e: e.matmul(ps[7][:, 0:NT], lhsT=asb[:], rhs=lm[:], start=False, stop=True),
             reads=["asb", "lm"], excl=[("ps", 7)])
        P.op("dve", lambda e: e.tensor_copy(out=csb[:], in_=ps[7][:, 0:NT]), excl=[("ps", 7)], writes=["csb"])
        P.op("dve", lambda e: e.tensor_scalar(out=negc[:], in0=csb[:], scalar1=-1.0, scalar2=None, op0=ALU.mult),
             reads=["csb"], writes=["negc"])
        P.op("dve", lambda e: e.tensor_copy(out=c3[:, 0, :], in_=csb[:]), reads=["csb"], writes=["c3"])
        P.op("dve", lambda e: e.tensor_tensor(out=cr[:], in0=csb[:], in1=c3[:, 0, :], op=ALU.subtract),
             reads=["csb", "c3"], writes=["cr"])
        P.op("dve", lambda e: e.tensor_copy(out=c3[:, 1, :], in_=cr[:]), reads=["cr"], writes=["c3"])
        P.op("dve", lambda e: e.tensor_tensor(out=cr[:], in0=cr[:], in1=c3[:, 1, :], op=ALU.subtract),
             reads=["cr", "c3"], writes=["cr"])
        P.op("dve", lambda e: e.tensor_copy(out=c3[:, 2, :], in_=cr[:]), reads=["cr"], writes=["c3"])
        for r in range(3):
            P.op("pe", lambda e, r=r: e.transpose(ps6b[0:NT, r * 128:(r + 1) * 128], c3[:, r, :], ibf[:]),
                 reads=["c3", "ibf"], excl=[("ps", 6)])
        P.op("act", lambda e: e.activation(out=ct3[:], in_=ps6b[0:NT, 0:384], func=AF.Copy), excl=[("ps", 6)], writes=["ct3"])
        P.dma("sp", lambda e, h=h: e.dma_start(out=scr[h].rearrange("r t j -> t r j"),
                                               in_=ct3[:].rearrange("t (r j) -> t r j", j=128)),
              reads=["ct3"], writes=[("scr", h)])
        P.dma("sp", lambda e, h=h: e.dma_start(out=cq3[:], in_=scr[h].rearrange("r t j -> r (t j)")),
              reads=[("scr", h)], writes=["cq3"])
        steps = []
        for q0 in range(0, NT, 4):
            nq = min(4, NT - q0)
            for kt in range(q0 + nq):
                steps.append((q0, nq, kt))
        LA = 2
        stb = [0, 1, 6]
        info = {}

        def emit_st(i):
            q0, nq, kt = steps[i]
            r0 = max(0, kt - q0)
            n = (nq - r0) * 128
            qa, qb = (q0 + r0) * 128, (q0 + nq) * 128
            sbk = stb[cnt["sb"] % 3]
            cnt["sb"] += 1
            diag = kt >= q0
            info[i] = (sbk, r0, n)
            P.op("pe", lambda e: e.matmul(ps[sbk][:, 0:n], lhsT=kh[:, kt * 128:(kt + 1) * 128], rhs=qh[:, qa:qb],
                                          start=True, stop=False), reads=["kh", "qh"], excl=[("ps", sbk)])
            P.op("pe", lambda e: e.matmul(ps[sbk][:, 0:n], lhsT=ones3[:], rhs=cq3[:, qa:qb], start=False, stop=(not diag)),
                 reads=["ones3", "cq3"], excl=[("ps", sbk)])
            if diag:
                P.op("pe", lambda e: e.matmul(ps[sbk][:, 0:128], lhsT=ibf[:], rhs=mneg[:], start=False, stop=True),
                     reads=["ibf", "mneg"], excl=[("ps", sbk)])

        def emit_exp_pv(i):
            q0, nq, kt = steps[i]
            sbk, r0, n = info[i]
            pb = cnt["pb"] % 3
            cnt["pb"] += 1
            P.op("act", lambda e: e.activation(out=pt[pb][:, 0:n], in_=ps[sbk][:, 0:n], func=AF.Exp,
                                               bias=negc[:, kt:kt + 1], scale=1.0),
                 reads=["negc"], excl=[("ps", sbk)], writes=[("pt", pb)])
            for r in range(r0, nq):
                qt = q0 + r
                P.op("pe", lambda e, r=r, qt=qt: e.matmul(
                    ps[2 + r][:, 0:129], lhsT=pt[pb][:, (r - r0) * 128:(r - r0 + 1) * 128], rhs=vaug[:, kt, :],
                    start=(kt == 0), stop=(kt == qt)),
                    reads=[("pt", pb), "vaug"], excl=[("ps", 2 + r)])

        def finalize(q0, nq):
            oi = cnt["ob"] % 2
            cnt["ob"] += 1
            for r in range(nq):
                P.op("dve", lambda e, r=r: e.reciprocal(out=rec[oi][:, r:r + 1], in_=ps[2 + r][:, 128:129]),
                     excl=[("ps", 2 + r)], writes=[("rec", oi)])
                P.op("dve", lambda e, r=r: e.tensor_scalar(out=osb[oi][:, r, :], in0=ps[2 + r][:, 0:128],
                                                          scalar1=rec[oi][:, r:r + 1], scalar2=None, op0=ALU.mult),
                     reads=[("rec", oi)], excl=[("ps", 2 + r)], writes=[("osb", oi)])
            P.dma("sp" if oi == 0 else "act",
                  lambda e, h=h: e.dma_start(
                      out=od[h, q0 * 128:(q0 + nq) * 128, :].rearrange("(r i) d -> i r d", i=128), in_=osb[oi][:, 0:nq, :]),
                  reads=[("osb", oi)])

        nst = len(steps)
        for i in range(min(LA, nst)):
            emit_st(i)
        for i in range(nst):
            if i + LA < nst:
                emit_st(i + LA)
            emit_exp_pv(i)
            q0, nq, kt = steps[i]
            if kt == q0 + nq - 1:
                finalize(q0, nq)
    P.finish()
    P.emit()
    return nc


def attn_consts():
    ii = np.arange(128)
    cm = np.zeros((128, 4, 128), np.float32)
    cm[:, 0, :] = (ii[:, None] <= ii[None, :])
    cm[:, 1, :] = 1.0
    cm[:, 2, :] = np.eye(128)
    cm[:, 3, :] = np.where(ii[:, None] > ii[None, :], NEG, 0.0)
    tt = np.arange(NT)
    lm = (tt[:, None] < tt[None, :]).astype(np.float32)
    return {"cm": cm, "lm": lm, "pm": np.eye(3, dtype=np.float32)}


_PROGS = {}


def _prog(name):
    if name not in _PROGS:
        if name.startswith("pre_"):
            _PROGS[name] = build_pre(name[4:])
        elif name.startswith("post_"):
            _PROGS[name] = build_post(name[5:])
        elif name == "scan":
            _PROGS[name] = build_scan()
        elif name == "attn":
            _PROGS[name] = build_attn()
    return _PROGS[name]


def _run(name, in_maps):
    res = run_bass_kernel_spmd(_prog(name), in_maps, core_ids=list(range(NCORE)))
    return res.results


def _fm(vec, nch):
    return np.ascontiguousarray(np.asarray(vec, np.float32).reshape(nch, 128).T)


def _cat_tokens(results, key, dtype):
    full = np.concatenate([np.asarray(r[key]) for r in results], axis=1)
    out = np.zeros((full.shape[0], LPAD), dtype)
    out[:, :NTOK] = full
    return out


def _scan_consts():
    ii = np.arange(128)
    same = (ii[:, None] // 64) == (ii[None, :] // 64)
    return np.stack([((ii[:, None] <= ii[None, :]) & same), ((ii[:, None] > ii[None, :]) & same)],
                    axis=1).astype(np.float32)


def _post_inputs(h, o_full, g_fm, c):
    lo = c * TC - HALO
    def tok_major(a):
        if lo < 0:
            blk = np.concatenate([np.zeros((HALO, D), np.float32), a[0:TC]], axis=0)
        else:
            blk = a[lo:lo + TP]
        return np.ascontiguousarray(blk.T)
    if lo < 0:
        g = np.concatenate([np.zeros((D, HALO), np.float32), g_fm[:, 0:TC]], axis=1)
    else:
        g = g_fm[:, lo:lo + TP]
    vm = np.ones((128, TP), np.float32)
    if c == 0:
        vm[:, 0:HALO] = 0.0
    return tok_major(h), tok_major(o_full), np.ascontiguousarray(g, dtype=np.float32), vm


def kernel(x, meta, ln_g, ln_b, gla_w_in, gla_w_g2, gla_b_g2, gla_norm_g, gla_w_out,
           kv_w, kv_bf, fox_w_in, fox_w_out, ffn_w_up, ffn_conv_w, ffn_conv_b, ffn_w_down):
    f32 = np.float32
    x = np.asarray(x, f32)
    h = np.concatenate([np.asarray(meta, f32), x[0]], axis=0)
    ln_g, ln_b = np.asarray(ln_g, f32), np.asarray(ln_b, f32)
    kF = vF = lfF = None
    for l in range(DEPTH):
        hts = [np.ascontiguousarray(h[c * TC:(c + 1) * TC].T) for c in range(NCORE)]
        if l < 2:
            w_in = np.asarray(gla_w_in[l], f32)
            wg2 = np.asarray(gla_w_g2[l], f32)
            bg2 = _fm(gla_b_g2[l], 8)
            res = _run("pre_gla", [{"ht": hts[c], "w": w_in, "wg2": wg2, "bg2": bg2} for c in range(NCORE)])
            qF = _cat_tokens(res, "q", NPBF)
            kFg = _cat_tokens(res, "k", NPBF)
            vFg = _cat_tokens(res, "v", NPBF)
            laF = _cat_tokens(res, "la", f32)
            rF = _cat_tokens(res, "r", f32)
            cmc = _scan_consts()
            ims = []
            for u in range(NCORE):
                hd, vh = divmod(u, 2)
                ks = kFg[hd * 256:(hd + 1) * 256]
                ims.append({"qf": np.ascontiguousarray(qF[hd * 256:(hd + 1) * 256]), "kf": np.ascontiguousarray(ks),
                            "ktm": np.ascontiguousarray(ks.T),
                            "vtm": np.ascontiguousarray(vFg[hd * 512 + vh * 256:hd * 512 + (vh + 1) * 256].T),
                            "latm": np.ascontiguousarray(laF[hd * 256:(hd + 1) * 256].T), "cm": cmc})
            res = _run("scan", ims)
            o_full = np.zeros((NTOK, D), f32)
            for u in range(NCORE):
                hd, vh = divmod(u, 2)
                o_full[:, hd * 512 + vh * 256:hd * 512 + (vh + 1) * 256] = np.asarray(res[u]["o"])[:NTOK]
            g_fm = rF
            pname = "post_gla"
            w_out = np.asarray(gla_w_out[l], f32)
        else:
            j = l - 2
            w_in = np.asarray(fox_w_in[j], f32)
            if l == 2:
                wkv = np.asarray(kv_w, f32)
                bfv = np.asarray(kv_bf, f32).reshape(FOX_HEADS, 1)
                res = _run("pre_fox_kv", [{"ht": hts[c], "w": w_in, "wkv": wkv, "bf": bfv} for c in range(NCORE)])
                kF = _cat_tokens(res, "kk", NPBF)
                vF = _cat_tokens(res, "vv", NPBF)
                lfF = _cat_tokens(res, "lf", f32)
            else:
                res = _run("pre_fox", [{"ht": hts[c], "w": w_in} for c in range(NCORE)])
            qF = _cat_tokens(res, "q", NPBF)
            ogF = _cat_tokens(res, "og", f32)
            ac = attn_consts()
            ims = []
            for c in range(NCORE):
                rows = slice(c * HPC * 128, (c + 1) * HPC * 128)
                im = dict(ac)
                im["qf"] = np.ascontiguousarray(qF[rows].reshape(HPC, 128, LPAD))
                im["kf"] = np.ascontiguousarray(kF[rows].reshape(HPC, 128, LPAD))
                im["vtm"] = np.ascontiguousarray(vF[rows].reshape(HPC, 128, LPAD).transpose(0, 2, 1))
                im["lft"] = np.ascontiguousarray(lfF[c * HPC:(c + 1) * HPC].reshape(HPC, NT, 128).transpose(0, 2, 1))
                ims.append(im)
            res = _run("attn", ims)
            o_full = np.zeros((NTOK, D), f32)
            for c in range(NCORE):
                o = np.asarray(res[c]["o"])
                for hh in range(HPC):
                    o_full[:, (c * HPC + hh) * 128:(c * HPC + hh + 1) * 128] = o[hh, :NTOK]
            g_fm = ogF
            pname = "post_fox"
            w_out = np.asarray(fox_w_out[j], f32)
        lnp = np.ascontiguousarray(np.stack([_fm(ln_g[l, 0], 16), _fm(ln_b[l, 0], 16), _fm(ln_g[l, 1], 16),
                                             _fm(ln_b[l, 1], 16)], axis=1))
        cw = np.asarray(ffn_conv_w[l], f32)
        cb = np.asarray(ffn_conv_b[l], f32)
        cwl = np.ascontiguousarray(np.concatenate([cw.reshape(3, 88, 128).transpose(2, 1, 0),
                                                   cb.reshape(88, 128).T[:, :, None]], axis=2))
        wup = np.asarray(ffn_w_up[l], f32)
        wdn = np.asarray(ffn_w_down[l], f32)
        ims = []
        for c in range(NCORE):
            ht_, ot_, gt_, vm_ = _post_inputs(h, o_full, g_fm, c)
            im = {"ht": ht_, "ot": ot_, "gt": gt_, "vmask": vm_, "wo": w_out, "lnp": lnp, "wup": wup, "cw": cwl, "wdn": wdn}
            if pname == "post_gla":
                im["ng"] = _fm(gla_norm_g[l], 4)
            ims.append(im)
        res = _run(pname, ims)
        h = np.concatenate([np.asarray(res[c]["hout"]).T for c in range(NCORE)], axis=0)
    return np.ascontiguousarray(h[N_META:].reshape(1, SEQ, D).astype(f32))
```
